# Optimizing a Trainium2 kernel written in Bass

```python
import jax, jax.numpy as jnp
from jax import lax
import numpy as np

D_MODEL = 1024
BATCH = 32
SEQ = 2048
DEPTH = 1

CHUNK = 64
CONV_WIDTH = D_MODEL // 2
CONV_GROUPS = 8
CONV_K = 3
ATTN_HEADS = 8
HEAD_DIM = 64
ATTN_WIDTH = ATTN_HEADS * HEAD_DIM
N_BRANCH = 2
Q_BLOCK = 128
FFN_HIDDEN = 2816
FFN_CONV_K = 3
EPS = 1e-6
IN_SPLITS = (CONV_WIDTH, CONV_WIDTH, CONV_WIDTH, ATTN_WIDTH, ATTN_WIDTH, ATTN_WIDTH, ATTN_HEADS, N_BRANCH * D_MODEL)
IN_WIDTH = sum(IN_SPLITS)

kernel_name = "hybrid_gated_conv_fox_convffn"


def rms_norm(x, g):
    x32 = x.astype(jnp.float32)
    y = x32 * lax.rsqrt(jnp.mean(x32 * x32, axis=-1, keepdims=True) + EPS)
    return y.astype(x.dtype) * g


def causal_dwconv(x, w):
    K = w.shape[0]
    S = x.shape[1]
    xp = jnp.pad(x, ((0, 0), (K - 1, 0), (0, 0)))
    y = xp[:, K - 1:K - 1 + S, :] * w[K - 1]
    for k in range(K - 1):
        y = y + xp[:, k:k + S, :] * w[k]
    return y


def forgetting_attention(q, k, v, log_f):
    B, S, H, hd = q.shape
    scale = 1.0 / np.sqrt(hd).astype(np.float32)
    F = jnp.cumsum(log_f, axis=1).transpose(0, 2, 1)
    qh = q.transpose(0, 2, 1, 3).astype(jnp.float32) * scale
    kh = k.transpose(0, 2, 1, 3).astype(jnp.float32)
    vh = v.transpose(0, 2, 1, 3).astype(jnp.float32)
    nb = S // Q_BLOCK
    q_blocks = qh.reshape(B, H, nb, Q_BLOCK, hd).transpose(2, 0, 1, 3, 4)
    fq_blocks = F.reshape(B, H, nb, Q_BLOCK).transpose(2, 0, 1, 3)
    k_pos = jnp.arange(S)

    def one_block(args):
        q_blk, fq_blk, i = args
        q_pos = i * Q_BLOCK + jnp.arange(Q_BLOCK)
        s = jnp.einsum('bhqd,bhkd->bhqk', q_blk, kh) + fq_blk[..., None] - F[:, :, None, :]
        s = jnp.where(k_pos[None, :] <= q_pos[:, None], s, -jnp.inf)
        p = jax.nn.softmax(s, axis=-1)
        return jnp.einsum('bhqk,bhkd->bhqd', p, vh)

    o = lax.map(one_block, (q_blocks, fq_blocks, jnp.arange(nb)))
    o = o.transpose(1, 0, 3, 2, 4).reshape(B, S, H * hd)
    return o.astype(q.dtype)


def setup_inputs(seed: int = 0) -> dict:
    key = jax.random.key(seed)
    ks = jax.random.split(key, 16)
    f32 = jnp.float32
    L = DEPTH
    x = jax.random.normal(ks[0], (BATCH, SEQ, D_MODEL), f32)
    norm_mix_g = 1.0 + 0.02 * jax.random.normal(ks[1], (L, D_MODEL), f32)
    w_in = jax.random.normal(ks[2], (L, D_MODEL, IN_WIDTH), f32) * D_MODEL ** -0.5
    b_f = 2.0 + 0.5 * jax.random.normal(ks[3], (L, ATTN_HEADS), f32)
    b_gate = 0.02 * jax.random.normal(ks[4], (L, N_BRANCH * D_MODEL), f32)
    conv_mix_w = jax.random.normal(ks[5], (L, CONV_K, CONV_WIDTH), f32) * CONV_K ** -0.5
    w_out_conv = jax.random.normal(ks[6], (L, CONV_WIDTH, D_MODEL), f32) * CONV_WIDTH ** -0.5
    w_out_attn = jax.random.normal(ks[7], (L, ATTN_WIDTH, D_MODEL), f32) * ATTN_WIDTH ** -0.5
    w_o = jax.random.normal(ks[8], (L, D_MODEL, D_MODEL), f32) * D_MODEL ** -0.5
    norm_ffn_g = 1.0 + 0.02 * jax.random.normal(ks[9], (L, D_MODEL), f32)
    w_up = jax.random.normal(ks[10], (L, D_MODEL, 2 * FFN_HIDDEN), f32) * D_MODEL ** -0.5
    conv_ffn_w = jax.random.normal(ks[11], (L, FFN_CONV_K, 2 * FFN_HIDDEN), f32) * FFN_CONV_K ** -0.5
    w_down = jax.random.normal(ks[12], (L, FFN_HIDDEN, D_MODEL), f32) * FFN_HIDDEN ** -0.5
    norm_f_g = 1.0 + 0.02 * jax.random.normal(ks[13], (D_MODEL,), f32)
    return {"x": x, "norm_mix_g": norm_mix_g, "w_in": w_in, "b_f": b_f, "b_gate": b_gate,
            "conv_mix_w": conv_mix_w, "w_out_conv": w_out_conv, "w_out_attn": w_out_attn,
            "w_o": w_o, "norm_ffn_g": norm_ffn_g, "w_up": w_up, "conv_ffn_w": conv_ffn_w,
            "w_down": w_down, "norm_f_g": norm_f_g}


def reference(x, norm_mix_g, w_in, b_f, b_gate, conv_mix_w, w_out_conv, w_out_attn,
              w_o, norm_ffn_g, w_up, conv_ffn_w, w_down, norm_f_g):
    B, S, _ = x.shape
    split_idx = list(np.cumsum(IN_SPLITS)[:-1])
    for l in range(DEPTH):
        h = rms_norm(x, norm_mix_g[l])
        proj = h @ w_in[l]
        cb, cc, cin, q, k, v, f_logit, g_logit = jnp.split(proj, split_idx, axis=-1)
        u = causal_dwconv(cc * cin, conv_mix_w[l])
        y_conv = (cb * u) @ w_out_conv[l]
        log_f = jax.nn.log_sigmoid((f_logit + b_f[l]).astype(jnp.float32))
        o = forgetting_attention(q.reshape(B, S, ATTN_HEADS, HEAD_DIM),
                                 k.reshape(B, S, ATTN_HEADS, HEAD_DIM),
                                 v.reshape(B, S, ATTN_HEADS, HEAD_DIM), log_f)
        y_attn = o @ w_out_attn[l]
        gates = jax.nn.sigmoid(g_logit + b_gate[l])
        g_conv, g_attn = jnp.split(gates, 2, axis=-1)
        x = x + (g_conv * y_conv + g_attn * y_attn) @ w_o[l]
        h = rms_norm(x, norm_ffn_g[l])
        up = causal_dwconv(h @ w_up[l], conv_ffn_w[l])
        a, b = jnp.split(up, 2, axis=-1)
        x = x + (jax.nn.silu(a) * b) @ w_down[l]
    return rms_norm(x, norm_f_g)
```

```python
import numpy as np
from contextlib import ExitStack
import concourse.bass as bass
import concourse.mybir as mybir
from concourse.bass_utils import run_bass_kernel_spmd

F32 = mybir.dt.float32
BF16 = mybir.dt.bfloat16
AF = mybir.ActivationFunctionType
ALU = mybir.AluOpType

N_CORES = 8
D = 1024
NKC = 8
H = 8
FF = 2816
NFC = 22
INW = 5128
TB = 512
EPS = 1e-6
NSLOT = 4
NUNIT = 31
MASKV = -30000.0
_LASTP = []
_DBG = {"stop": 0}


class _Op:
    __slots__ = ("idx", "eng", "fn", "dma", "deps", "signal", "val", "grp")

    def __init__(self, idx, eng, fn, dma, grp):
        self.idx = idx
        self.eng = eng
        self.fn = fn
        self.dma = dma
        self.deps = []
        self.signal = False
        self.val = 0
        self.grp = grp


class _Rec:
    def __init__(self):
        self.call = None

    def __getattr__(self, name):
        def f(*a, **k):
            assert self.call is None
            self.call = (name, a, k)
            return self
        return f


class Prog:
    ENGS = ("pe", "act", "dve", "pool", "sp")

    def __init__(self):
        self.ops = []
        self.last_w = {}
        self.readers = {}
        self.barrier_keys = set()
        self.batch_keys = {}
        self.grp_last = {"M": {}, "F": {}}

    def pending(self, res):
        out = []
        w = self.last_w.get(res)
        if w is not None:
            out.append(w)
        for rd in self.readers.get(res, {}).values():
            out.extend(rd if isinstance(rd, list) else [rd])
        return out

    def add(self, eng, fn, reads=(), writes=(), dma=None, grp=None, after=()):
        if fn is not None:
            rec = _Rec()
            fn(rec)
            fn = rec.call
        op = _Op(len(self.ops), eng, fn, dma, grp)
        hard = {}
        soft = {}
        for o in after:
            hard[o.idx] = o
        for r in reads:
            w = self.last_w.get(r)
            if w is not None:
                hard[w.idx] = w
            if r.startswith("ps") and dma is None:
                for e2, rd in self.readers.get(r, {}).items():
                    if e2 != eng and e2 != "dma":
                        hard[rd.idx] = rd
        for r in writes:
            w = self.last_w.get(r)
            if w is not None:
                hard[w.idx] = w
            for rd in self.readers.get(r, {}).values():
                for o in (rd if isinstance(rd, list) else [rd]):
                    soft[o.idx] = o
        if grp is not None:
            other = "F" if grp == "M" else "M"
            for o in self.grp_last[other].values():
                soft[o.idx] = o
            if dma is None:
                self.grp_last[grp][eng] = op
        deps = []
        for d in hard.values():
            if d.dma is None and op.dma is None and d.eng == eng and eng == "pe":
                continue
            deps.append(d)
        for d in soft.values():
            if d.idx in hard:
                continue
            if d.dma is None and d.eng == eng and op.dma is None and eng != "pool":
                continue
            deps.append(d)
        op.deps = deps
        for d in deps:
            d.signal = True
        for r in reads:
            rd = self.readers.setdefault(r, {})
            if dma is not None:
                rd.setdefault("dma", []).append(op)
            else:
                rd[eng] = op
        for r in writes:
            self.last_w[r] = op
            self.readers[r] = {}
        self.ops.append(op)
        return op

    def emit(self, nc, es):
        sems = {}
        for e in self.ENGS:
            sems[e] = es.enter_context(nc.semaphore("s_" + e))
        cnt = {e: 0 for e in self.ENGS}
        dcnt = {}
        for op in self.ops:
            if op.dma is not None:
                if op.dma not in sems:
                    sems[op.dma] = es.enter_context(nc.semaphore("d_" + op.dma))
                    dcnt[op.dma] = 0
                dcnt[op.dma] += 16
                op.val = dcnt[op.dma]
            elif op.signal:
                cnt[op.eng] += 1
                op.val = cnt[op.eng]
        final = dict(dcnt)
        block = es.enter_context(nc.Block())
        per_eng = {e: [o for o in self.ops if o.eng == e] for e in self.ENGS}

        def run(engine, ename):
            waited = {}
            for op in per_eng[ename]:
                need = {}
                for d in op.deps:
                    if d.dma is not None:
                        k = d.dma
                        v = final[k] if k in self.barrier_keys else d.val
                        if k in self.batch_keys:
                            bsz = self.batch_keys[k]
                            v = (v + bsz - 1) // bsz * bsz
                    else:
                        k = d.eng
                        v = d.val
                    if v > need.get(k, 0):
                        need[k] = v
                for k, v in need.items():
                    if v > waited.get(k, 0):
                        engine.wait_ge(sems[k], v)
                        waited[k] = v
                if op.fn is None:
                    continue
                name, a, k = op.fn
                ins = getattr(engine, name)(*a, **k)
                if op.dma is not None:
                    ins.then_inc(sems[op.dma], 16)
                elif op.signal:
                    ins.then_inc(sems[ename], 1)

        @block.tensor
        def _(e):
            run(e, "pe")

        @block.scalar
        def _(e):
            run(e, "act")

        @block.vector
        def _(e):
            run(e, "dve")

        @block.gpsimd
        def _(e):
            run(e, "pool")

        @block.sync
        def _(e):
            run(e, "sp")


def build_program(NSEQ=4, SEQ=2048):
    NQB = SEQ // TB
    NBLK = NSEQ * NQB
    NKT = SEQ // 128
    TOK = NSEQ * SEQ
    nc = bass.Bass("TRN2", target_bir_lowering=False)

    def din(name, shape):
        return nc.dram_tensor(name, shape, F32, kind="ExternalInput").ap()

    x_d = din("x", [TOK, D])
    norm_mix_g = din("norm_mix_g", [D])
    w_in = din("w_in", [D, INW])
    b_f = din("b_f", [1, H])
    b_gate = din("b_gate", [2 * D])
    conv_mix_w = din("conv_mix_w", [3, 512])
    w_out_conv = din("w_out_conv", [512, D])
    w_out_attn = din("w_out_attn", [512, D])
    w_o = din("w_o", [D, D])
    norm_ffn_g = din("norm_ffn_g", [D])
    w_up = din("w_up", [D, 2 * FF])
    conv_ffn_w = din("conv_ffn_w", [3, 2 * FF])
    w_down = din("w_down", [FF, D])
    norm_f_g = din("norm_f_g", [1, D])
    out_d = nc.dram_tensor("out", [TOK, D], F32, kind="ExternalOutput").ap()
    ws = nc.dram_tensor("ws", [NUNIT, 128, 4096], BF16, kind="Internal").ap()

    es = ExitStack()
    with es:
        def sb(name, shape, dt):
            return es.enter_context(nc.sbuf_tensor(name, shape, dt))

        ident = sb("ident", [128, 128], BF16)
        trineg = sb("trineg", [128, 128], F32)
        negones = sb("negones", [128, 128], F32)
        maskw = sb("maskw", [128, 896], BF16)
        gmix = sb("gmix", [128, NKC], F32)
        gffn = sb("gffn", [128, NKC], F32)
        gfbc = sb("gfbc", [128, D], F32)
        cmw = sb("cmw", [128, 4, 3], F32)
        cfw = sb("cfw", [128, 44, 3], F32)
        bgate = sb("bgate", [128, 16], F32)
        bf4 = sb("bf4", [128, 4, H], F32)
        wf = sb("wf", [128, NKC, H], BF16)
        ss = sb("ss", [128, 4], F32)
        epsT = sb("epsT", [128, 1], F32)
        ss1 = sb("ss1", [128, 4], F32)
        lnv1 = sb("lnv1", [128, 4], F32)
        rstd1 = sb("rstd1", [128, 4], F32)
        sqj = sb("sqj", [128, D], BF16)
        identf = sb("identf", [128, 64], F32)
        lnv = sb("lnv", [128, 4], F32)
        rstd = sb("rstd", [128, 4], F32)
        PH = sb("PH", [128, 4, 2], F32)
        HS = sb("HS", [128, 44, 2], F32)
        carry = sb("carry", [128, 2], F32)
        x_tm = [sb("x_tm%d" % i, [128, 4, D], F32) for i in range(2)]
        xs = [sb("xs%d" % i, [128, D], BF16) for i in range(2)]
        hT = sb("hT", [128, NKC, TB], BF16)
        KA = sb("KA", [128, H, SEQ], BF16)
        VA = sb("VA", [128, NKT, 4, 192], BF16)
        wslot = [sb("wslot%d" % i, [128, 8, 512], BF16) for i in range(NSLOT)]
        ps_all = es.enter_context(nc.psum_tensor("ps_all", [128, 8, 512], F32))

        OVL_BYTES = 66048
        ovl = sb("ovl", [128, OVL_BYTES // 2], BF16)

        class Carver:
            def __init__(self):
                self.off = 0

            def get(self, free_shape, dt):
                n = int(np.prod(free_shape))
                nb = n * (4 if dt == F32 else 2)
                nb_al = (nb + 63) // 64 * 64
                assert self.off + nb_al <= OVL_BYTES, ("overlay overflow", self.off, nb_al)
                a = ovl[:, self.off // 2:(self.off + nb) // 2]
                self.off += nb_al
                if dt == F32:
                    a = a.bitcast(F32)
                if len(free_shape) == 2:
                    a = a.rearrange("p (a b) -> p a b", a=free_shape[0])
                elif len(free_shape) == 3:
                    a = a.rearrange("p (a b c) -> p a b c", a=free_shape[0], b=free_shape[1])
                return a

        cm = Carver()
        cu = cm.get([4, TB], F32)
        Pb = cm.get([4, TB + 2], F32)
        zT = cm.get([4, TB], BF16)
        QA = cm.get([H, TB], BF16)
        zf = cm.get([4, H], F32)
        ef = cm.get([4, H], F32)
        spf = cm.get([4, H], F32)
        Fq = cm.get([3, TB], BF16)
        Fk = cm.get([3, TB], BF16)
        PT = [cm.get([2, TB], BF16) for _ in range(3)]
        rl = [cm.get([TB], F32) for _ in range(2)]
        Fblk, r1 = rl[0], rl[1]
        oT = cm.get([4, TB], BF16)
        sgc = [cm.get([TB], BF16) for _ in range(2)]
        sga = [cm.get([TB], BF16) for _ in range(2)]
        t1 = [cm.get([TB], F32) for _ in range(2)]
        mT = cm.get([NKC, TB], BF16)
        mixer_bytes = cm.off
        cf = Carver()
        NR = 8
        A_sb = [cf.get([TB + 2], F32) for _ in range(NR)]
        acc = [cf.get([TB], F32) for _ in range(NR)]
        sa = [cf.get([TB], F32) for _ in range(2)]
        G = cf.get([NFC, TB], BF16)

        P = Prog()
        P.barrier_keys = {"constA", "constB"}
        P.batch_keys = {"fs": 96}

        def psn(b):
            return "ps%d" % b

        cres = {"A": [], "B": []}
        cdefer = []

        def cdma(out_ap, in_ap, g="B"):
            def go():
                cres[g].append("cdma%s%d" % (g, len(cres[g])))
                P.add("pool", lambda e, o=out_ap, i=in_ap: e.dma_start(out=o, in_=i, allow_slow_non_contiguous=True),
                      writes=[cres[g][-1]], dma="const" + g)
            if g == "A":
                go()
            else:
                cdefer.append(go)

        ovl32 = ovl[:, :].bitcast(F32)
        stgF = ovl32[:, 0:384].rearrange("p (k n) -> p k n", k=3)
        stgM = ovl32[:, 384:512]
        stgG = ovl32[:, 512:640]
        stgG2 = ovl32[:, 640:768]
        stgBg = ovl32[:, 768:896]
        for k in range(3):
            cdma(stgF[0:44, k, :], conv_ffn_w[k, :].rearrange("(c p) -> c p", p=128), "A")
        cdma(stgM[0:12, :], conv_mix_w.rearrange("k (c p) -> (k c) p", p=128), "A")
        cdma(stgG[0:8, :], norm_mix_g.rearrange("(c p) -> c p", p=128), "A")
        cdma(stgG2[0:8, :], norm_ffn_g.rearrange("(c p) -> c p", p=128), "A")
        cdma(stgBg[0:16, :], b_gate.rearrange("(c p) -> c p", p=128), "A")
        cdma(gfbc[:, :], norm_f_g[0:1, :].broadcast_to([128, D]))
        for i in range(4):
            cdma(bf4[:, i, :], b_f[0:1, :].broadcast_to([128, H]), "A")
        cdma(wf[:, :, :], w_in[:, 3072:3080].rearrange("(k p) n -> p k n", p=128), "A")

        def wsv(u):
            return ws[u].rearrange("p (k n) -> p k n", k=8)

        ws_parts = {}
        cvq = []

        def conv(u, k0, k1, n0, n1, src, grp):
            ws_parts.setdefault(u, []).append((k0, k1, n0, n1, src))

        def kin(w, r0, r1_, c0, c1):
            return w[r0:r1_, c0:c1].rearrange("(k p) n -> p k n", p=128)

        U_CIN, U_CC, U_CB, U_Q, U_K, U_V = 0, 1, 2, 3, 4, 5
        U_G = [6, 8, 9, 11]
        U_M = [7, 10]
        U_WO = [12, 13]
        U_UP = list(range(14, 25))
        U_DN = list(range(25, 31))
        conv(U_CIN, 0, 8, 0, 512, kin(w_in, 0, D, 1024, 1536), 0)
        conv(U_CC, 0, 8, 0, 512, kin(w_in, 0, D, 512, 1024), 0)
        conv(U_CB, 0, 8, 0, 512, kin(w_in, 0, D, 0, 512), 0)
        conv(U_Q, 0, 8, 0, 512, kin(w_in, 0, D, 1536, 2048), 0)
        conv(U_K, 0, 8, 0, 512, kin(w_in, 0, D, 2048, 2560), 0)
        conv(U_V, 0, 8, 0, 512, kin(w_in, 0, D, 2560, 3072), 0)
        for n in range(2):
            conv(U_M[n], 0, 4, 0, 512, kin(w_out_conv, 0, 512, n * 512, (n + 1) * 512), 1)
            conv(U_M[n], 4, 8, 0, 512, kin(w_out_attn, 0, 512, n * 512, (n + 1) * 512), 1)
        for q in range(4):
            conv(U_G[q], 0, 8, 0, 256, kin(w_in, 0, D, 3080 + q * 256, 3080 + (q + 1) * 256), 1)
            conv(U_G[q], 0, 8, 256, 512, kin(w_in, 0, D, 4104 + q * 256, 4104 + (q + 1) * 256), 1)
        for n in range(2):
            conv(U_WO[n], 0, 8, 0, 512, kin(w_o, 0, D, n * 512, (n + 1) * 512), 1)
        for u in range(11):
            conv(U_UP[u], 0, 8, 0, 256, kin(w_up, 0, D, 256 * u, 256 * (u + 1)), 2)
            conv(U_UP[u], 0, 8, 256, 512, kin(w_up, 0, D, FF + 256 * u, FF + 256 * (u + 1)), 2)
        DN_K = [(0, 8), (8, 16), (16, 22)]
        for n in range(2):
            for gi, (ka, kb) in enumerate(DN_K):
                conv(U_DN[n * 3 + gi], 0, kb - ka, 0, 512, kin(w_down, ka * 128, kb * 128, n * 512, (n + 1) * 512), 3)

        def pool_c(fn):
            P.add("pool", fn, reads=list(cres["A"]), writes=["cA"])

        pool_c(lambda e: e.memset(ident[:, :], 0.0))
        pool_c(lambda e: e.affine_select(out=ident[:, :], in_=ident[:, :], pattern=[[-1, 128]],
                                         compare_op=ALU.not_equal, fill=1.0, base=0, channel_multiplier=1))
        pool_c(lambda e: e.memset(trineg[:, :], -1.0))
        pool_c(lambda e: e.affine_select(out=trineg[:, :], in_=trineg[:, :], pattern=[[1, 128]],
                                         compare_op=ALU.is_ge, fill=0.0, base=0, channel_multiplier=-1))
        pool_c(lambda e: e.memset(negones[:, :], -1.0))
        pool_c(lambda e: e.memset(maskw[:, :], 0.0))
        pool_c(lambda e: e.affine_select(out=maskw[:, :], in_=maskw[:, :], pattern=[[1, 896]],
                                         compare_op=ALU.is_ge, fill=MASKV, base=-384, channel_multiplier=-1))
        P.add("pool", lambda e: e.memset(ss[:, :], 0.0), writes=["ss"])
        P.add("pool", lambda e: e.memset(ss1[:, :], 0.0), writes=["ss1"])
        pool_c(lambda e: e.memset(epsT[:, :], EPS))
        pool_c(lambda e: e.memset(identf[:, :], 0.0))
        pool_c(lambda e: e.affine_select(out=identf[:, :], in_=identf[:, :], pattern=[[-1, 64]],
                                         compare_op=ALU.not_equal, fill=1.0, base=0, channel_multiplier=1))
        for go in cdefer:
            go()
        cbk = 7
        def cmm(c0, c1, lhsT, K):
            P.add("pe", lambda e: e.matmul(ps_all[:, cbk, c0:c1], lhsT, identf[0:K, 0:K], start=True, stop=True),
                  reads=["cA"], writes=[psn(cbk)], grp="F")

        for k in range(3):
            cmm(k * 44, (k + 1) * 44, stgF[0:44, k, :], 44)
        cmm(132, 144, stgM[0:12, :], 12)
        cmm(144, 152, stgG[0:8, :], 8)
        cmm(152, 160, stgG2[0:8, :], 8)
        cmm(160, 176, stgBg[0:16, :], 16)
        P.add("dve", lambda e: e.tensor_copy(out=gmix[:, :], in_=ps_all[:, cbk, 144:152]), reads=[psn(cbk)], writes=["cG"])
        P.add("dve", lambda e: e.tensor_copy(out=gffn[:, :], in_=ps_all[:, cbk, 152:160]), reads=[psn(cbk)], writes=["cB0"])
        P.add("dve", lambda e: e.tensor_copy(out=bgate[:, :], in_=ps_all[:, cbk, 160:176]), reads=[psn(cbk)], writes=["cB1"])
        P.add("dve", lambda e: e.tensor_copy(out=cmw[:, :, :],
                                             in_=ps_all[:, cbk, 132:144].rearrange("p (k c) -> p c k", k=3)),
              reads=[psn(cbk)], writes=["cB2"])
        P.add("dve", lambda e: e.tensor_copy(out=cfw[:, :, :],
                                             in_=ps_all[:, cbk, 0:132].rearrange("p (k c) -> p c k", k=3)),
              reads=[psn(cbk), "cB0", "cB1", "cB2"] + list(cres["B"]), writes=["cB"])

        block_units = [U_CIN, U_CC, U_CB, U_Q, U_K, U_V,
                       U_G[0], U_M[0], U_G[1], U_G[2], U_M[1], U_G[3],
                       U_WO[0], U_WO[1]] + U_UP + [U_DN[0], U_DN[1], U_DN[3], U_DN[2], U_DN[4], U_DN[5]]
        stream = []
        for b in range(NBLK):
            stream += block_units
        st = {"issued": 0, "next_use": 0}

        def issue_load(k):
            if k >= len(stream):
                return
            assert k == st["issued"]
            u = stream[k]
            s = k % NSLOT
            nk = 6 if u in (U_DN[2], U_DN[5]) else 8
            if k < len(block_units):
                R = "wslot%d" % s
                if u in U_UP or u in U_G:
                    prev = P.pending(R)
                    for pi, (k0, k1, n0, n1, src) in enumerate(ws_parts[u]):
                        P.add("pool", lambda e, o=wslot[s][:, k0:k1, n0:n1], i=src: e.dma_start(out=o, in_=i),
                              writes=[R + "ab"[pi]], dma="wp%d%s" % (s, "ab"[pi]), after=prev)
                else:
                    for (k0, k1, n0, n1, src) in ws_parts[u]:
                        P.add("pool", lambda e, o=wslot[s][:, k0:k1, n0:n1], i=src: e.dma_start(out=o, in_=i),
                              writes=[R, R + "a", R + "b"], dma="wp%d" % s)
                P.add("sp", lambda e, o=wsv(u)[:, 0:nk, :], i=wslot[s][:, 0:nk, :]: e.dma_start(out=o, in_=i),
                      reads=[R, R + "a", R + "b"], writes=["ws%d" % u], dma="wst%d" % s)
            else:
                P.add("sp", lambda e, o=wslot[s][:, 0:nk, :], i=wsv(u)[:, 0:nk, :]: e.dma_start(out=o, in_=i),
                      reads=["ws%d" % u], writes=["wslot%d" % s, "wslot%da" % s, "wslot%db" % s], dma="w%d" % s)
            st["issued"] += 1

        P.add("sp", lambda e: e.dma_start(out=x_tm[0][:, :, :],
                                          in_=x_d[0:TB, :].rearrange("(i p) d -> p i d", p=128)),
              writes=["x0.%d" % i for i in range(4)], dma="xl0")
        for k0 in range(NSLOT):
            issue_load(k0)
        P.add("pool", lambda e: e.memset(VA[:, :, :, 64:128], 1.0), writes=["VA"])
        P.add("pool", lambda e: e.memset(KA[64:70, :, :], 1.0), writes=["KAaug.0", "KAaug.1", "KAaug.2"])

        def take_unit(u_expected):
            k = st["next_use"]
            assert stream[k] == u_expected, (k, stream[k], u_expected)
            assert k < st["issued"]
            st["next_use"] += 1
            s = k % NSLOT
            return wslot[s], "wslot%d" % s

        rel_state = {"k": 0}

        def rel():
            k = rel_state["k"]
            rel_state["k"] += 1
            issue_load(k + NSLOT)

        bank_rr = {"i": 0}

        def next_bank():
            b = bank_rr["i"] % 8
            bank_rr["i"] += 1
            return b

        STA = {"ss": ss, "lnv": lnv, "rstd": rstd, "n": ("ss", "lnv", "rstd")}
        STB = {"ss": ss1, "lnv": lnv1, "rstd": rstd1, "n": ("ss1", "lnv1", "rstd1")}

        def norm_stats(xb, S):
            X = x_tm[xb]
            rs, rl_, rr = S["n"]
            for i in range(4):
                P.add("act", lambda e, i=i: e.activation(out=sqj[:, :], in_=X[:, i, :], func=AF.Square,
                                                         accum_out=S["ss"][:, i:i + 1]),
                      reads=["x%d.%d" % (xb, i)], writes=["sqj", rs])
            P.add("act", lambda e: e.activation(out=S["lnv"][:, :], in_=S["ss"][:, :], func=AF.Ln, scale=1.0 / D,
                                                bias=epsT[:, 0:1]),
                  reads=[rs, "cA"], writes=[rl_])
            P.add("act", lambda e: e.activation(out=S["rstd"][:, :], in_=S["lnv"][:, :], func=AF.Exp, scale=-0.5),
                  reads=[rl_], writes=[rr])
            P.add("dve", lambda e: e.memset(S["ss"][:, :], 0.0), reads=[rl_], writes=[rs])

        def norm_scale(xb, i, S):
            X = x_tm[xb]
            if i % 2 == 0:
                P.add("dve", lambda e: e.tensor_scalar(out=xs[i % 2][:, :], in0=X[:, i, :],
                                                       scalar1=S["rstd"][:, i:i + 1], scalar2=None, op0=ALU.mult),
                      reads=["x%d.%d" % (xb, i), S["n"][2]], writes=["xs%d" % (i % 2)])
            else:
                P.add("act", lambda e: e.activation(out=xs[i % 2][:, :], in_=X[:, i, :], func=AF.Copy,
                                                    scale=S["rstd"][:, i:i + 1]),
                      reads=["x%d.%d" % (xb, i), S["n"][2]], writes=["xs%d" % (i % 2)])

        def norm_transp(i, tb):
            for c in range(NKC):
                bk = tb[c // 2]
                o = ps_all[:, bk, :].bitcast(BF16)[:, (c % 2) * 512 + i * 128:(c % 2) * 512 + (i + 1) * 128]
                P.add("pe", lambda e, o=o, c=c: e.transpose(o, xs[i % 2][:, c * 128:(c + 1) * 128], ident[:, :]),
                      reads=["xs%d" % (i % 2), "cA"], writes=[psn(bk)])

        def norm_evac(tb, gvec, gres):
            for c in range(NKC):
                bk = tb[c // 2]
                src = ps_all[:, bk, :].bitcast(BF16)[:, (c % 2) * 512:(c % 2 + 1) * 512]
                if (c // 2) % 2 == 0:
                    P.add("act", lambda e, c=c, src=src: e.activation(out=hT[:, c, :], in_=src, func=AF.Copy,
                                                                      scale=gvec[:, c:c + 1]),
                          reads=[psn(bk), gres], writes=["hT.%d" % c])
                else:
                    P.add("dve", lambda e, c=c, src=src: e.tensor_scalar(out=hT[:, c, :], in0=src,
                                                                         scalar1=gvec[:, c:c + 1], scalar2=None,
                                                                         op0=ALU.mult),
                          reads=[psn(bk), gres], writes=["hT.%d" % c])

        def rmsnorm_to_hT(xb, gvec, tag, grp):
            gres = "cG" if tag == "n1" else "cB"
            S = STB if tag == "n1" else STA
            norm_stats(xb, S)
            tb = [next_bank() for _ in range(4)]
            for i in range(4):
                norm_scale(xb, i, S)
                norm_transp(i, tb)
            norm_evac(tb, gvec, gres)

        hT_all = ["hT.%d" % c for c in range(NKC)]

        def proj_fm(wt, wname, col0, bk, grp):
            for kc in range(NKC):
                P.add("pe", lambda e, kc=kc: e.matmul(ps_all[:, bk, :], wt[:, kc, col0:col0 + 128], hT[:, kc, :],
                                                      start=(kc == 0), stop=(kc == NKC - 1)),
                      reads=[wname, "hT.%d" % kc], writes=[psn(bk)], grp=grp)

        pending_store = []
        for blk in range(0 if _DBG["stop"] == 1 else NBLK):
            seq = blk // NQB
            qb = blk % NQB
            xb = blk % 2
            t0 = qb * TB
            row0 = blk * TB
            nkt = 4 * (qb + 1)
            X = x_tm[xb]
            xr = ["x%d.%d" % (xb, i) for i in range(4)]

            if blk == 0:
                rmsnorm_to_hT(xb, gmix, "n1", "M")

            if _DBG["stop"] in (2, 20, 21, 22, 23, 24):
                break
            fb = next_bank()
            for i in range(4):
                for kc in range(NKC):
                    P.add("pe", lambda e, i=i, kc=kc: e.matmul(ps_all[:, fb, i * H:(i + 1) * H],
                                                               hT[:, kc, i * 128:(i + 1) * 128], wf[:, kc, :],
                                                               start=(kc == 0), stop=(kc == NKC - 1)),
                          reads=["hT.%d" % kc, "cA"], writes=[psn(fb)], grp="M")
            P.add("dve", lambda e: e.tensor_tensor(out=zf[:, :, :],
                                                   in0=ps_all[:, fb, 0:4 * H].rearrange("p (i h) -> p i h", i=4),
                                                   in1=bf4[:, :, :], op=ALU.add),
                  reads=[psn(fb), "cA"], writes=["zf"], grp="M")
            P.add("act", lambda e: e.activation(out=ef[:, :, :], in_=zf[:, :, :], func=AF.Exp, scale=-1.0),
                  reads=["zf"], writes=["ef"], grp="M")
            P.add("act", lambda e: e.activation(out=spf[:, :, :], in_=ef[:, :, :], func=AF.Ln, bias=1.0),
                  reads=["ef"], writes=["spf"], grp="M")
            wt, wn = take_unit(U_CIN)
            for c in range(4):
                bk = next_bank()
                proj_fm(wt, wn, c * 128, bk, "M")
                P.add("act", lambda e, c=c, bk=bk: e.activation(out=cu[:, c, :], in_=ps_all[:, bk, :], func=AF.Copy),
                      reads=[psn(bk)], writes=["cu.%d" % c], grp="M")
            rel()
            if qb == 0:
                P.add("pool", lambda e: e.memset(Pb[:, :, 0:2], 0.0), writes=["P.halo"], grp="M")
            else:
                P.add("pool", lambda e: e.tensor_copy(out=Pb[:, :, 0:2], in_=PH[:, :, :]), reads=["PH"],
                      writes=["P.halo"], grp="M")
            wt, wn = take_unit(U_CC)
            for c in range(4):
                bk = next_bank()
                proj_fm(wt, wn, c * 128, bk, "M")
                P.add("dve", lambda e, c=c, bk=bk: e.tensor_tensor(out=Pb[:, c, 2:TB + 2], in0=ps_all[:, bk, :],
                                                                   in1=cu[:, c, :], op=ALU.mult),
                      reads=[psn(bk), "cu.%d" % c], writes=["P.%d" % c], grp="M")
                P.add("act", lambda e, c=c: e.activation(out=cu[:, c, :], in_=Pb[:, c, 2:TB + 2], func=AF.Copy,
                                                         scale=cmw[:, c, 2:3]),
                      reads=["P.%d" % c, "cB"], writes=["cu.%d" % c], grp="M")
                P.add("dve", lambda e, c=c: e.scalar_tensor_tensor(out=cu[:, c, :], in0=Pb[:, c, 1:TB + 1],
                                                                    scalar=cmw[:, c, 1:2], in1=cu[:, c, :],
                                                                    op0=ALU.mult, op1=ALU.add),
                      reads=["P.%d" % c, "P.halo", "cB", "cu.%d" % c], writes=["cu.%d" % c], grp="M")
                P.add("dve", lambda e, c=c: e.scalar_tensor_tensor(out=cu[:, c, :], in0=Pb[:, c, 0:TB],
                                                                    scalar=cmw[:, c, 0:1], in1=cu[:, c, :],
                                                                    op0=ALU.mult, op1=ALU.add),
                      reads=["P.%d" % c, "P.halo", "cB", "cu.%d" % c], writes=["cu.%d" % c], grp="M")
            rel()
            P.add("pool", lambda e: e.tensor_copy(out=PH[:, :, :], in_=Pb[:, :, TB:TB + 2]),
                  reads=["P.%d" % c for c in range(4)], writes=["PH"], grp="M")
            wt, wn = take_unit(U_CB)
            for c in range(4):
                bk = next_bank()
                proj_fm(wt, wn, c * 128, bk, "M")
                P.add("dve", lambda e, c=c, bk=bk: e.tensor_tensor(out=zT[:, c, :], in0=ps_all[:, bk, :],
                                                                   in1=cu[:, c, :], op=ALU.mult),
                      reads=[psn(bk), "cu.%d" % c], writes=["zT.%d" % c], grp="M")

            rel()
            if _DBG["stop"] == 3:
                break
            fb2 = next_bank()
            for i in range(4):
                P.add("pe", lambda e, i=i: e.matmul(ps_all[0:H, fb2, i * 128:(i + 1) * 128], spf[:, i, :],
                                                    trineg[:, :], start=True, stop=(i == 0)),
                      reads=["spf", "cA"], writes=[psn(fb2)], grp="M")
                for i2 in range(i):
                    P.add("pe", lambda e, i=i, i2=i2: e.matmul(ps_all[0:H, fb2, i * 128:(i + 1) * 128],
                                                               spf[:, i2, :], negones[:, :], start=False,
                                                               stop=(i2 == i - 1)),
                          reads=["spf", "cA"], writes=[psn(fb2)], grp="M")
            cb_ = blk % 2
            if qb == 0:
                P.add("dve", lambda e, cb_=cb_: e.memset(carry[0:H, cb_:cb_ + 1], 0.0), writes=["carry%d" % cb_])
            P.add("dve", lambda e, cb_=cb_: e.tensor_scalar(out=Fblk[0:H, :], in0=ps_all[0:H, fb2, :],
                                                            scalar1=carry[0:H, cb_:cb_ + 1], scalar2=None,
                                                            op0=ALU.add),
                  reads=[psn(fb2), "carry%d" % cb_], writes=["rl0"], grp="M")
            f_ops = []
            f_ops.append(lambda: P.add("dve", lambda e, cb_=cb_: e.tensor_copy(out=carry[0:H, 1 - cb_:2 - cb_], in_=Fblk[0:H, TB - 1:TB]),
                      reads=["rl0"], writes=["carry%d" % (1 - cb_)], grp="M"))
            f_ops.append(lambda: P.add("dve", lambda e: e.tensor_copy(out=Fq[0:H, 0, :], in_=Fblk[0:H, :]),
                      reads=["rl0"], writes=["Fq0"], grp="M"))
            f_ops.append(lambda: P.add("dve", lambda e: e.tensor_tensor(out=r1[0:H, :], in0=Fblk[0:H, :], in1=Fq[0:H, 0, :],
                                                        op=ALU.subtract),
                      reads=["rl0", "Fq0"], writes=["rl1"], grp="M"))
            f_ops.append(lambda: P.add("dve", lambda e: e.tensor_copy(out=Fq[0:H, 1, :], in_=r1[0:H, :]),
                      reads=["rl1"], writes=["Fq1"], grp="M"))
            f_ops.append(lambda: P.add("dve", lambda e: e.tensor_tensor(out=Fblk[0:H, :], in0=r1[0:H, :], in1=Fq[0:H, 1, :],
                                                        op=ALU.subtract),
                      reads=["rl1", "Fq1"], writes=["rl0"], grp="M"))
            f_ops.append(lambda: P.add("dve", lambda e: e.tensor_copy(out=Fq[0:H, 2, :], in_=Fblk[0:H, :]),
                      reads=["rl0"], writes=["Fq2"], grp="M"))
            f_ops.append(lambda: P.add("dve", lambda e: e.tensor_scalar(out=Fk[0:H, :, :], in0=Fq[0:H, :, :], scalar1=-1.0,
                                                        scalar2=None, op0=ALU.mult),
                      reads=["Fq0", "Fq1", "Fq2"], writes=["Fk"], grp="M"))

            if _DBG["stop"] == 4:
                break
            wt, wn = take_unit(U_Q)
            for c in range(4):
                bk = next_bank()
                proj_fm(wt, wn, c * 128, bk, "M")
                P.add("act", lambda e, c=c, bk=bk: e.activation(out=QA[0:64, 2 * c, :], in_=ps_all[0:64, bk, :],
                                                                func=AF.Copy, scale=0.125),
                      reads=[psn(bk)], writes=["QAq.%d" % (2 * c)], grp="M")
                P.add("dve", lambda e, c=c, bk=bk: e.tensor_scalar(out=QA[0:64, 2 * c + 1, :],
                                                                   in0=ps_all[64:128, bk, :], scalar1=0.125,
                                                                   scalar2=None, op0=ALU.mult),
                      reads=[psn(bk)], writes=["QAq.%d" % (2 * c + 1)], grp="M")
                if f_ops:
                    f_ops.pop(0)()
            rel()
            wt, wn = take_unit(U_K)
            for c in range(4):
                bk = next_bank()
                proj_fm(wt, wn, c * 128, bk, "M")
                P.add("act", lambda e, c=c, bk=bk: e.activation(out=KA[0:64, 2 * c, t0:t0 + TB],
                                                                in_=ps_all[0:64, bk, :], func=AF.Copy),
                      reads=[psn(bk)], writes=["KAk.%d" % (2 * c)], grp="M")
                P.add("dve", lambda e, c=c, bk=bk: e.tensor_copy(out=KA[0:64, 2 * c + 1, t0:t0 + TB],
                                                                 in_=ps_all[64:128, bk, :]),
                      reads=[psn(bk)], writes=["KAk.%d" % (2 * c + 1)], grp="M")
                if f_ops:
                    f_ops.pop(0)()
            rel()
            while f_ops:
                f_ops.pop(0)()
            P.add("pool", lambda e: e.memset(QA[64:70, :, :], 1.0), writes=["QAaug.0", "QAaug.1", "QAaug.2"], grp="M")
            for r in range(3):
                P.add("pool", lambda e, r=r: e.dma_start(out=QA[64 + r:65 + r, :, :], in_=Fq[0:H, r, :]),
                      reads=["Fq%d" % r], writes=["QAaug.%d" % r], dma="fs", grp="M")
                P.add("pool", lambda e, r=r: e.dma_start(out=KA[67 + r:68 + r, :, t0:t0 + TB], in_=Fk[0:H, r, :]),
                      reads=["Fk"], writes=["KAaug.%d" % r], dma="fs", grp="M")

            wt, wn = take_unit(U_V)
            for i in range(4):
                bk = next_bank()
                kt = 4 * qb + i
                for kc in range(NKC):
                    P.add("pe", lambda e, i=i, kc=kc, bk=bk: e.matmul(ps_all[:, bk, :],
                                                                      hT[:, kc, i * 128:(i + 1) * 128],
                                                                      wt[:, kc, :], start=(kc == 0),
                                                                      stop=(kc == NKC - 1)),
                          reads=[wn, "hT.%d" % kc], writes=[psn(bk)], grp="M")
                pv_ = ps_all[:, bk, :].rearrange("p (j e d) -> p j e d", j=4, e=2)
                P.add("act", lambda e, kt=kt, pv_=pv_: e.activation(out=VA[:, kt, :, 0:64], in_=pv_[:, :, 0, :],
                                                                    func=AF.Copy),
                      reads=[psn(bk)], writes=["VA"], grp="M")
                P.add("dve", lambda e, kt=kt, pv_=pv_: e.tensor_copy(out=VA[:, kt, :, 128:192], in_=pv_[:, :, 1, :]),
                      reads=[psn(bk)], writes=["VA"], grp="M")

            rel()
            if pending_store:
                pb_row0, pb_xb = pending_store.pop()
                P.add("sp", lambda e, r=pb_row0, b_=pb_xb: e.dma_start(
                    out=out_d[r:r + TB, :].rearrange("(i p) d -> p i d", p=128), in_=x_tm[b_][:, :, :]),
                    reads=["x%d.%d" % (pb_xb, i) for i in range(4)], writes=["outstore%d" % pb_xb], dma="st%d" % pb_xb)
            if blk + 1 < NBLK:
                nb_ = (blk + 1) % 2
                P.add("sp", lambda e, r=row0 + TB, b_=nb_: e.dma_start(
                    out=x_tm[b_][:, :, :], in_=x_d[r:r + TB, :].rearrange("(i p) d -> p i d", p=128)),
                    reads=["outstore%d" % nb_], writes=["x%d.%d" % (nb_, i) for i in range(4)], dma="xl%d" % nb_)
            if _DBG["stop"] == 5:
                break
            ng = nkt // 2
            items = [(h, g) for h in range(H) for g in range(ng)]
            LAG = 2

            def grp_c0(g):
                return 256 if 2 * g == 4 * qb + 2 else 0

            def emit_qk(idx, h, g):
                sg_ = idx % 3
                qk_reads = (["KAk.%d" % h, "QAq.%d" % h] + ["KAaug.%d" % r_ for r_ in range(3)]
                            + ["QAaug.%d" % r_ for r_ in range(3)])
                c0 = grp_c0(g)
                for sl in range(2):
                    kt = 2 * g + sl
                    bk = 2 * sg_ + sl
                    diag = kt >= 4 * qb
                    kap = KA[0:70, h, kt * 128:(kt + 1) * 128]
                    P.add("pe", lambda e, bk=bk, kap=kap, diag=diag: e.matmul(
                        ps_all[:, bk, c0:TB], kap, QA[0:70, h, c0:TB], start=True, stop=(not diag)),
                        reads=qk_reads, writes=[psn(bk)], grp="M")
                    if diag:
                        j = kt - 4 * qb
                        c1 = 128 * (j + 1)
                        m0 = 384 - 128 * j + c0
                        P.add("pe", lambda e, bk=bk, c1=c1, m0=m0: e.matmul(
                            ps_all[:, bk, c0:c1], ident[:, :], maskw[:, m0:m0 + (c1 - c0)], start=False, stop=True),
                            reads=["cA"], writes=[psn(bk)], grp="M")
                P.add("act", lambda e, sg_=sg_: e.activation(out=PT[sg_][:, :, c0:TB],
                                                             in_=ps_all[:, 2 * sg_:2 * sg_ + 2, c0:TB], func=AF.Exp),
                      reads=[psn(2 * sg_), psn(2 * sg_ + 1)], writes=["PT%d" % sg_], grp="M")

            def emit_pv(idx, h, g):
                sg_ = idx % 3
                pvb = 6 + h % 2
                j = h // 2
                eo = h % 2
                for sl in range(2):
                    kt = 2 * g + sl
                    c0 = grp_c0(g)
                    P.add("pe", lambda e, kt=kt, sl=sl, j=j, eo=eo, pvb=pvb, sg_=sg_, c0=c0: e.matmul(
                        ps_all[:, pvb, c0:TB], VA[:, kt, j, eo * 64:eo * 64 + 128], PT[sg_][:, sl, c0:TB],
                        start=(kt == 0), stop=(kt == nkt - 1)),
                        reads=["VA", "PT%d" % sg_], writes=[psn(pvb)], grp="M")
                if g == ng - 1:
                    oP = slice(0, 64) if eo == 0 else slice(64, 128)
                    lP = slice(64, 128) if eo == 0 else slice(0, 64)
                    k = h % 2
                    if qb <= 1:
                        P.add("act", lambda e, lP=lP, k=k, pvb=pvb: e.activation(out=t1[k][lP, :], in_=ps_all[lP, pvb, :],
                                                                                 func=AF.Ln),
                              reads=[psn(pvb)], writes=["t1%d" % k], grp="M")
                        P.add("act", lambda e, lP=lP, k=k: e.activation(out=t1[k][lP, :], in_=t1[k][lP, :], func=AF.Exp,
                                                                        scale=-1.0),
                              reads=["t1%d" % k], writes=["t1%d" % k], grp="M")
                        P.add("dve", lambda e, oP=oP, lP=lP, k=k: e.tensor_copy(out=rl[k][oP, :], in_=t1[k][lP, :]),
                              reads=["t1%d" % k], writes=["rl%d" % k], grp="M")
                    else:
                        P.add("dve", lambda e, oP=oP, lP=lP, k=k, pvb=pvb: e.reciprocal(out=rl[k][oP, :],
                                                                                        in_=ps_all[lP, pvb, :]),
                              reads=[psn(pvb)], writes=["rl%d" % k], grp="M")
                    P.add("dve", lambda e, oP=oP, k=k, pvb=pvb, j=j: e.tensor_tensor(
                        out=oT[oP, j, :], in0=ps_all[oP, pvb, :], in1=rl[k][oP, :], op=ALU.mult),
                        reads=[psn(pvb), "rl%d" % k], writes=["oT.%d" % h], grp="M")

            for idx, (h, g) in enumerate(items):
                emit_qk(idx, h, g)
                if idx >= LAG:
                    emit_pv(idx - LAG, *items[idx - LAG])
            for idx in range(max(0, len(items) - LAG), len(items)):
                emit_pv(idx, *items[idx])
            bank_rr["i"] = 0

            if _DBG["stop"] == 6:
                break
            for n in range(2):
                for qq in range(2):
                    q = 2 * n + qq
                    wg, wgn = take_unit(U_G[q])
                    if qq == 0:
                        wm, wmn = take_unit(U_M[n])
                    for cc in range(2):
                        c = 2 * q + cc
                        r = c % 2
                        b_gc, b_ga = next_bank(), next_bank()
                        proj_fm(wg, wgn + "a", cc * 128, b_gc, "M")
                        P.add("act", lambda e, c=c, r=r, b_gc=b_gc: e.activation(
                            out=sgc[r][:, :], in_=ps_all[:, b_gc, :], func=AF.Sigmoid, bias=bgate[:, c:c + 1]),
                            reads=[psn(b_gc), "cB"], writes=["sgc%d" % r], grp="M")
                        proj_fm(wg, wgn + "b", 256 + cc * 128, b_ga, "M")
                        P.add("act", lambda e, c=c, r=r, b_ga=b_ga: e.activation(
                            out=sga[r][:, :], in_=ps_all[:, b_ga, :], func=AF.Sigmoid, bias=bgate[:, 8 + c:9 + c]),
                            reads=[psn(b_ga), "cB"], writes=["sga%d" % r], grp="M")
                    for cc in range(2):
                        c = 2 * q + cc
                        r = c % 2
                        cl = c - 4 * n
                        b_yc, b_ya = next_bank(), next_bank()
                        for kc in range(4):
                            P.add("pe", lambda e, kc=kc, cl=cl, b_yc=b_yc: e.matmul(
                                ps_all[:, b_yc, :], wm[:, kc, cl * 128:(cl + 1) * 128], zT[:, kc, :],
                                start=(kc == 0), stop=(kc == 3)),
                                reads=[wmn, "zT.%d" % kc], writes=[psn(b_yc)], grp="M")
                        for kc in range(4):
                            P.add("pe", lambda e, kc=kc, cl=cl, b_ya=b_ya: e.matmul(
                                ps_all[:, b_ya, :], wm[:, 4 + kc, cl * 128:(cl + 1) * 128], oT[:, kc, :],
                                start=(kc == 0), stop=(kc == 3)),
                                reads=[wmn, "oT.%d" % (2 * kc), "oT.%d" % (2 * kc + 1)], writes=[psn(b_ya)], grp="M")
                        P.add("dve", lambda e, r=r, b_yc=b_yc: e.tensor_tensor(out=t1[r][:, :], in0=ps_all[:, b_yc, :],
                                                                               in1=sgc[r][:, :], op=ALU.mult),
                              reads=[psn(b_yc), "sgc%d" % r], writes=["t1%d" % r], grp="M")
                        P.add("dve", lambda e, r=r, b_ya=b_ya: e.tensor_tensor(out=sga[r][:, :], in0=ps_all[:, b_ya, :],
                                                                               in1=sga[r][:, :], op=ALU.mult),
                              reads=[psn(b_ya), "sga%d" % r], writes=["sga%d" % r], grp="M")
                        P.add("pool", lambda e, r=r, c=c: e.tensor_tensor(out=mT[:, c, :], in0=t1[r][:, :],
                                                                          in1=sga[r][:, :], op=ALU.add),
                              reads=["t1%d" % r, "sga%d" % r], writes=["mT.%d" % c], grp="M")
                    rel()
                    if qq == 1:
                        rel()
            P.add("act", lambda e: e.activation(out=lnv[:, 0:1], in_=epsT[:, 0:1], func=AF.Ln), reads=["cA"],
                  writes=["lnv"])
            wo = [take_unit(U_WO[0]), take_unit(U_WO[1])]
            for i in range(4):
                for n in range(2):
                    wt, wn = wo[n]
                    bk = next_bank()
                    for kc in range(NKC):
                        P.add("pe", lambda e, i=i, kc=kc, bk=bk, wt=wt: e.matmul(
                            ps_all[:, bk, :], mT[:, kc, i * 128:(i + 1) * 128], wt[:, kc, :],
                            start=(kc == 0), stop=(kc == NKC - 1)),
                            reads=[wn, "mT.%d" % kc], writes=[psn(bk)], grp="M")
                    P.add("dve", lambda e, i=i, n=n, bk=bk: e.tensor_tensor(
                        out=X[:, i, n * 512:(n + 1) * 512], in0=ps_all[:, bk, :], in1=X[:, i, n * 512:(n + 1) * 512],
                        op=ALU.add),
                        reads=[psn(bk), xr[i]], writes=[xr[i]], grp="M")
            rel()
            rel()

            if _DBG["stop"] == 7:
                break
            rmsnorm_to_hT(xb, gffn, "n2", "F")

            rr = {"i": 0}

            def ffn_chunk(wt, wn, colbase, jj, want_silu, j):
                bk = next_bank()
                proj_fm(wt, wn, colbase, bk, "F")
                r = rr["i"] % NR
                rr["i"] += 1
                A = A_sb[r]
                ac = acc[r]
                P.add("act", lambda e, A=A, bk=bk: e.activation(out=A[:, 2:TB + 2], in_=ps_all[:, bk, :], func=AF.Copy),
                      reads=[psn(bk)], writes=["A%d" % r], grp="F")
                P.add("act", lambda e, ac=ac, bk=bk, jj=jj: e.activation(out=ac[:, :], in_=ps_all[:, bk, :],
                                                                         func=AF.Copy, scale=cfw[:, jj, 2:3]),
                      reads=[psn(bk), "cB"], writes=["acc%d" % r], grp="F")
                if qb == 0:
                    P.add("pool", lambda e, A=A: e.memset(A[:, 0:2], 0.0), writes=["Ah%d" % r], grp="F")
                else:
                    P.add("pool", lambda e, A=A, jj=jj: e.tensor_copy(out=A[:, 0:2], in_=HS[:, jj, :]),
                          reads=["HS.%d" % jj], writes=["Ah%d" % r], grp="F")
                P.add("pool", lambda e, A=A, jj=jj: e.tensor_copy(out=HS[:, jj, :], in_=A[:, TB:TB + 2]),
                      reads=["A%d" % r], writes=["HS.%d" % jj], grp="F")
                eng = "dve"
                P.add(eng, lambda e, A=A, ac=ac, jj=jj: e.scalar_tensor_tensor(
                    out=ac[:, :], in0=A[:, 1:TB + 1], scalar=cfw[:, jj, 1:2], in1=ac[:, :], op0=ALU.mult, op1=ALU.add),
                    reads=["A%d" % r, "Ah%d" % r, "acc%d" % r, "cB"], writes=["acc%d" % r], grp="F")
                P.add(eng, lambda e, A=A, ac=ac, jj=jj: e.scalar_tensor_tensor(
                    out=ac[:, :], in0=A[:, 0:TB], scalar=cfw[:, jj, 0:1], in1=ac[:, :], op0=ALU.mult, op1=ALU.add),
                    reads=["A%d" % r, "Ah%d" % r, "acc%d" % r, "cB"], writes=["acc%d" % r], grp="F")
                return r

            pend_fin = []

            def ffn_finish(j, ra, rb):
                sr = j % 2
                P.add("act", lambda e, ra=ra, sr=sr: e.activation(out=sa[sr][:, :], in_=acc[ra][:, :], func=AF.Silu),
                      reads=["acc%d" % ra], writes=["sa%d" % sr], grp="F")
                P.add("pool", lambda e, rb=rb, sr=sr, j=j: e.tensor_tensor(out=G[:, j, :], in0=sa[sr][:, :],
                                                                           in1=acc[rb][:, :], op=ALU.mult),
                      reads=["sa%d" % sr, "acc%d" % rb], writes=["G.%d" % j], grp="F")

            for u in range(11):
                wt, wn = take_unit(U_UP[u])
                for cl in range(2):
                    j = 2 * u + cl
                    ra = ffn_chunk(wt, wn + "a", cl * 128, j, True, j)
                    rb = ffn_chunk(wt, wn + "b", 256 + cl * 128, NFC + j, False, j)
                    pend_fin.append((j, ra, rb))
                    if len(pend_fin) > 1:
                        ffn_finish(*pend_fin.pop(0))
                rel()
            while pend_fin:
                ffn_finish(*pend_fin.pop(0))

            if _DBG["stop"] == 8:
                break
            nxt = blk + 1 < NBLK
            nxb = (blk + 1) % 2
            if nxt:
                norm_stats(nxb, STB)
                norm_scale(nxb, 0, STB)
                norm_scale(nxb, 1, STB)
            bankset = [[next_bank() for _ in range(4)], [next_bank() for _ in range(4)]]

            def dn(n, gi):
                ka, kb = DN_K[gi]
                wt, wn = take_unit(U_DN[n * 3 + gi])
                for i in range(4):
                    for j in range(ka, kb):
                        P.add("pe", lambda e, i=i, j=j, bk=bankset[n][i]: e.matmul(
                            ps_all[:, bk, :], G[:, j, i * 128:(i + 1) * 128], wt[:, j - ka, :],
                            start=(j == 0), stop=(j == NFC - 1)),
                            reads=[wn, "G.%d" % j], writes=[psn(bankset[n][i])], grp="F")
                rel()

            def dn_add(n):
                for i in range(4):
                    P.add("dve", lambda e, i=i, bk=bankset[n][i]: e.tensor_tensor(
                        out=X[:, i, n * 512:(n + 1) * 512], in0=ps_all[:, bk, :], in1=X[:, i, n * 512:(n + 1) * 512],
                        op=ALU.add),
                        reads=[psn(bankset[n][i]), xr[i]], writes=[xr[i]], grp="F")

            dn(0, 0)
            dn(0, 1)
            dn(1, 0)
            dn(0, 2)
            dn_add(0)
            dn(1, 1)
            ntb = bankset[0]
            if nxt:
                norm_transp(0, ntb)
                norm_transp(1, ntb)
                norm_scale(nxb, 2, STB)
                norm_scale(nxb, 3, STB)
            dn(1, 2)
            dn_add(1)
            if nxt:
                norm_transp(2, ntb)
                norm_transp(3, ntb)
                norm_evac(ntb, gmix, "cG")

            for i in range(4):
                P.add("act", lambda e, i=i: e.activation(out=sqj[:, :], in_=X[:, i, :], func=AF.Square,
                                                         accum_out=ss[:, i:i + 1]),
                      reads=[xr[i]], writes=["sqj", "ss"])
            P.add("act", lambda e: e.activation(out=lnv[:, :], in_=ss[:, :], func=AF.Ln, scale=1.0 / D, bias=epsT[:, 0:1]),
                  reads=["ss", "cA"], writes=["lnv"])
            P.add("act", lambda e: e.activation(out=rstd[:, :], in_=lnv[:, :], func=AF.Exp, scale=-0.5),
                  reads=["lnv"], writes=["rstd"])
            P.add("dve", lambda e: e.memset(ss[:, :], 0.0), reads=["lnv"], writes=["ss"])
            for i in range(4):
                P.add("act", lambda e, i=i: e.activation(out=X[:, i, :], in_=X[:, i, :], func=AF.Copy,
                                                         scale=rstd[:, i:i + 1]),
                      reads=[xr[i], "rstd"], writes=[xr[i]])
                P.add("pool", lambda e, i=i: e.tensor_tensor(out=X[:, i, :], in0=X[:, i, :], in1=gfbc[:, :],
                                                             op=ALU.mult),
                      reads=[xr[i], "cB"], writes=[xr[i]])
            pending_store.append((row0, xb))

        if not pending_store:
            pending_store.append((0, 0))
        pb_row0, pb_xb = pending_store.pop()
        P.add("sp", lambda e, r=pb_row0, b_=pb_xb: e.dma_start(
            out=out_d[r:r + TB, :].rearrange("(i p) d -> p i d", p=128), in_=x_tm[b_][:, :, :]),
            reads=["x%d.%d" % (pb_xb, i) for i in range(4)], writes=["outstore%d" % pb_xb], dma="st%d" % pb_xb)
        P.add("sp", None, reads=["outstore0", "outstore1", "cA", "cB", "KAaug.0", "ss"] + ["wslot%d" % i for i in range(NSLOT)])

        _LASTP.append(P)
        P.emit(nc, es)
    return nc


_CACHE = {}


def _get_program(NSEQ, SEQ):
    key = (NSEQ, SEQ)
    if key not in _CACHE:
        _CACHE[key] = build_program(NSEQ, SEQ)
    return _CACHE[key]


def kernel(x, norm_mix_g, w_in, b_f, b_gate, conv_mix_w, w_out_conv, w_out_attn, w_o, norm_ffn_g, w_up,
           conv_ffn_w, w_down, norm_f_g, _n_cores=N_CORES):
    x = np.asarray(x, dtype=np.float32)
    B, S, _ = x.shape
    n = _n_cores
    assert B % n == 0
    nseq = B // n
    nc = _get_program(nseq, S)
    f = lambda a: np.ascontiguousarray(np.asarray(a, dtype=np.float32))
    shared = {
        "norm_mix_g": f(norm_mix_g).reshape(D),
        "w_in": f(w_in).reshape(D, INW),
        "b_f": f(b_f).reshape(1, H),
        "b_gate": f(b_gate).reshape(2 * D),
        "conv_mix_w": f(conv_mix_w).reshape(3, 512),
        "w_out_conv": f(w_out_conv).reshape(512, D),
        "w_out_attn": f(w_out_attn).reshape(512, D),
        "w_o": f(w_o).reshape(D, D),
        "norm_ffn_g": f(norm_ffn_g).reshape(D),
        "w_up": f(w_up).reshape(D, 2 * FF),
        "conv_ffn_w": f(conv_ffn_w).reshape(3, 2 * FF),
        "w_down": f(w_down).reshape(FF, D),
        "norm_f_g": f(norm_f_g).reshape(1, D),
    }
    in_maps = []
    for c in range(n):
        m = dict(shared)
        m["x"] = np.ascontiguousarray(x[c * nseq:(c + 1) * nseq].reshape(nseq * S, D))
        in_maps.append(m)
    res = run_bass_kernel_spmd(nc, in_maps, core_ids=list(range(n)))
    outs = [np.asarray(r["out"]).reshape(nseq, S, D) for r in res.results]
    return np.concatenate(outs, axis=0).astype(np.float32)
```

```python
import numpy as np
from contextlib import ExitStack
import concourse.bass as bass
import concourse.mybir as mybir
from concourse.bass_utils import run_bass_kernel_spmd

F32 = mybir.dt.float32
BF16 = mybir.dt.bfloat16
AF = mybir.ActivationFunctionType
ALU = mybir.AluOpType

N_CORES = 8
D = 1024
NKC = 8
H = 8
FF = 2816
NFC = 22
INW = 5128
TB = 512
EPS = 1e-6
NSLOT = 4
NUNIT = 31
MASKV = -30000.0
_LASTP = []
_DBG = {"stop": 0}


class _Op:
    __slots__ = ("idx", "eng", "fn", "dma", "deps", "signal", "val", "grp")

    def __init__(self, idx, eng, fn, dma, grp):
        self.idx = idx
        self.eng = eng
        self.fn = fn
        self.dma = dma
        self.deps = []
        self.signal = False
        self.val = 0
        self.grp = grp


class _Rec:
    def __init__(self):
        self.call = None

    def __getattr__(self, name):
        def f(*a, **k):
            assert self.call is None
            self.call = (name, a, k)
            return self
        return f


class Prog:
    ENGS = ("pe", "act", "dve", "pool", "sp")

    def __init__(self):
        self.ops = []
        self.last_w = {}
        self.readers = {}
        self.barrier_keys = set()
        self.batch_keys = {}
        self.grp_last = {"M": {}, "F": {}}

    def pending(self, res):
        out = []
        w = self.last_w.get(res)
        if w is not None:
            out.append(w)
        for rd in self.readers.get(res, {}).values():
            out.extend(rd if isinstance(rd, list) else [rd])
        return out

    def add(self, eng, fn, reads=(), writes=(), dma=None, grp=None, after=()):
        if fn is not None:
            rec = _Rec()
            fn(rec)
            fn = rec.call
        op = _Op(len(self.ops), eng, fn, dma, grp)
        hard = {}
        soft = {}
        for o in after:
            hard[o.idx] = o
        for r in reads:
            w = self.last_w.get(r)
            if w is not None:
                hard[w.idx] = w
            if r.startswith("ps") and dma is None:
                for e2, rd in self.readers.get(r, {}).items():
                    if e2 != eng and e2 != "dma":
                        hard[rd.idx] = rd
        for r in writes:
            w = self.last_w.get(r)
            if w is not None:
                hard[w.idx] = w
            for rd in self.readers.get(r, {}).values():
                for o in (rd if isinstance(rd, list) else [rd]):
                    soft[o.idx] = o
        if grp is not None:
            other = "F" if grp == "M" else "M"
            for o in self.grp_last[other].values():
                soft[o.idx] = o
            if dma is None:
                self.grp_last[grp][eng] = op
        deps = []
        for d in hard.values():
            if d.dma is None and op.dma is None and d.eng == eng and eng == "pe":
                continue
            deps.append(d)
        for d in soft.values():
            if d.idx in hard:
                continue
            if d.dma is None and d.eng == eng and op.dma is None and eng != "pool":
                continue
            deps.append(d)
        op.deps = deps
        for d in deps:
            d.signal = True
        for r in reads:
            rd = self.readers.setdefault(r, {})
            if dma is not None:
                rd.setdefault("dma", []).append(op)
            else:
                rd[eng] = op
        for r in writes:
            self.last_w[r] = op
            self.readers[r] = {}
        self.ops.append(op)
        return op

    def emit(self, nc, es):
        sems = {}
        for e in self.ENGS:
            sems[e] = es.enter_context(nc.semaphore("s_" + e))
        cnt = {e: 0 for e in self.ENGS}
        dcnt = {}
        for op in self.ops:
            if op.dma is not None:
                if op.dma not in sems:
                    sems[op.dma] = es.enter_context(nc.semaphore("d_" + op.dma))
                    dcnt[op.dma] = 0
                dcnt[op.dma] += 16
                op.val = dcnt[op.dma]
            elif op.signal:
                cnt[op.eng] += 1
                op.val = cnt[op.eng]
        final = dict(dcnt)
        block = es.enter_context(nc.Block())
        per_eng = {e: [o for o in self.ops if o.eng == e] for e in self.ENGS}

        def run(engine, ename):
            waited = {}
            for op in per_eng[ename]:
                need = {}
                for d in op.deps:
                    if d.dma is not None:
                        k = d.dma
                        v = final[k] if k in self.barrier_keys else d.val
                        if k in self.batch_keys:
                            bsz = self.batch_keys[k]
                            v = (v + bsz - 1) // bsz * bsz
                    else:
                        k = d.eng
                        v = d.val
                    if v > need.get(k, 0):
                        need[k] = v
                for k, v in need.items():
                    if v > waited.get(k, 0):
                        engine.wait_ge(sems[k], v)
                        waited[k] = v
                if op.fn is None:
                    continue
                name, a, k = op.fn
                ins = getattr(engine, name)(*a, **k)
                if op.dma is not None:
                    ins.then_inc(sems[op.dma], 16)
                elif op.signal:
                    ins.then_inc(sems[ename], 1)

        @block.tensor
        def _(e):
            run(e, "pe")

        @block.scalar
        def _(e):
            run(e, "act")

        @block.vector
        def _(e):
            run(e, "dve")

        @block.gpsimd
        def _(e):
            run(e, "pool")

        @block.sync
        def _(e):
            run(e, "sp")


def build_program(NSEQ=4, SEQ=2048):
    NQB = SEQ // TB
    NBLK = NSEQ * NQB
    NKT = SEQ // 128
    TOK = NSEQ * SEQ
    nc = bass.Bass("TRN2", target_bir_lowering=False)

    def din(name, shape):
        return nc.dram_tensor(name, shape, F32, kind="ExternalInput").ap()

    x_d = din("x", [TOK, D])
    norm_mix_g = din("norm_mix_g", [D])
    w_in = din("w_in", [D, INW])
    b_f = din("b_f", [1, H])
    b_gate = din("b_gate", [2 * D])
    conv_mix_w = din("conv_mix_w", [3, 512])
    w_out_conv = din("w_out_conv", [512, D])
    w_out_attn = din("w_out_attn", [512, D])
    w_o = din("w_o", [D, D])
    norm_ffn_g = din("norm_ffn_g", [D])
    w_up = din("w_up", [D, 2 * FF])
    conv_ffn_w = din("conv_ffn_w", [3, 2 * FF])
    w_down = din("w_down", [FF, D])
    norm_f_g = din("norm_f_g", [1, D])
    out_d = nc.dram_tensor("out", [TOK, D], F32, kind="ExternalOutput").ap()
    ws = nc.dram_tensor("ws", [NUNIT, 128, 4096], BF16, kind="Internal").ap()

    es = ExitStack()
    with es:
        def sb(name, shape, dt):
            return es.enter_context(nc.sbuf_tensor(name, shape, dt))

        ident = sb("ident", [128, 128], BF16)
        trineg = sb("trineg", [128, 128], F32)
        negones = sb("negones", [128, 128], F32)
        maskw = sb("maskw", [128, 896], BF16)
        gmix = sb("gmix", [128, NKC], F32)
        gffn = sb("gffn", [128, NKC], F32)
        gfbc = sb("gfbc", [128, D], F32)
        cmw = sb("cmw", [128, 4, 3], F32)
        cfw = sb("cfw", [128, 44, 3], F32)
        bgate = sb("bgate", [128, 16], F32)
        bf4 = sb("bf4", [128, 4, H], F32)
        wf = sb("wf", [128, NKC, H], BF16)
        ss = sb("ss", [128, 4], F32)
        epsT = sb("epsT", [128, 1], F32)
        ss1 = sb("ss1", [128, 4], F32)
        lnv1 = sb("lnv1", [128, 4], F32)
        rstd1 = sb("rstd1", [128, 4], F32)
        sqj = sb("sqj", [128, D], BF16)
        identf = sb("identf", [128, 64], F32)
        lnv = sb("lnv", [128, 4], F32)
        rstd = sb("rstd", [128, 4], F32)
        PH = sb("PH", [128, 4, 2], F32)
        HS = sb("HS", [128, 44, 2], F32)
        carry = sb("carry", [128, 2], F32)
        x_tm = [sb("x_tm%d" % i, [128, 4, D], F32) for i in range(2)]
        xs = [sb("xs%d" % i, [128, D], BF16) for i in range(2)]
        hT = sb("hT", [128, NKC, TB], BF16)
        KA = sb("KA", [128, H, SEQ], BF16)
        VA = sb("VA", [128, NKT, 4, 192], BF16)
        wslot = [sb("wslot%d" % i, [128, 8, 512], BF16) for i in range(NSLOT)]
        ps_all = es.enter_context(nc.psum_tensor("ps_all", [128, 8, 512], F32))

        OVL_BYTES = 66048
        ovl = sb("ovl", [128, OVL_BYTES // 2], BF16)

        class Carver:
            def __init__(self):
                self.off = 0

            def get(self, free_shape, dt):
                n = int(np.prod(free_shape))
                nb = n * (4 if dt == F32 else 2)
                nb_al = (nb + 63) // 64 * 64
                assert self.off + nb_al <= OVL_BYTES, ("overlay overflow", self.off, nb_al)
                a = ovl[:, self.off // 2:(self.off + nb) // 2]
                self.off += nb_al
                if dt == F32:
                    a = a.bitcast(F32)
                if len(free_shape) == 2:
                    a = a.rearrange("p (a b) -> p a b", a=free_shape[0])
                elif len(free_shape) == 3:
                    a = a.rearrange("p (a b c) -> p a b c", a=free_shape[0], b=free_shape[1])
                return a

        cm = Carver()
        cu = cm.get([4, TB], F32)
        Pb = cm.get([4, TB + 2], F32)
        zT = cm.get([4, TB], BF16)
        QA = cm.get([H, TB], BF16)
        zf = cm.get([4, H], F32)
        ef = cm.get([4, H], F32)
        spf = cm.get([4, H], F32)
        Fq = cm.get([3, TB], BF16)
        Fk = cm.get([3, TB], BF16)
        PT = [cm.get([2, TB], BF16) for _ in range(3)]
        rl = [cm.get([TB], F32) for _ in range(2)]
        Fblk, r1 = rl[0], rl[1]
        oT = cm.get([4, TB], BF16)
        sgc = [cm.get([TB], BF16) for _ in range(2)]
        sga = [cm.get([TB], BF16) for _ in range(2)]
        t1 = [cm.get([TB], F32) for _ in range(2)]
        mT = cm.get([NKC, TB], BF16)
        mixer_bytes = cm.off
        cf = Carver()
        NR = 8
        A_sb = [cf.get([TB + 2], F32) for _ in range(NR)]
        acc = [cf.get([TB], F32) for _ in range(NR)]
        sa = [cf.get([TB], F32) for _ in range(2)]
        G = cf.get([NFC, TB], BF16)

        P = Prog()
        P.barrier_keys = {"constA", "constB"}
        P.batch_keys = {"fs": 96}

        def psn(b):
            return "ps%d" % b

        cres = {"A": [], "B": []}
        cdefer = []

        def cdma(out_ap, in_ap, g="B"):
            def go():
                cres[g].append("cdma%s%d" % (g, len(cres[g])))
                P.add("pool", lambda e, o=out_ap, i=in_ap: e.dma_start(out=o, in_=i, allow_slow_non_contiguous=True),
                      writes=[cres[g][-1]], dma="const" + g)
            if g == "A":
                go()
            else:
                cdefer.append(go)

        ovl32 = ovl[:, :].bitcast(F32)
        stgF = ovl32[:, 0:384].rearrange("p (k n) -> p k n", k=3)
        stgM = ovl32[:, 384:512]
        stgG = ovl32[:, 512:640]
        stgG2 = ovl32[:, 640:768]
        stgBg = ovl32[:, 768:896]
        for k in range(3):
            cdma(stgF[0:44, k, :], conv_ffn_w[k, :].rearrange("(c p) -> c p", p=128), "A")
        cdma(stgM[0:12, :], conv_mix_w.rearrange("k (c p) -> (k c) p", p=128), "A")
        cdma(stgG[0:8, :], norm_mix_g.rearrange("(c p) -> c p", p=128), "A")
        cdma(stgG2[0:8, :], norm_ffn_g.rearrange("(c p) -> c p", p=128), "A")
        cdma(stgBg[0:16, :], b_gate.rearrange("(c p) -> c p", p=128), "A")
        cdma(gfbc[:, :], norm_f_g[0:1, :].broadcast_to([128, D]))
        for i in range(4):
            cdma(bf4[:, i, :], b_f[0:1, :].broadcast_to([128, H]), "A")
        cdma(wf[:, :, :], w_in[:, 3072:3080].rearrange("(k p) n -> p k n", p=128), "A")

        def wsv(u):
            return ws[u].rearrange("p (k n) -> p k n", k=8)

        ws_parts = {}
        cvq = []

        def conv(u, k0, k1, n0, n1, src, grp):
            ws_parts.setdefault(u, []).append((k0, k1, n0, n1, src))

        def kin(w, r0, r1_, c0, c1):
            return w[r0:r1_, c0:c1].rearrange("(k p) n -> p k n", p=128)

        U_CIN, U_CC, U_CB, U_Q, U_K, U_V = 0, 1, 2, 3, 4, 5
        U_G = [6, 8, 9, 11]
        U_M = [7, 10]
        U_WO = [12, 13]
        U_UP = list(range(14, 25))
        U_DN = list(range(25, 31))
        conv(U_CIN, 0, 8, 0, 512, kin(w_in, 0, D, 1024, 1536), 0)
        conv(U_CC, 0, 8, 0, 512, kin(w_in, 0, D, 512, 1024), 0)
        conv(U_CB, 0, 8, 0, 512, kin(w_in, 0, D, 0, 512), 0)
        conv(U_Q, 0, 8, 0, 512, kin(w_in, 0, D, 1536, 2048), 0)
        conv(U_K, 0, 8, 0, 512, kin(w_in, 0, D, 2048, 2560), 0)
        conv(U_V, 0, 8, 0, 512, kin(w_in, 0, D, 2560, 3072), 0)
        for n in range(2):
            conv(U_M[n], 0, 4, 0, 512, kin(w_out_conv, 0, 512, n * 512, (n + 1) * 512), 1)
            conv(U_M[n], 4, 8, 0, 512, kin(w_out_attn, 0, 512, n * 512, (n + 1) * 512), 1)
        for q in range(4):
            conv(U_G[q], 0, 8, 0, 256, kin(w_in, 0, D, 3080 + q * 256, 3080 + (q + 1) * 256), 1)
            conv(U_G[q], 0, 8, 256, 512, kin(w_in, 0, D, 4104 + q * 256, 4104 + (q + 1) * 256), 1)
        for n in range(2):
            conv(U_WO[n], 0, 8, 0, 512, kin(w_o, 0, D, n * 512, (n + 1) * 512), 1)
        for u in range(11):
            conv(U_UP[u], 0, 8, 0, 256, kin(w_up, 0, D, 256 * u, 256 * (u + 1)), 2)
            conv(U_UP[u], 0, 8, 256, 512, kin(w_up, 0, D, FF + 256 * u, FF + 256 * (u + 1)), 2)
        DN_K = [(0, 8), (8, 16), (16, 22)]
        for n in range(2):
            for gi, (ka, kb) in enumerate(DN_K):
                conv(U_DN[n * 3 + gi], 0, kb - ka, 0, 512, kin(w_down, ka * 128, kb * 128, n * 512, (n + 1) * 512), 3)

        def pool_c(fn):
            P.add("pool", fn, reads=list(cres["A"]), writes=["cA"])

        pool_c(lambda e: e.memset(ident[:, :], 0.0))
        pool_c(lambda e: e.affine_select(out=ident[:, :], in_=ident[:, :], pattern=[[-1, 128]],
                                         compare_op=ALU.not_equal, fill=1.0, base=0, channel_multiplier=1))
        pool_c(lambda e: e.memset(trineg[:, :], -1.0))
        pool_c(lambda e: e.affine_select(out=trineg[:, :], in_=trineg[:, :], pattern=[[1, 128]],
                                         compare_op=ALU.is_ge, fill=0.0, base=0, channel_multiplier=-1))
        pool_c(lambda e: e.memset(negones[:, :], -1.0))
        pool_c(lambda e: e.memset(maskw[:, :], 0.0))
        pool_c(lambda e: e.affine_select(out=maskw[:, :], in_=maskw[:, :], pattern=[[1, 896]],
                                         compare_op=ALU.is_ge, fill=MASKV, base=-384, channel_multiplier=-1))
        P.add("pool", lambda e: e.memset(ss[:, :], 0.0), writes=["ss"])
        P.add("pool", lambda e: e.memset(ss1[:, :], 0.0), writes=["ss1"])
        pool_c(lambda e: e.memset(epsT[:, :], EPS))
        pool_c(lambda e: e.memset(identf[:, :], 0.0))
        pool_c(lambda e: e.affine_select(out=identf[:, :], in_=identf[:, :], pattern=[[-1, 64]],
                                         compare_op=ALU.not_equal, fill=1.0, base=0, channel_multiplier=1))
        for go in cdefer:
            go()
        cbk = 7
        def cmm(c0, c1, lhsT, K):
            P.add("pe", lambda e: e.matmul(ps_all[:, cbk, c0:c1], lhsT, identf[0:K, 0:K], start=True, stop=True),
                  reads=["cA"], writes=[psn(cbk)], grp="F")

        for k in range(3):
            cmm(k * 44, (k + 1) * 44, stgF[0:44, k, :], 44)
        cmm(132, 144, stgM[0:12, :], 12)
        cmm(144, 152, stgG[0:8, :], 8)
        cmm(152, 160, stgG2[0:8, :], 8)
        cmm(160, 176, stgBg[0:16, :], 16)
        P.add("dve", lambda e: e.tensor_copy(out=gmix[:, :], in_=ps_all[:, cbk, 144:152]), reads=[psn(cbk)], writes=["cG"])
        P.add("dve", lambda e: e.tensor_copy(out=gffn[:, :], in_=ps_all[:, cbk, 152:160]), reads=[psn(cbk)], writes=["cB0"])
        P.add("dve", lambda e: e.tensor_copy(out=bgate[:, :], in_=ps_all[:, cbk, 160:176]), reads=[psn(cbk)], writes=["cB1"])
        P.add("dve", lambda e: e.tensor_copy(out=cmw[:, :, :],
                                             in_=ps_all[:, cbk, 132:144].rearrange("p (k c) -> p c k", k=3)),
              reads=[psn(cbk)], writes=["cB2"])
        P.add("dve", lambda e: e.tensor_copy(out=cfw[:, :, :],
                                             in_=ps_all[:, cbk, 0:132].rearrange("p (k c) -> p c k", k=3)),
              reads=[psn(cbk), "cB0", "cB1", "cB2"] + list(cres["B"]), writes=["cB"])

        block_units = [U_CIN, U_CC, U_CB, U_Q, U_K, U_V,
                       U_G[0], U_M[0], U_G[1], U_G[2], U_M[1], U_G[3],
                       U_WO[0], U_WO[1]] + U_UP + [U_DN[0], U_DN[1], U_DN[3], U_DN[2], U_DN[4], U_DN[5]]
        stream = []
        for b in range(NBLK):
            stream += block_units
        st = {"issued": 0, "next_use": 0}

        def issue_load(k):
            if k >= len(stream):
                return
            assert k == st["issued"]
            u = stream[k]
            s = k % NSLOT
            nk = 6 if u in (U_DN[2], U_DN[5]) else 8
            if k < len(block_units):
                R = "wslot%d" % s
                if u in U_UP or u in U_G:
                    prev = P.pending(R)
                    for pi, (k0, k1, n0, n1, src) in enumerate(ws_parts[u]):
                        P.add("pool", lambda e, o=wslot[s][:, k0:k1, n0:n1], i=src: e.dma_start(out=o, in_=i),
                              writes=[R + "ab"[pi]], dma="wp%d%s" % (s, "ab"[pi]), after=prev)
                else:
                    for (k0, k1, n0, n1, src) in ws_parts[u]:
                        P.add("pool", lambda e, o=wslot[s][:, k0:k1, n0:n1], i=src: e.dma_start(out=o, in_=i),
                              writes=[R, R + "a", R + "b"], dma="wp%d" % s)
                P.add("sp", lambda e, o=wsv(u)[:, 0:nk, :], i=wslot[s][:, 0:nk, :]: e.dma_start(out=o, in_=i),
                      reads=[R, R + "a", R + "b"], writes=["ws%d" % u], dma="wst%d" % s)
            else:
                P.add("sp", lambda e, o=wslot[s][:, 0:nk, :], i=wsv(u)[:, 0:nk, :]: e.dma_start(out=o, in_=i),
                      reads=["ws%d" % u], writes=["wslot%d" % s, "wslot%da" % s, "wslot%db" % s], dma="w%d" % s)
            st["issued"] += 1

        P.add("sp", lambda e: e.dma_start(out=x_tm[0][:, :, :],
                                          in_=x_d[0:TB, :].rearrange("(i p) d -> p i d", p=128)),
              writes=["x0.%d" % i for i in range(4)], dma="xl0")
        for k0 in range(NSLOT):
            issue_load(k0)
        P.add("pool", lambda e: e.memset(VA[:, :, :, 64:128], 1.0), writes=["VA"])
        P.add("pool", lambda e: e.memset(KA[64:70, :, :], 1.0), writes=["KAaug.0", "KAaug.1", "KAaug.2"])

        def take_unit(u_expected):
            k = st["next_use"]
            assert stream[k] == u_expected, (k, stream[k], u_expected)
            assert k < st["issued"]
            st["next_use"] += 1
            s = k % NSLOT
            return wslot[s], "wslot%d" % s

        rel_state = {"k": 0}

        def rel():
            k = rel_state["k"]
            rel_state["k"] += 1
            issue_load(k + NSLOT)

        bank_rr = {"i": 0}

        def next_bank():
            b = bank_rr["i"] % 8
            bank_rr["i"] += 1
            return b

        STA = {"ss": ss, "lnv": lnv, "rstd": rstd, "n": ("ss", "lnv", "rstd")}
        STB = {"ss": ss1, "lnv": lnv1, "rstd": rstd1, "n": ("ss1", "lnv1", "rstd1")}

        def norm_stats(xb, S):
            X = x_tm[xb]
            rs, rl_, rr = S["n"]
            for i in range(4):
                P.add("act", lambda e, i=i: e.activation(out=sqj[:, :], in_=X[:, i, :], func=AF.Square,
                                                         accum_out=S["ss"][:, i:i + 1]),
                      reads=["x%d.%d" % (xb, i)], writes=["sqj", rs])
            P.add("act", lambda e: e.activation(out=S["lnv"][:, :], in_=S["ss"][:, :], func=AF.Ln, scale=1.0 / D,
                                                bias=epsT[:, 0:1]),
                  reads=[rs, "cA"], writes=[rl_])
            P.add("act", lambda e: e.activation(out=S["rstd"][:, :], in_=S["lnv"][:, :], func=AF.Exp, scale=-0.5),
                  reads=[rl_], writes=[rr])
            P.add("dve", lambda e: e.memset(S["ss"][:, :], 0.0), reads=[rl_], writes=[rs])

        def norm_scale(xb, i, S):
            X = x_tm[xb]
            if i % 2 == 0:
                P.add("dve", lambda e: e.tensor_scalar(out=xs[i % 2][:, :], in0=X[:, i, :],
                                                       scalar1=S["rstd"][:, i:i + 1], scalar2=None, op0=ALU.mult),
                      reads=["x%d.%d" % (xb, i), S["n"][2]], writes=["xs%d" % (i % 2)])
            else:
                P.add("act", lambda e: e.activation(out=xs[i % 2][:, :], in_=X[:, i, :], func=AF.Copy,
                                                    scale=S["rstd"][:, i:i + 1]),
                      reads=["x%d.%d" % (xb, i), S["n"][2]], writes=["xs%d" % (i % 2)])

        def norm_transp(i, tb):
            for c in range(NKC):
                bk = tb[c // 2]
                o = ps_all[:, bk, :].bitcast(BF16)[:, (c % 2) * 512 + i * 128:(c % 2) * 512 + (i + 1) * 128]
                P.add("pe", lambda e, o=o, c=c: e.transpose(o, xs[i % 2][:, c * 128:(c + 1) * 128], ident[:, :]),
                      reads=["xs%d" % (i % 2), "cA"], writes=[psn(bk)])

        def norm_evac(tb, gvec, gres):
            for c in range(NKC):
                bk = tb[c // 2]
                src = ps_all[:, bk, :].bitcast(BF16)[:, (c % 2) * 512:(c % 2 + 1) * 512]
                if (c // 2) % 2 == 0:
                    P.add("act", lambda e, c=c, src=src: e.activation(out=hT[:, c, :], in_=src, func=AF.Copy,
                                                                      scale=gvec[:, c:c + 1]),
                          reads=[psn(bk), gres], writes=["hT.%d" % c])
                else:
                    P.add("dve", lambda e, c=c, src=src: e.tensor_scalar(out=hT[:, c, :], in0=src,
                                                                         scalar1=gvec[:, c:c + 1], scalar2=None,
                                                                         op0=ALU.mult),
                          reads=[psn(bk), gres], writes=["hT.%d" % c])

        def rmsnorm_to_hT(xb, gvec, tag, grp):
            gres = "cG" if tag == "n1" else "cB"
            S = STB if tag == "n1" else STA
            norm_stats(xb, S)
            tb = [next_bank() for _ in range(4)]
            for i in range(4):
                norm_scale(xb, i, S)
                norm_transp(i, tb)
            norm_evac(tb, gvec, gres)

        hT_all = ["hT.%d" % c for c in range(NKC)]

        def proj_fm(wt, wname, col0, bk, grp):
            for kc in range(NKC):
                P.add("pe", lambda e, kc=kc: e.matmul(ps_all[:, bk, :], wt[:, kc, col0:col0 + 128], hT[:, kc, :],
                                                      start=(kc == 0), stop=(kc == NKC - 1)),
                      reads=[wname, "hT.%d" % kc], writes=[psn(bk)], grp=grp)

        pending_store = []
        for blk in range(0 if _DBG["stop"] == 1 else NBLK):
            seq = blk // NQB
            qb = blk % NQB
            xb = blk % 2
            t0 = qb * TB
            row0 = blk * TB
            nkt = 4 * (qb + 1)
            X = x_tm[xb]
            xr = ["x%d.%d" % (xb, i) for i in range(4)]

            if blk == 0:
                rmsnorm_to_hT(xb, gmix, "n1", "M")

            if _DBG["stop"] in (2, 20, 21, 22, 23, 24):
                break
            fb = next_bank()
            for i in range(4):
                for kc in range(NKC):
                    P.add("pe", lambda e, i=i, kc=kc: e.matmul(ps_all[:, fb, i * H:(i + 1) * H],
                                                               hT[:, kc, i * 128:(i + 1) * 128], wf[:, kc, :],
                                                               start=(kc == 0), stop=(kc == NKC - 1)),
                          reads=["hT.%d" % kc, "cA"], writes=[psn(fb)], grp="M")
            P.add("dve", lambda e: e.tensor_tensor(out=zf[:, :, :],
                                                   in0=ps_all[:, fb, 0:4 * H].rearrange("p (i h) -> p i h", i=4),
                                                   in1=bf4[:, :, :], op=ALU.add),
                  reads=[psn(fb), "cA"], writes=["zf"], grp="M")
            P.add("act", lambda e: e.activation(out=ef[:, :, :], in_=zf[:, :, :], func=AF.Exp, scale=-1.0),
                  reads=["zf"], writes=["ef"], grp="M")
            P.add("act", lambda e: e.activation(out=spf[:, :, :], in_=ef[:, :, :], func=AF.Ln, bias=1.0),
                  reads=["ef"], writes=["spf"], grp="M")
            wt, wn = take_unit(U_CIN)
            for c in range(4):
                bk = next_bank()
                proj_fm(wt, wn, c * 128, bk, "M")
                P.add("act", lambda e, c=c, bk=bk: e.activation(out=cu[:, c, :], in_=ps_all[:, bk, :], func=AF.Copy),
                      reads=[psn(bk)], writes=["cu.%d" % c], grp="M")
            rel()
            if qb == 0:
                P.add("pool", lambda e: e.memset(Pb[:, :, 0:2], 0.0), writes=["P.halo"], grp="M")
            else:
                P.add("pool", lambda e: e.tensor_copy(out=Pb[:, :, 0:2], in_=PH[:, :, :]), reads=["PH"],
                      writes=["P.halo"], grp="M")
            wt, wn = take_unit(U_CC)
            for c in range(4):
                bk = next_bank()
                proj_fm(wt, wn, c * 128, bk, "M")
                P.add("dve", lambda e, c=c, bk=bk: e.tensor_tensor(out=Pb[:, c, 2:TB + 2], in0=ps_all[:, bk, :],
                                                                   in1=cu[:, c, :], op=ALU.mult),
                      reads=[psn(bk), "cu.%d" % c], writes=["P.%d" % c], grp="M")
                P.add("act", lambda e, c=c: e.activation(out=cu[:, c, :], in_=Pb[:, c, 2:TB + 2], func=AF.Copy,
                                                         scale=cmw[:, c, 2:3]),
                      reads=["P.%d" % c, "cB"], writes=["cu.%d" % c], grp="M")
                P.add("dve", lambda e, c=c: e.scalar_tensor_tensor(out=cu[:, c, :], in0=Pb[:, c, 1:TB + 1],
                                                                    scalar=cmw[:, c, 1:2], in1=cu[:, c, :],
                                                                    op0=ALU.mult, op1=ALU.add),
                      reads=["P.%d" % c, "P.halo", "cB", "cu.%d" % c], writes=["cu.%d" % c], grp="M")
                P.add("dve", lambda e, c=c: e.scalar_tensor_tensor(out=cu[:, c, :], in0=Pb[:, c, 0:TB],
                                                                    scalar=cmw[:, c, 0:1], in1=cu[:, c, :],
                                                                    op0=ALU.mult, op1=ALU.add),
                      reads=["P.%d" % c, "P.halo", "cB", "cu.%d" % c], writes=["cu.%d" % c], grp="M")
            rel()
            P.add("pool", lambda e: e.tensor_copy(out=PH[:, :, :], in_=Pb[:, :, TB:TB + 2]),
                  reads=["P.%d" % c for c in range(4)], writes=["PH"], grp="M")
            wt, wn = take_unit(U_CB)
            for c in range(4):
                bk = next_bank()
                proj_fm(wt, wn, c * 128, bk, "M")
                P.add("dve", lambda e, c=c, bk=bk: e.tensor_tensor(out=zT[:, c, :], in0=ps_all[:, bk, :],
                                                                   in1=cu[:, c, :], op=ALU.mult),
                      reads=[psn(bk), "cu.%d" % c], writes=["zT.%d" % c], grp="M")

            rel()
            if _DBG["stop"] == 3:
                break
            fb2 = next_bank()
            for i in range(4):
                P.add("pe", lambda e, i=i: e.matmul(ps_all[0:H, fb2, i * 128:(i + 1) * 128], spf[:, i, :],
                                                    trineg[:, :], start=True, stop=(i == 0)),
                      reads=["spf", "cA"], writes=[psn(fb2)], grp="M")
                for i2 in range(i):
                    P.add("pe", lambda e, i=i, i2=i2: e.matmul(ps_all[0:H, fb2, i * 128:(i + 1) * 128],
                                                               spf[:, i2, :], negones[:, :], start=False,
                                                               stop=(i2 == i - 1)),
                          reads=["spf", "cA"], writes=[psn(fb2)], grp="M")
            cb_ = blk % 2
            if qb == 0:
                P.add("dve", lambda e, cb_=cb_: e.memset(carry[0:H, cb_:cb_ + 1], 0.0), writes=["carry%d" % cb_])
            P.add("dve", lambda e, cb_=cb_: e.tensor_scalar(out=Fblk[0:H, :], in0=ps_all[0:H, fb2, :],
                                                            scalar1=carry[0:H, cb_:cb_ + 1], scalar2=None,
                                                            op0=ALU.add),
                  reads=[psn(fb2), "carry%d" % cb_], writes=["rl0"], grp="M")
            f_ops = []
            f_ops.append(lambda: P.add("dve", lambda e, cb_=cb_: e.tensor_copy(out=carry[0:H, 1 - cb_:2 - cb_], in_=Fblk[0:H, TB - 1:TB]),
                      reads=["rl0"], writes=["carry%d" % (1 - cb_)], grp="M"))
            f_ops.append(lambda: P.add("dve", lambda e: e.tensor_copy(out=Fq[0:H, 0, :], in_=Fblk[0:H, :]),
                      reads=["rl0"], writes=["Fq0"], grp="M"))
            f_ops.append(lambda: P.add("dve", lambda e: e.tensor_tensor(out=r1[0:H, :], in0=Fblk[0:H, :], in1=Fq[0:H, 0, :],
                                                        op=ALU.subtract),
                      reads=["rl0", "Fq0"], writes=["rl1"], grp="M"))
            f_ops.append(lambda: P.add("dve", lambda e: e.tensor_copy(out=Fq[0:H, 1, :], in_=r1[0:H, :]),
                      reads=["rl1"], writes=["Fq1"], grp="M"))
            f_ops.append(lambda: P.add("dve", lambda e: e.tensor_tensor(out=Fblk[0:H, :], in0=r1[0:H, :], in1=Fq[0:H, 1, :],
                                                        op=ALU.subtract),
                      reads=["rl1", "Fq1"], writes=["rl0"], grp="M"))
            f_ops.append(lambda: P.add("dve", lambda e: e.tensor_copy(out=Fq[0:H, 2, :], in_=Fblk[0:H, :]),
                      reads=["rl0"], writes=["Fq2"], grp="M"))
            f_ops.append(lambda: P.add("dve", lambda e: e.tensor_scalar(out=Fk[0:H, :, :], in0=Fq[0:H, :, :], scalar1=-1.0,
                                                        scalar2=None, op0=ALU.mult),
                      reads=["Fq0", "Fq1", "Fq2"], writes=["Fk"], grp="M"))

            if _DBG["stop"] == 4:
                break
            wt, wn = take_unit(U_Q)
            for c in range(4):
                bk = next_bank()
                proj_fm(wt, wn, c * 128, bk, "M")
                P.add("act", lambda e, c=c, bk=bk: e.activation(out=QA[0:64, 2 * c, :], in_=ps_all[0:64, bk, :],
                                                                func=AF.Copy, scale=0.125),
                      reads=[psn(bk)], writes=["QAq.%d" % (2 * c)], grp="M")
                P.add("dve", lambda e, c=c, bk=bk: e.tensor_scalar(out=QA[0:64, 2 * c + 1, :],
                                                                   in0=ps_all[64:128, bk, :], scalar1=0.125,
                                                                   scalar2=None, op0=ALU.mult),
                      reads=[psn(bk)], writes=["QAq.%d" % (2 * c + 1)], grp="M")
                if f_ops:
                    f_ops.pop(0)()
            rel()
            wt, wn = take_unit(U_K)
            for c in range(4):
                bk = next_bank()
                proj_fm(wt, wn, c * 128, bk, "M")
                P.add("act", lambda e, c=c, bk=bk: e.activation(out=KA[0:64, 2 * c, t0:t0 + TB],
                                                                in_=ps_all[0:64, bk, :], func=AF.Copy),
                      reads=[psn(bk)], writes=["KAk.%d" % (2 * c)], grp="M")
                P.add("dve", lambda e, c=c, bk=bk: e.tensor_copy(out=KA[0:64, 2 * c + 1, t0:t0 + TB],
                                                                 in_=ps_all[64:128, bk, :]),
                      reads=[psn(bk)], writes=["KAk.%d" % (2 * c + 1)], grp="M")
                if f_ops:
                    f_ops.pop(0)()
            rel()
            while f_ops:
                f_ops.pop(0)()
            P.add("pool", lambda e: e.memset(QA[64:70, :, :], 1.0), writes=["QAaug.0", "QAaug.1", "QAaug.2"], grp="M")
            for r in range(3):
                P.add("pool", lambda e, r=r: e.dma_start(out=QA[64 + r:65 + r, :, :], in_=Fq[0:H, r, :]),
                      reads=["Fq%d" % r], writes=["QAaug.%d" % r], dma="fs", grp="M")
                P.add("pool", lambda e, r=r: e.dma_start(out=KA[67 + r:68 + r, :, t0:t0 + TB], in_=Fk[0:H, r, :]),
                      reads=["Fk"], writes=["KAaug.%d" % r], dma="fs", grp="M")

            wt, wn = take_unit(U_V)
            for i in range(4):
                bk = next_bank()
                kt = 4 * qb + i
                for kc in range(NKC):
                    P.add("pe", lambda e, i=i, kc=kc, bk=bk: e.matmul(ps_all[:, bk, :],
                                                                      hT[:, kc, i * 128:(i + 1) * 128],
                                                                      wt[:, kc, :], start=(kc == 0),
                                                                      stop=(kc == NKC - 1)),
                          reads=[wn, "hT.%d" % kc], writes=[psn(bk)], grp="M")
                pv_ = ps_all[:, bk, :].rearrange("p (j e d) -> p j e d", j=4, e=2)
                P.add("act", lambda e, kt=kt, pv_=pv_: e.activation(out=VA[:, kt, :, 0:64], in_=pv_[:, :, 0, :],
                                                                    func=AF.Copy),
                      reads=[psn(bk)], writes=["VA"], grp="M")
                P.add("dve", lambda e, kt=kt, pv_=pv_: e.tensor_copy(out=VA[:, kt, :, 128:192], in_=pv_[:, :, 1, :]),
                      reads=[psn(bk)], writes=["VA"], grp="M")

            rel()
            if pending_store:
                pb_row0, pb_xb = pending_store.pop()
                P.add("sp", lambda e, r=pb_row0, b_=pb_xb: e.dma_start(
                    out=out_d[r:r + TB, :].rearrange("(i p) d -> p i d", p=128), in_=x_tm[b_][:, :, :]),
                    reads=["x%d.%d" % (pb_xb, i) for i in range(4)], writes=["outstore%d" % pb_xb], dma="st%d" % pb_xb)
            if blk + 1 < NBLK:
                nb_ = (blk + 1) % 2
                P.add("sp", lambda e, r=row0 + TB, b_=nb_: e.dma_start(
                    out=x_tm[b_][:, :, :], in_=x_d[r:r + TB, :].rearrange("(i p) d -> p i d", p=128)),
                    reads=["outstore%d" % nb_], writes=["x%d.%d" % (nb_, i) for i in range(4)], dma="xl%d" % nb_)
            if _DBG["stop"] == 5:
                break
            ng = nkt // 2
            items = [(h, g) for h in range(H) for g in range(ng)]
            LAG = 2

            def grp_c0(g):
                return 256 if 2 * g == 4 * qb + 2 else 0

            def emit_qk(idx, h, g):
                sg_ = idx % 3
                qk_reads = (["KAk.%d" % h, "QAq.%d" % h] + ["KAaug.%d" % r_ for r_ in range(3)]
                            + ["QAaug.%d" % r_ for r_ in range(3)])
                c0 = grp_c0(g)
                for sl in range(2):
                    kt = 2 * g + sl
                    bk = 2 * sg_ + sl
                    diag = kt >= 4 * qb
                    kap = KA[0:70, h, kt * 128:(kt + 1) * 128]
                    P.add("pe", lambda e, bk=bk, kap=kap, diag=diag: e.matmul(
                        ps_all[:, bk, c0:TB], kap, QA[0:70, h, c0:TB], start=True, stop=(not diag)),
                        reads=qk_reads, writes=[psn(bk)], grp="M")
                    if diag:
                        j = kt - 4 * qb
                        c1 = 128 * (j + 1)
                        m0 = 384 - 128 * j + c0
                        P.add("pe", lambda e, bk=bk, c1=c1, m0=m0: e.matmul(
                            ps_all[:, bk, c0:c1], ident[:, :], maskw[:, m0:m0 + (c1 - c0)], start=False, stop=True),
                            reads=["cA"], writes=[psn(bk)], grp="M")
                P.add("act", lambda e, sg_=sg_: e.activation(out=PT[sg_][:, :, c0:TB],
                                                             in_=ps_all[:, 2 * sg_:2 * sg_ + 2, c0:TB], func=AF.Exp),
                      reads=[psn(2 * sg_), psn(2 * sg_ + 1)], writes=["PT%d" % sg_], grp="M")

            def emit_pv(idx, h, g):
                sg_ = idx % 3
                pvb = 6 + h % 2
                j = h // 2
                eo = h % 2
                for sl in range(2):
                    kt = 2 * g + sl
                    c0 = grp_c0(g)
                    P.add("pe", lambda e, kt=kt, sl=sl, j=j, eo=eo, pvb=pvb, sg_=sg_, c0=c0: e.matmul(
                        ps_all[:, pvb, c0:TB], VA[:, kt, j, eo * 64:eo * 64 + 128], PT[sg_][:, sl, c0:TB],
                        start=(kt == 0), stop=(kt == nkt - 1)),
                        reads=["VA", "PT%d" % sg_], writes=[psn(pvb)], grp="M")
                if g == ng - 1:
                    oP = slice(0, 64) if eo == 0 else slice(64, 128)
                    lP = slice(64, 128) if eo == 0 else slice(0, 64)
                    k = h % 2
                    if qb == 0:
                        P.add("act", lambda e, lP=lP, k=k, pvb=pvb: e.activation(out=t1[k][lP, :], in_=ps_all[lP, pvb, :],
                                                                                 func=AF.Ln),
                              reads=[psn(pvb)], writes=["t1%d" % k], grp="M")
                        P.add("act", lambda e, lP=lP, k=k: e.activation(out=t1[k][lP, :], in_=t1[k][lP, :], func=AF.Exp,
                                                                        scale=-1.0),
                              reads=["t1%d" % k], writes=["t1%d" % k], grp="M")
                        P.add("dve", lambda e, oP=oP, lP=lP, k=k: e.tensor_copy(out=rl[k][oP, :], in_=t1[k][lP, :]),
                              reads=["t1%d" % k], writes=["rl%d" % k], grp="M")
                    else:
                        P.add("dve", lambda e, oP=oP, lP=lP, k=k, pvb=pvb: e.reciprocal(out=rl[k][oP, :],
                                                                                        in_=ps_all[lP, pvb, :]),
                              reads=[psn(pvb)], writes=["rl%d" % k], grp="M")
                    P.add("dve", lambda e, oP=oP, k=k, pvb=pvb, j=j: e.tensor_tensor(
                        out=oT[oP, j, :], in0=ps_all[oP, pvb, :], in1=rl[k][oP, :], op=ALU.mult),
                        reads=[psn(pvb), "rl%d" % k], writes=["oT.%d" % h], grp="M")

            for idx, (h, g) in enumerate(items):
                emit_qk(idx, h, g)
                if idx >= LAG:
                    emit_pv(idx - LAG, *items[idx - LAG])
            for idx in range(max(0, len(items) - LAG), len(items)):
                emit_pv(idx, *items[idx])
            bank_rr["i"] = 0

            if _DBG["stop"] == 6:
                break
            for n in range(2):
                for qq in range(2):
                    q = 2 * n + qq
                    wg, wgn = take_unit(U_G[q])
                    if qq == 0:
                        wm, wmn = take_unit(U_M[n])
                    for cc in range(2):
                        c = 2 * q + cc
                        r = c % 2
                        b_gc, b_ga = next_bank(), next_bank()
                        proj_fm(wg, wgn + "a", cc * 128, b_gc, "M")
                        P.add("act", lambda e, c=c, r=r, b_gc=b_gc: e.activation(
                            out=sgc[r][:, :], in_=ps_all[:, b_gc, :], func=AF.Sigmoid, bias=bgate[:, c:c + 1]),
                            reads=[psn(b_gc), "cB"], writes=["sgc%d" % r], grp="M")
                        proj_fm(wg, wgn + "b", 256 + cc * 128, b_ga, "M")
                        P.add("act", lambda e, c=c, r=r, b_ga=b_ga: e.activation(
                            out=sga[r][:, :], in_=ps_all[:, b_ga, :], func=AF.Sigmoid, bias=bgate[:, 8 + c:9 + c]),
                            reads=[psn(b_ga), "cB"], writes=["sga%d" % r], grp="M")
                    for cc in range(2):
                        c = 2 * q + cc
                        r = c % 2
                        cl = c - 4 * n
                        b_yc, b_ya = next_bank(), next_bank()
                        for kc in range(4):
                            P.add("pe", lambda e, kc=kc, cl=cl, b_yc=b_yc: e.matmul(
                                ps_all[:, b_yc, :], wm[:, kc, cl * 128:(cl + 1) * 128], zT[:, kc, :],
                                start=(kc == 0), stop=(kc == 3)),
                                reads=[wmn, "zT.%d" % kc], writes=[psn(b_yc)], grp="M")
                        for kc in range(4):
                            P.add("pe", lambda e, kc=kc, cl=cl, b_ya=b_ya: e.matmul(
                                ps_all[:, b_ya, :], wm[:, 4 + kc, cl * 128:(cl + 1) * 128], oT[:, kc, :],
                                start=(kc == 0), stop=(kc == 3)),
                                reads=[wmn, "oT.%d" % (2 * kc), "oT.%d" % (2 * kc + 1)], writes=[psn(b_ya)], grp="M")
                        P.add("dve", lambda e, r=r, b_yc=b_yc: e.tensor_tensor(out=t1[r][:, :], in0=ps_all[:, b_yc, :],
                                                                               in1=sgc[r][:, :], op=ALU.mult),
                              reads=[psn(b_yc), "sgc%d" % r], writes=["t1%d" % r], grp="M")
                        P.add("dve", lambda e, r=r, b_ya=b_ya: e.tensor_tensor(out=sga[r][:, :], in0=ps_all[:, b_ya, :],
                                                                               in1=sga[r][:, :], op=ALU.mult),
                              reads=[psn(b_ya), "sga%d" % r], writes=["sga%d" % r], grp="M")
                        P.add("pool", lambda e, r=r, c=c: e.tensor_tensor(out=mT[:, c, :], in0=t1[r][:, :],
                                                                          in1=sga[r][:, :], op=ALU.add),
                              reads=["t1%d" % r, "sga%d" % r], writes=["mT.%d" % c], grp="M")
                    rel()
                    if qq == 1:
                        rel()
            P.add("act", lambda e: e.activation(out=lnv[:, 0:1], in_=epsT[:, 0:1], func=AF.Ln), reads=["cA"],
                  writes=["lnv"])
            wo = [take_unit(U_WO[0]), take_unit(U_WO[1])]
            for i in range(4):
                for n in range(2):
                    wt, wn = wo[n]
                    bk = next_bank()
                    for kc in range(NKC):
                        P.add("pe", lambda e, i=i, kc=kc, bk=bk, wt=wt: e.matmul(
                            ps_all[:, bk, :], mT[:, kc, i * 128:(i + 1) * 128], wt[:, kc, :],
                            start=(kc == 0), stop=(kc == NKC - 1)),
                            reads=[wn, "mT.%d" % kc], writes=[psn(bk)], grp="M")
                    P.add("dve", lambda e, i=i, n=n, bk=bk: e.tensor_tensor(
                        out=X[:, i, n * 512:(n + 1) * 512], in0=ps_all[:, bk, :], in1=X[:, i, n * 512:(n + 1) * 512],
                        op=ALU.add),
                        reads=[psn(bk), xr[i]], writes=[xr[i]], grp="M")
            rel()
            rel()

            if _DBG["stop"] == 7:
                break
            rmsnorm_to_hT(xb, gffn, "n2", "F")

            rr = {"i": 0}

            def ffn_chunk(wt, wn, colbase, jj, want_silu, j):
                bk = next_bank()
                proj_fm(wt, wn, colbase, bk, "F")
                r = rr["i"] % NR
                rr["i"] += 1
                A = A_sb[r]
                ac = acc[r]
                P.add("act", lambda e, A=A, bk=bk: e.activation(out=A[:, 2:TB + 2], in_=ps_all[:, bk, :], func=AF.Copy),
                      reads=[psn(bk)], writes=["A%d" % r], grp="F")
                P.add("act", lambda e, ac=ac, bk=bk, jj=jj: e.activation(out=ac[:, :], in_=ps_all[:, bk, :],
                                                                         func=AF.Copy, scale=cfw[:, jj, 2:3]),
                      reads=[psn(bk), "cB"], writes=["acc%d" % r], grp="F")
                if qb == 0:
                    P.add("pool", lambda e, A=A: e.memset(A[:, 0:2], 0.0), writes=["Ah%d" % r], grp="F")
                else:
                    P.add("pool", lambda e, A=A, jj=jj: e.tensor_copy(out=A[:, 0:2], in_=HS[:, jj, :]),
                          reads=["HS.%d" % jj], writes=["Ah%d" % r], grp="F")
                P.add("pool", lambda e, A=A, jj=jj: e.tensor_copy(out=HS[:, jj, :], in_=A[:, TB:TB + 2]),
                      reads=["A%d" % r], writes=["HS.%d" % jj], grp="F")
                eng = "dve"
                P.add(eng, lambda e, A=A, ac=ac, jj=jj: e.scalar_tensor_tensor(
                    out=ac[:, :], in0=A[:, 1:TB + 1], scalar=cfw[:, jj, 1:2], in1=ac[:, :], op0=ALU.mult, op1=ALU.add),
                    reads=["A%d" % r, "Ah%d" % r, "acc%d" % r, "cB"], writes=["acc%d" % r], grp="F")
                P.add(eng, lambda e, A=A, ac=ac, jj=jj: e.scalar_tensor_tensor(
                    out=ac[:, :], in0=A[:, 0:TB], scalar=cfw[:, jj, 0:1], in1=ac[:, :], op0=ALU.mult, op1=ALU.add),
                    reads=["A%d" % r, "Ah%d" % r, "acc%d" % r, "cB"], writes=["acc%d" % r], grp="F")
                return r

            pend_fin = []

            def ffn_finish(j, ra, rb):
                sr = j % 2
                P.add("act", lambda e, ra=ra, sr=sr: e.activation(out=sa[sr][:, :], in_=acc[ra][:, :], func=AF.Silu),
                      reads=["acc%d" % ra], writes=["sa%d" % sr], grp="F")
                P.add("pool", lambda e, rb=rb, sr=sr, j=j: e.tensor_tensor(out=G[:, j, :], in0=sa[sr][:, :],
                                                                           in1=acc[rb][:, :], op=ALU.mult),
                      reads=["sa%d" % sr, "acc%d" % rb], writes=["G.%d" % j], grp="F")

            for u in range(11):
                wt, wn = take_unit(U_UP[u])
                for cl in range(2):
                    j = 2 * u + cl
                    ra = ffn_chunk(wt, wn + "a", cl * 128, j, True, j)
                    rb = ffn_chunk(wt, wn + "b", 256 + cl * 128, NFC + j, False, j)
                    pend_fin.append((j, ra, rb))
                    if len(pend_fin) > 1:
                        ffn_finish(*pend_fin.pop(0))
                rel()
            while pend_fin:
                ffn_finish(*pend_fin.pop(0))

            if _DBG["stop"] == 8:
                break
            nxt = blk + 1 < NBLK
            nxb = (blk + 1) % 2
            if nxt:
                norm_stats(nxb, STB)
                norm_scale(nxb, 0, STB)
                norm_scale(nxb, 1, STB)
            bankset = [[next_bank() for _ in range(4)], [next_bank() for _ in range(4)]]

            def dn(n, gi):
                ka, kb = DN_K[gi]
                wt, wn = take_unit(U_DN[n * 3 + gi])
                for i in range(4):
                    for j in range(ka, kb):
                        P.add("pe", lambda e, i=i, j=j, bk=bankset[n][i]: e.matmul(
                            ps_all[:, bk, :], G[:, j, i * 128:(i + 1) * 128], wt[:, j - ka, :],
                            start=(j == 0), stop=(j == NFC - 1)),
                            reads=[wn, "G.%d" % j], writes=[psn(bankset[n][i])], grp="F")
                rel()

            def dn_add(n):
                for i in range(4):
                    P.add("dve", lambda e, i=i, bk=bankset[n][i]: e.tensor_tensor(
                        out=X[:, i, n * 512:(n + 1) * 512], in0=ps_all[:, bk, :], in1=X[:, i, n * 512:(n + 1) * 512],
                        op=ALU.add),
                        reads=[psn(bankset[n][i]), xr[i]], writes=[xr[i]], grp="F")

            dn(0, 0)
            dn(0, 1)
            dn(1, 0)
            dn(0, 2)
            dn_add(0)
            dn(1, 1)
            ntb = bankset[0]
            if nxt:
                norm_transp(0, ntb)
                norm_transp(1, ntb)
                norm_scale(nxb, 2, STB)
                norm_scale(nxb, 3, STB)
            dn(1, 2)
            dn_add(1)
            if nxt:
                norm_transp(2, ntb)
                norm_transp(3, ntb)
                norm_evac(ntb, gmix, "cG")

            for i in range(4):
                P.add("act", lambda e, i=i: e.activation(out=sqj[:, :], in_=X[:, i, :], func=AF.Square,
                                                         accum_out=ss[:, i:i + 1]),
                      reads=[xr[i]], writes=["sqj", "ss"])
            P.add("act", lambda e: e.activation(out=lnv[:, :], in_=ss[:, :], func=AF.Ln, scale=1.0 / D, bias=epsT[:, 0:1]),
                  reads=["ss", "cA"], writes=["lnv"])
            P.add("act", lambda e: e.activation(out=rstd[:, :], in_=lnv[:, :], func=AF.Exp, scale=-0.5),
                  reads=["lnv"], writes=["rstd"])
            P.add("dve", lambda e: e.memset(ss[:, :], 0.0), reads=["lnv"], writes=["ss"])
            for i in range(4):
                P.add("act", lambda e, i=i: e.activation(out=X[:, i, :], in_=X[:, i, :], func=AF.Copy,
                                                         scale=rstd[:, i:i + 1]),
                      reads=[xr[i], "rstd"], writes=[xr[i]])
                P.add("pool", lambda e, i=i: e.tensor_tensor(out=X[:, i, :], in0=X[:, i, :], in1=gfbc[:, :],
                                                             op=ALU.mult),
                      reads=[xr[i], "cB"], writes=[xr[i]])
            pending_store.append((row0, xb))

        if not pending_store:
            pending_store.append((0, 0))
        pb_row0, pb_xb = pending_store.pop()
        P.add("sp", lambda e, r=pb_row0, b_=pb_xb: e.dma_start(
            out=out_d[r:r + TB, :].rearrange("(i p) d -> p i d", p=128), in_=x_tm[b_][:, :, :]),
            reads=["x%d.%d" % (pb_xb, i) for i in range(4)], writes=["outstore%d" % pb_xb], dma="st%d" % pb_xb)
        P.add("sp", None, reads=["outstore0", "outstore1", "cA", "cB", "KAaug.0", "ss"] + ["wslot%d" % i for i in range(NSLOT)])

        _LASTP.append(P)
        P.emit(nc, es)
    return nc


_CACHE = {}


def _get_program(NSEQ, SEQ):
    key = (NSEQ, SEQ)
    if key not in _CACHE:
        _CACHE[key] = build_program(NSEQ, SEQ)
    return _CACHE[key]


def kernel(x, norm_mix_g, w_in, b_f, b_gate, conv_mix_w, w_out_conv, w_out_attn, w_o, norm_ffn_g, w_up,
           conv_ffn_w, w_down, norm_f_g, _n_cores=N_CORES):
    x = np.asarray(x, dtype=np.float32)
    B, S, _ = x.shape
    n = _n_cores
    assert B % n == 0
    nseq = B // n
    nc = _get_program(nseq, S)
    f = lambda a: np.ascontiguousarray(np.asarray(a, dtype=np.float32))
    shared = {
        "norm_mix_g": f(norm_mix_g).reshape(D),
        "w_in": f(w_in).reshape(D, INW),
        "b_f": f(b_f).reshape(1, H),
        "b_gate": f(b_gate).reshape(2 * D),
        "conv_mix_w": f(conv_mix_w).reshape(3, 512),
        "w_out_conv": f(w_out_conv).reshape(512, D),
        "w_out_attn": f(w_out_attn).reshape(512, D),
        "w_o": f(w_o).reshape(D, D),
        "norm_ffn_g": f(norm_ffn_g).reshape(D),
        "w_up": f(w_up).reshape(D, 2 * FF),
        "conv_ffn_w": f(conv_ffn_w).reshape(3, 2 * FF),
        "w_down": f(w_down).reshape(FF, D),
        "norm_f_g": f(norm_f_g).reshape(1, D),
    }
    in_maps = []
    for c in range(n):
        m = dict(shared)
        m["x"] = np.ascontiguousarray(x[c * nseq:(c + 1) * nseq].reshape(nseq * S, D))
        in_maps.append(m)
    res = run_bass_kernel_spmd(nc, in_maps, core_ids=list(range(n)))
    outs = [np.asarray(r["out"]).reshape(nseq, S, D) for r in res.results]
    return np.concatenate(outs, axis=0).astype(np.float32)
```

```python
import numpy as np
from contextlib import ExitStack
import concourse.bass as bass
import concourse.mybir as mybir
from concourse.bass_utils import run_bass_kernel_spmd

F32 = mybir.dt.float32
BF16 = mybir.dt.bfloat16
AF = mybir.ActivationFunctionType
ALU = mybir.AluOpType

N_CORES = 8
D = 1024
NKC = 8
H = 8
FF = 2816
NFC = 22
INW = 5128
TB = 512
EPS = 1e-6
NSLOT = 4
NUNIT = 31
MASKV = -30000.0
_LASTP = []
_DBG = {"stop": 0}


class _Op:
    __slots__ = ("idx", "eng", "fn", "dma", "deps", "signal", "val", "grp")

    def __init__(self, idx, eng, fn, dma, grp):
        self.idx = idx
        self.eng = eng
        self.fn = fn
        self.dma = dma
        self.deps = []
        self.signal = False
        self.val = 0
        self.grp = grp


class _Rec:
    def __init__(self):
        self.call = None

    def __getattr__(self, name):
        def f(*a, **k):
            assert self.call is None
            self.call = (name, a, k)
            return self
        return f


class Prog:
    ENGS = ("pe", "act", "dve", "pool", "sp")

    def __init__(self):
        self.ops = []
        self.last_w = {}
        self.readers = {}
        self.barrier_keys = set()
        self.batch_keys = {}
        self.grp_last = {"M": {}, "F": {}}

    def pending(self, res):
        out = []
        w = self.last_w.get(res)
        if w is not None:
            out.append(w)
        for rd in self.readers.get(res, {}).values():
            out.extend(rd if isinstance(rd, list) else [rd])
        return out

    def add(self, eng, fn, reads=(), writes=(), dma=None, grp=None, after=()):
        if fn is not None:
            rec = _Rec()
            fn(rec)
            fn = rec.call
        op = _Op(len(self.ops), eng, fn, dma, grp)
        hard = {}
        soft = {}
        for o in after:
            hard[o.idx] = o
        for r in reads:
            w = self.last_w.get(r)
            if w is not None:
                hard[w.idx] = w
            if r.startswith("ps") and dma is None:
                for e2, rd in self.readers.get(r, {}).items():
                    if e2 != eng and e2 != "dma":
                        hard[rd.idx] = rd
        for r in writes:
            w = self.last_w.get(r)
            if w is not None:
                hard[w.idx] = w
            for rd in self.readers.get(r, {}).values():
                for o in (rd if isinstance(rd, list) else [rd]):
                    soft[o.idx] = o
        if grp is not None:
            other = "F" if grp == "M" else "M"
            for o in self.grp_last[other].values():
                soft[o.idx] = o
            if dma is None:
                self.grp_last[grp][eng] = op
        deps = []
        for d in hard.values():
            if d.dma is None and op.dma is None and d.eng == eng and eng == "pe":
                continue
            deps.append(d)
        for d in soft.values():
            if d.idx in hard:
                continue
            if d.dma is None and d.eng == eng and op.dma is None and eng != "pool":
                continue
            deps.append(d)
        op.deps = deps
        for d in deps:
            d.signal = True
        for r in reads:
            rd = self.readers.setdefault(r, {})
            if dma is not None:
                rd.setdefault("dma", []).append(op)
            else:
                rd[eng] = op
        for r in writes:
            self.last_w[r] = op
            self.readers[r] = {}
        self.ops.append(op)
        return op

    def emit(self, nc, es):
        sems = {}
        for e in self.ENGS:
            sems[e] = es.enter_context(nc.semaphore("s_" + e))
        cnt = {e: 0 for e in self.ENGS}
        dcnt = {}
        for op in self.ops:
            if op.dma is not None:
                if op.dma not in sems:
                    sems[op.dma] = es.enter_context(nc.semaphore("d_" + op.dma))
                    dcnt[op.dma] = 0
                dcnt[op.dma] += 16
                op.val = dcnt[op.dma]
            elif op.signal:
                cnt[op.eng] += 1
                op.val = cnt[op.eng]
        final = dict(dcnt)
        block = es.enter_context(nc.Block())
        per_eng = {e: [o for o in self.ops if o.eng == e] for e in self.ENGS}

        def run(engine, ename):
            waited = {}
            for op in per_eng[ename]:
                need = {}
                for d in op.deps:
                    if d.dma is not None:
                        k = d.dma
                        v = final[k] if k in self.barrier_keys else d.val
                        if k in self.batch_keys:
                            bsz = self.batch_keys[k]
                            v = (v + bsz - 1) // bsz * bsz
                    else:
                        k = d.eng
                        v = d.val
                    if v > need.get(k, 0):
                        need[k] = v
                for k, v in need.items():
                    if v > waited.get(k, 0):
                        engine.wait_ge(sems[k], v)
                        waited[k] = v
                if op.fn is None:
                    continue
                name, a, k = op.fn
                ins = getattr(engine, name)(*a, **k)
                if op.dma is not None:
                    ins.then_inc(sems[op.dma], 16)
                elif op.signal:
                    ins.then_inc(sems[ename], 1)

        @block.tensor
        def _(e):
            run(e, "pe")

        @block.scalar
        def _(e):
            run(e, "act")

        @block.vector
        def _(e):
            run(e, "dve")

        @block.gpsimd
        def _(e):
            run(e, "pool")

        @block.sync
        def _(e):
            run(e, "sp")


def build_program(NSEQ=4, SEQ=2048):
    NQB = SEQ // TB
    NBLK = NSEQ * NQB
    NKT = SEQ // 128
    TOK = NSEQ * SEQ
    nc = bass.Bass("TRN2", target_bir_lowering=False)

    def din(name, shape):
        return nc.dram_tensor(name, shape, F32, kind="ExternalInput").ap()

    x_d = din("x", [TOK, D])
    norm_mix_g = din("norm_mix_g", [D])
    w_in = din("w_in", [D, INW])
    b_f = din("b_f", [1, H])
    b_gate = din("b_gate", [2 * D])
    conv_mix_w = din("conv_mix_w", [3, 512])
    w_out_conv = din("w_out_conv", [512, D])
    w_out_attn = din("w_out_attn", [512, D])
    w_o = din("w_o", [D, D])
    norm_ffn_g = din("norm_ffn_g", [D])
    w_up = din("w_up", [D, 2 * FF])
    conv_ffn_w = din("conv_ffn_w", [3, 2 * FF])
    w_down = din("w_down", [FF, D])
    norm_f_g = din("norm_f_g", [1, D])
    out_d = nc.dram_tensor("out", [TOK, D], F32, kind="ExternalOutput").ap()
    ws = nc.dram_tensor("ws", [NUNIT, 128, 4096], BF16, kind="Internal").ap()

    es = ExitStack()
    with es:
        def sb(name, shape, dt):
            return es.enter_context(nc.sbuf_tensor(name, shape, dt))

        ident = sb("ident", [128, 128], BF16)
        trineg = sb("trineg", [128, 128], F32)
        negones = sb("negones", [128, 128], F32)
        maskw = sb("maskw", [128, 896], BF16)
        gmix = sb("gmix", [128, NKC], F32)
        gffn = sb("gffn", [128, NKC], F32)
        gfbc = sb("gfbc", [128, D], F32)
        cmw = sb("cmw", [128, 4, 3], F32)
        cfw = sb("cfw", [128, 44, 3], F32)
        bgate = sb("bgate", [128, 16], F32)
        bf4 = sb("bf4", [128, 4, H], F32)
        wf = sb("wf", [128, NKC, H], BF16)
        ss = sb("ss", [128, 4], F32)
        epsT = sb("epsT", [128, 1], F32)
        ss1 = sb("ss1", [128, 4], F32)
        lnv1 = sb("lnv1", [128, 4], F32)
        rstd1 = sb("rstd1", [128, 4], F32)
        sqj = sb("sqj", [128, D], BF16)
        identf = sb("identf", [128, 64], F32)
        lnv = sb("lnv", [128, 4], F32)
        rstd = sb("rstd", [128, 4], F32)
        PH = sb("PH", [128, 4, 2], F32)
        HS = sb("HS", [128, 44, 2], F32)
        carry = sb("carry", [128, 2], F32)
        x_tm = [sb("x_tm%d" % i, [128, 4, D], F32) for i in range(2)]
        xs = [sb("xs%d" % i, [128, D], BF16) for i in range(2)]
        hT = sb("hT", [128, NKC, TB], BF16)
        KA = sb("KA", [128, H, SEQ], BF16)
        VA = sb("VA", [128, NKT, 4, 192], BF16)
        wslot = [sb("wslot%d" % i, [128, 8, 512], BF16) for i in range(NSLOT)]
        ps_all = es.enter_context(nc.psum_tensor("ps_all", [128, 8, 512], F32))

        OVL_BYTES = 66048
        ovl = sb("ovl", [128, OVL_BYTES // 2], BF16)

        class Carver:
            def __init__(self):
                self.off = 0

            def get(self, free_shape, dt):
                n = int(np.prod(free_shape))
                nb = n * (4 if dt == F32 else 2)
                nb_al = (nb + 63) // 64 * 64
                assert self.off + nb_al <= OVL_BYTES, ("overlay overflow", self.off, nb_al)
                a = ovl[:, self.off // 2:(self.off + nb) // 2]
                self.off += nb_al
                if dt == F32:
                    a = a.bitcast(F32)
                if len(free_shape) == 2:
                    a = a.rearrange("p (a b) -> p a b", a=free_shape[0])
                elif len(free_shape) == 3:
                    a = a.rearrange("p (a b c) -> p a b c", a=free_shape[0], b=free_shape[1])
                return a

        cm = Carver()
        cu = cm.get([4, TB], F32)
        Pb = cm.get([4, TB + 2], F32)
        zT = cm.get([4, TB], BF16)
        QA = cm.get([H, TB], BF16)
        zf = cm.get([4, H], F32)
        ef = cm.get([4, H], F32)
        spf = cm.get([4, H], F32)
        Fq = cm.get([3, TB], BF16)
        Fk = cm.get([3, TB], BF16)
        PT = [cm.get([2, TB], BF16) for _ in range(3)]
        rl = [cm.get([TB], F32) for _ in range(2)]
        Fblk, r1 = rl[0], rl[1]
        oT = cm.get([4, TB], BF16)
        sgc = [cm.get([TB], BF16) for _ in range(2)]
        sga = [cm.get([TB], BF16) for _ in range(2)]
        t1 = [cm.get([TB], F32) for _ in range(2)]
        mT = cm.get([NKC, TB], BF16)
        mixer_bytes = cm.off
        cf = Carver()
        NR = 8
        A_sb = [cf.get([TB + 2], F32) for _ in range(NR)]
        acc = [cf.get([TB], F32) for _ in range(NR)]
        sa = [cf.get([TB], F32) for _ in range(2)]
        G = cf.get([NFC, TB], BF16)

        P = Prog()
        P.barrier_keys = {"constA", "constB"}
        P.batch_keys = {"fs": 96}

        def psn(b):
            return "ps%d" % b

        cres = {"A": [], "B": []}
        cdefer = []

        def cdma(out_ap, in_ap, g="B"):
            def go():
                cres[g].append("cdma%s%d" % (g, len(cres[g])))
                P.add("pool", lambda e, o=out_ap, i=in_ap: e.dma_start(out=o, in_=i, allow_slow_non_contiguous=True),
                      writes=[cres[g][-1]], dma="const" + g)
            if g == "A":
                go()
            else:
                cdefer.append(go)

        ovl32 = ovl[:, :].bitcast(F32)
        stgF = ovl32[:, 0:384].rearrange("p (k n) -> p k n", k=3)
        stgM = ovl32[:, 384:512]
        stgG = ovl32[:, 512:640]
        stgG2 = ovl32[:, 640:768]
        stgBg = ovl32[:, 768:896]
        for k in range(3):
            cdma(stgF[0:44, k, :], conv_ffn_w[k, :].rearrange("(c p) -> c p", p=128), "A")
        cdma(stgM[0:12, :], conv_mix_w.rearrange("k (c p) -> (k c) p", p=128), "A")
        cdma(stgG[0:8, :], norm_mix_g.rearrange("(c p) -> c p", p=128), "A")
        cdma(stgG2[0:8, :], norm_ffn_g.rearrange("(c p) -> c p", p=128), "A")
        cdma(stgBg[0:16, :], b_gate.rearrange("(c p) -> c p", p=128), "A")
        cdma(gfbc[:, :], norm_f_g[0:1, :].broadcast_to([128, D]))
        for i in range(4):
            cdma(bf4[:, i, :], b_f[0:1, :].broadcast_to([128, H]), "A")
        cdma(wf[:, :, :], w_in[:, 3072:3080].rearrange("(k p) n -> p k n", p=128), "A")

        def wsv(u):
            return ws[u].rearrange("p (k n) -> p k n", k=8)

        ws_parts = {}
        cvq = []

        def conv(u, k0, k1, n0, n1, src, grp):
            ws_parts.setdefault(u, []).append((k0, k1, n0, n1, src))

        def kin(w, r0, r1_, c0, c1):
            return w[r0:r1_, c0:c1].rearrange("(k p) n -> p k n", p=128)

        U_CIN, U_CC, U_CB, U_Q, U_K, U_V = 0, 1, 2, 3, 4, 5
        U_G = [6, 8, 9, 11]
        U_M = [7, 10]
        U_WO = [12, 13]
        U_UP = list(range(14, 25))
        U_DN = list(range(25, 31))
        conv(U_CIN, 0, 8, 0, 512, kin(w_in, 0, D, 1024, 1536), 0)
        conv(U_CC, 0, 8, 0, 512, kin(w_in, 0, D, 512, 1024), 0)
        conv(U_CB, 0, 8, 0, 512, kin(w_in, 0, D, 0, 512), 0)
        conv(U_Q, 0, 8, 0, 512, kin(w_in, 0, D, 1536, 2048), 0)
        conv(U_K, 0, 8, 0, 512, kin(w_in, 0, D, 2048, 2560), 0)
        conv(U_V, 0, 8, 0, 512, kin(w_in, 0, D, 2560, 3072), 0)
        for n in range(2):
            conv(U_M[n], 0, 4, 0, 512, kin(w_out_conv, 0, 512, n * 512, (n + 1) * 512), 1)
            conv(U_M[n], 4, 8, 0, 512, kin(w_out_attn, 0, 512, n * 512, (n + 1) * 512), 1)
        for q in range(4):
            conv(U_G[q], 0, 8, 0, 256, kin(w_in, 0, D, 3080 + q * 256, 3080 + (q + 1) * 256), 1)
            conv(U_G[q], 0, 8, 256, 512, kin(w_in, 0, D, 4104 + q * 256, 4104 + (q + 1) * 256), 1)
        for n in range(2):
            conv(U_WO[n], 0, 8, 0, 512, kin(w_o, 0, D, n * 512, (n + 1) * 512), 1)
        for u in range(11):
            conv(U_UP[u], 0, 8, 0, 256, kin(w_up, 0, D, 256 * u, 256 * (u + 1)), 2)
            conv(U_UP[u], 0, 8, 256, 512, kin(w_up, 0, D, FF + 256 * u, FF + 256 * (u + 1)), 2)
        DN_K = [(0, 8), (8, 16), (16, 22)]
        for n in range(2):
            for gi, (ka, kb) in enumerate(DN_K):
                conv(U_DN[n * 3 + gi], 0, kb - ka, 0, 512, kin(w_down, ka * 128, kb * 128, n * 512, (n + 1) * 512), 3)

        def pool_c(fn):
            P.add("pool", fn, reads=list(cres["A"]), writes=["cA"])

        pool_c(lambda e: e.memset(ident[:, :], 0.0))
        pool_c(lambda e: e.affine_select(out=ident[:, :], in_=ident[:, :], pattern=[[-1, 128]],
                                         compare_op=ALU.not_equal, fill=1.0, base=0, channel_multiplier=1))
        pool_c(lambda e: e.memset(trineg[:, :], -1.0))
        pool_c(lambda e: e.affine_select(out=trineg[:, :], in_=trineg[:, :], pattern=[[1, 128]],
                                         compare_op=ALU.is_ge, fill=0.0, base=0, channel_multiplier=-1))
        pool_c(lambda e: e.memset(negones[:, :], -1.0))
        pool_c(lambda e: e.memset(maskw[:, :], 0.0))
        pool_c(lambda e: e.affine_select(out=maskw[:, :], in_=maskw[:, :], pattern=[[1, 896]],
                                         compare_op=ALU.is_ge, fill=MASKV, base=-384, channel_multiplier=-1))
        P.add("pool", lambda e: e.memset(ss[:, :], 0.0), writes=["ss"])
        P.add("pool", lambda e: e.memset(ss1[:, :], 0.0), writes=["ss1"])
        pool_c(lambda e: e.memset(epsT[:, :], EPS))
        pool_c(lambda e: e.memset(identf[:, :], 0.0))
        pool_c(lambda e: e.affine_select(out=identf[:, :], in_=identf[:, :], pattern=[[-1, 64]],
                                         compare_op=ALU.not_equal, fill=1.0, base=0, channel_multiplier=1))
        for go in cdefer:
            go()
        cbk = 7
        def cmm(c0, c1, lhsT, K):
            P.add("pe", lambda e: e.matmul(ps_all[:, cbk, c0:c1], lhsT, identf[0:K, 0:K], start=True, stop=True),
                  reads=["cA"], writes=[psn(cbk)], grp="F")

        for k in range(3):
            cmm(k * 44, (k + 1) * 44, stgF[0:44, k, :], 44)
        cmm(132, 144, stgM[0:12, :], 12)
        cmm(144, 152, stgG[0:8, :], 8)
        cmm(152, 160, stgG2[0:8, :], 8)
        cmm(160, 176, stgBg[0:16, :], 16)
        P.add("dve", lambda e: e.tensor_copy(out=gmix[:, :], in_=ps_all[:, cbk, 144:152]), reads=[psn(cbk)], writes=["cG"])
        P.add("dve", lambda e: e.tensor_copy(out=gffn[:, :], in_=ps_all[:, cbk, 152:160]), reads=[psn(cbk)], writes=["cB0"])
        P.add("dve", lambda e: e.tensor_copy(out=bgate[:, :], in_=ps_all[:, cbk, 160:176]), reads=[psn(cbk)], writes=["cB1"])
        P.add("dve", lambda e: e.tensor_copy(out=cmw[:, :, :],
                                             in_=ps_all[:, cbk, 132:144].rearrange("p (k c) -> p c k", k=3)),
              reads=[psn(cbk)], writes=["cB2"])
        P.add("dve", lambda e: e.tensor_copy(out=cfw[:, :, :],
                                             in_=ps_all[:, cbk, 0:132].rearrange("p (k c) -> p c k", k=3)),
              reads=[psn(cbk), "cB0", "cB1", "cB2"] + list(cres["B"]), writes=["cB"])

        block_units = [U_CIN, U_CC, U_CB, U_Q, U_K, U_V,
                       U_G[0], U_M[0], U_G[1], U_G[2], U_M[1], U_G[3],
                       U_WO[0], U_WO[1]] + U_UP + [U_DN[0], U_DN[1], U_DN[3], U_DN[2], U_DN[4], U_DN[5]]
        stream = []
        for b in range(NBLK):
            stream += block_units
        st = {"issued": 0, "next_use": 0}

        def issue_load(k):
            if k >= len(stream):
                return
            assert k == st["issued"]
            u = stream[k]
            s = k % NSLOT
            nk = 6 if u in (U_DN[2], U_DN[5]) else 8
            if k < len(block_units):
                R = "wslot%d" % s
                if u in U_UP or u in U_G:
                    prev = P.pending(R)
                    for pi, (k0, k1, n0, n1, src) in enumerate(ws_parts[u]):
                        P.add("pool", lambda e, o=wslot[s][:, k0:k1, n0:n1], i=src: e.dma_start(out=o, in_=i),
                              writes=[R + "ab"[pi]], dma="wp%d%s" % (s, "ab"[pi]), after=prev)
                else:
                    for (k0, k1, n0, n1, src) in ws_parts[u]:
                        P.add("pool", lambda e, o=wslot[s][:, k0:k1, n0:n1], i=src: e.dma_start(out=o, in_=i),
                              writes=[R, R + "a", R + "b"], dma="wp%d" % s)
                P.add("sp", lambda e, o=wsv(u)[:, 0:nk, :], i=wslot[s][:, 0:nk, :]: e.dma_start(out=o, in_=i),
                      reads=[R, R + "a", R + "b"], writes=["ws%d" % u], dma="wst%d" % s)
            else:
                P.add("sp", lambda e, o=wslot[s][:, 0:nk, :], i=wsv(u)[:, 0:nk, :]: e.dma_start(out=o, in_=i),
                      reads=["ws%d" % u], writes=["wslot%d" % s, "wslot%da" % s, "wslot%db" % s], dma="w%d" % s)
            st["issued"] += 1

        P.add("sp", lambda e: e.dma_start(out=x_tm[0][:, :, :],
                                          in_=x_d[0:TB, :].rearrange("(i p) d -> p i d", p=128)),
              writes=["x0.%d" % i for i in range(4)], dma="xl0")
        for k0 in range(NSLOT):
            issue_load(k0)
        P.add("pool", lambda e: e.memset(VA[:, :, :, 64:128], 1.0), writes=["VA"])
        P.add("pool", lambda e: e.memset(KA[64:70, :, :], 1.0), writes=["KAaug.0", "KAaug.1", "KAaug.2"])

        def take_unit(u_expected):
            k = st["next_use"]
            assert stream[k] == u_expected, (k, stream[k], u_expected)
            assert k < st["issued"]
            st["next_use"] += 1
            s = k % NSLOT
            return wslot[s], "wslot%d" % s

        rel_state = {"k": 0}

        def rel():
            k = rel_state["k"]
            rel_state["k"] += 1
            issue_load(k + NSLOT)

        bank_rr = {"i": 0}

        def next_bank():
            b = bank_rr["i"] % 8
            bank_rr["i"] += 1
            return b

        STA = {"ss": ss, "lnv": lnv, "rstd": rstd, "n": ("ss", "lnv", "rstd")}
        STB = {"ss": ss1, "lnv": lnv1, "rstd": rstd1, "n": ("ss1", "lnv1", "rstd1")}

        def norm_stats(xb, S):
            X = x_tm[xb]
            rs, rl_, rr = S["n"]
            for i in range(4):
                P.add("act", lambda e, i=i: e.activation(out=sqj[:, :], in_=X[:, i, :], func=AF.Square,
                                                         accum_out=S["ss"][:, i:i + 1]),
                      reads=["x%d.%d" % (xb, i)], writes=["sqj", rs])
            P.add("act", lambda e: e.activation(out=S["lnv"][:, :], in_=S["ss"][:, :], func=AF.Ln, scale=1.0 / D,
                                                bias=epsT[:, 0:1]),
                  reads=[rs, "cA"], writes=[rl_])
            P.add("act", lambda e: e.activation(out=S["rstd"][:, :], in_=S["lnv"][:, :], func=AF.Exp, scale=-0.5),
                  reads=[rl_], writes=[rr])
            P.add("dve", lambda e: e.memset(S["ss"][:, :], 0.0), reads=[rl_], writes=[rs])

        def norm_scale(xb, i, S):
            X = x_tm[xb]
            if i % 2 == 0:
                P.add("dve", lambda e: e.tensor_scalar(out=xs[i % 2][:, :], in0=X[:, i, :],
                                                       scalar1=S["rstd"][:, i:i + 1], scalar2=None, op0=ALU.mult),
                      reads=["x%d.%d" % (xb, i), S["n"][2]], writes=["xs%d" % (i % 2)])
            else:
                P.add("act", lambda e: e.activation(out=xs[i % 2][:, :], in_=X[:, i, :], func=AF.Copy,
                                                    scale=S["rstd"][:, i:i + 1]),
                      reads=["x%d.%d" % (xb, i), S["n"][2]], writes=["xs%d" % (i % 2)])

        def norm_transp(i, tb):
            for c in range(NKC):
                bk = tb[c // 2]
                o = ps_all[:, bk, :].bitcast(BF16)[:, (c % 2) * 512 + i * 128:(c % 2) * 512 + (i + 1) * 128]
                P.add("pe", lambda e, o=o, c=c: e.transpose(o, xs[i % 2][:, c * 128:(c + 1) * 128], ident[:, :]),
                      reads=["xs%d" % (i % 2), "cA"], writes=[psn(bk)])

        def norm_evac(tb, gvec, gres):
            for c in range(NKC):
                bk = tb[c // 2]
                src = ps_all[:, bk, :].bitcast(BF16)[:, (c % 2) * 512:(c % 2 + 1) * 512]
                if (c // 2) % 2 == 0:
                    P.add("act", lambda e, c=c, src=src: e.activation(out=hT[:, c, :], in_=src, func=AF.Copy,
                                                                      scale=gvec[:, c:c + 1]),
                          reads=[psn(bk), gres], writes=["hT.%d" % c])
                else:
                    P.add("dve", lambda e, c=c, src=src: e.tensor_scalar(out=hT[:, c, :], in0=src,
                                                                         scalar1=gvec[:, c:c + 1], scalar2=None,
                                                                         op0=ALU.mult),
                          reads=[psn(bk), gres], writes=["hT.%d" % c])

        def rmsnorm_to_hT(xb, gvec, tag, grp):
            gres = "cG" if tag == "n1" else "cB"
            S = STB if tag == "n1" else STA
            norm_stats(xb, S)
            tb = [next_bank() for _ in range(4)]
            for i in range(4):
                norm_scale(xb, i, S)
                norm_transp(i, tb)
            norm_evac(tb, gvec, gres)

        hT_all = ["hT.%d" % c for c in range(NKC)]

        def proj_fm(wt, wname, col0, bk, grp):
            for kc in range(NKC):
                P.add("pe", lambda e, kc=kc: e.matmul(ps_all[:, bk, :], wt[:, kc, col0:col0 + 128], hT[:, kc, :],
                                                      start=(kc == 0), stop=(kc == NKC - 1)),
                      reads=[wname, "hT.%d" % kc], writes=[psn(bk)], grp=grp)

        pending_store = []
        for blk in range(0 if _DBG["stop"] == 1 else NBLK):
            seq = blk // NQB
            qb = blk % NQB
            xb = blk % 2
            t0 = qb * TB
            row0 = blk * TB
            nkt = 4 * (qb + 1)
            X = x_tm[xb]
            xr = ["x%d.%d" % (xb, i) for i in range(4)]

            if blk == 0:
                rmsnorm_to_hT(xb, gmix, "n1", "M")

            if _DBG["stop"] in (2, 20, 21, 22, 23, 24):
                break
            fb = next_bank()
            for i in range(4):
                for kc in range(NKC):
                    P.add("pe", lambda e, i=i, kc=kc: e.matmul(ps_all[:, fb, i * H:(i + 1) * H],
                                                               hT[:, kc, i * 128:(i + 1) * 128], wf[:, kc, :],
                                                               start=(kc == 0), stop=(kc == NKC - 1)),
                          reads=["hT.%d" % kc, "cA"], writes=[psn(fb)], grp="M")
            P.add("dve", lambda e: e.tensor_tensor(out=zf[:, :, :],
                                                   in0=ps_all[:, fb, 0:4 * H].rearrange("p (i h) -> p i h", i=4),
                                                   in1=bf4[:, :, :], op=ALU.add),
                  reads=[psn(fb), "cA"], writes=["zf"], grp="M")
            P.add("act", lambda e: e.activation(out=ef[:, :, :], in_=zf[:, :, :], func=AF.Exp, scale=-1.0),
                  reads=["zf"], writes=["ef"], grp="M")
            P.add("act", lambda e: e.activation(out=spf[:, :, :], in_=ef[:, :, :], func=AF.Ln, bias=1.0),
                  reads=["ef"], writes=["spf"], grp="M")
            wt, wn = take_unit(U_CIN)
            for c in range(4):
                bk = next_bank()
                proj_fm(wt, wn, c * 128, bk, "M")
                P.add("act", lambda e, c=c, bk=bk: e.activation(out=cu[:, c, :], in_=ps_all[:, bk, :], func=AF.Copy),
                      reads=[psn(bk)], writes=["cu.%d" % c], grp="M")
            rel()
            if qb == 0:
                P.add("pool", lambda e: e.memset(Pb[:, :, 0:2], 0.0), writes=["P.halo"], grp="M")
            else:
                P.add("pool", lambda e: e.tensor_copy(out=Pb[:, :, 0:2], in_=PH[:, :, :]), reads=["PH"],
                      writes=["P.halo"], grp="M")
            wt, wn = take_unit(U_CC)
            for c in range(4):
                bk = next_bank()
                proj_fm(wt, wn, c * 128, bk, "M")
                P.add("dve", lambda e, c=c, bk=bk: e.tensor_tensor(out=Pb[:, c, 2:TB + 2], in0=ps_all[:, bk, :],
                                                                   in1=cu[:, c, :], op=ALU.mult),
                      reads=[psn(bk), "cu.%d" % c], writes=["P.%d" % c], grp="M")
                P.add("act", lambda e, c=c: e.activation(out=cu[:, c, :], in_=Pb[:, c, 2:TB + 2], func=AF.Copy,
                                                         scale=cmw[:, c, 2:3]),
                      reads=["P.%d" % c, "cB"], writes=["cu.%d" % c], grp="M")
                P.add("dve", lambda e, c=c: e.scalar_tensor_tensor(out=cu[:, c, :], in0=Pb[:, c, 1:TB + 1],
                                                                    scalar=cmw[:, c, 1:2], in1=cu[:, c, :],
                                                                    op0=ALU.mult, op1=ALU.add),
                      reads=["P.%d" % c, "P.halo", "cB", "cu.%d" % c], writes=["cu.%d" % c], grp="M")
                P.add("dve", lambda e, c=c: e.scalar_tensor_tensor(out=cu[:, c, :], in0=Pb[:, c, 0:TB],
                                                                    scalar=cmw[:, c, 0:1], in1=cu[:, c, :],
                                                                    op0=ALU.mult, op1=ALU.add),
                      reads=["P.%d" % c, "P.halo", "cB", "cu.%d" % c], writes=["cu.%d" % c], grp="M")
            rel()
            P.add("pool", lambda e: e.tensor_copy(out=PH[:, :, :], in_=Pb[:, :, TB:TB + 2]),
                  reads=["P.%d" % c for c in range(4)], writes=["PH"], grp="M")
            wt, wn = take_unit(U_CB)
            for c in range(4):
                bk = next_bank()
                proj_fm(wt, wn, c * 128, bk, "M")
                P.add("dve", lambda e, c=c, bk=bk: e.tensor_tensor(out=zT[:, c, :], in0=ps_all[:, bk, :],
                                                                   in1=cu[:, c, :], op=ALU.mult),
                      reads=[psn(bk), "cu.%d" % c], writes=["zT.%d" % c], grp="M")

            rel()
            if _DBG["stop"] == 3:
                break
            fb2 = next_bank()
            for i in range(4):
                P.add("pe", lambda e, i=i: e.matmul(ps_all[0:H, fb2, i * 128:(i + 1) * 128], spf[:, i, :],
                                                    trineg[:, :], start=True, stop=(i == 0)),
                      reads=["spf", "cA"], writes=[psn(fb2)], grp="M")
                for i2 in range(i):
                    P.add("pe", lambda e, i=i, i2=i2: e.matmul(ps_all[0:H, fb2, i * 128:(i + 1) * 128],
                                                               spf[:, i2, :], negones[:, :], start=False,
                                                               stop=(i2 == i - 1)),
                          reads=["spf", "cA"], writes=[psn(fb2)], grp="M")
            cb_ = blk % 2
            if qb == 0:
                P.add("dve", lambda e, cb_=cb_: e.memset(carry[0:H, cb_:cb_ + 1], 0.0), writes=["carry%d" % cb_])
            P.add("dve", lambda e, cb_=cb_: e.tensor_scalar(out=Fblk[0:H, :], in0=ps_all[0:H, fb2, :],
                                                            scalar1=carry[0:H, cb_:cb_ + 1], scalar2=None,
                                                            op0=ALU.add),
                  reads=[psn(fb2), "carry%d" % cb_], writes=["rl0"], grp="M")
            f_ops = []
            f_ops.append(lambda: P.add("dve", lambda e, cb_=cb_: e.tensor_copy(out=carry[0:H, 1 - cb_:2 - cb_], in_=Fblk[0:H, TB - 1:TB]),
                      reads=["rl0"], writes=["carry%d" % (1 - cb_)], grp="M"))
            f_ops.append(lambda: P.add("dve", lambda e: e.tensor_copy(out=Fq[0:H, 0, :], in_=Fblk[0:H, :]),
                      reads=["rl0"], writes=["Fq0"], grp="M"))
            f_ops.append(lambda: P.add("dve", lambda e: e.tensor_tensor(out=r1[0:H, :], in0=Fblk[0:H, :], in1=Fq[0:H, 0, :],
                                                        op=ALU.subtract),
                      reads=["rl0", "Fq0"], writes=["rl1"], grp="M"))
            f_ops.append(lambda: P.add("dve", lambda e: e.tensor_copy(out=Fq[0:H, 1, :], in_=r1[0:H, :]),
                      reads=["rl1"], writes=["Fq1"], grp="M"))
            f_ops.append(lambda: P.add("dve", lambda e: e.tensor_tensor(out=Fblk[0:H, :], in0=r1[0:H, :], in1=Fq[0:H, 1, :],
                                                        op=ALU.subtract),
                      reads=["rl1", "Fq1"], writes=["rl0"], grp="M"))
            f_ops.append(lambda: P.add("dve", lambda e: e.tensor_copy(out=Fq[0:H, 2, :], in_=Fblk[0:H, :]),
                      reads=["rl0"], writes=["Fq2"], grp="M"))
            f_ops.append(lambda: P.add("dve", lambda e: e.tensor_scalar(out=Fk[0:H, :, :], in0=Fq[0:H, :, :], scalar1=-1.0,
                                                        scalar2=None, op0=ALU.mult),
                      reads=["Fq0", "Fq1", "Fq2"], writes=["Fk"], grp="M"))

            if _DBG["stop"] == 4:
                break
            wt, wn = take_unit(U_Q)
            for c in range(4):
                bk = next_bank()
                proj_fm(wt, wn, c * 128, bk, "M")
                P.add("act", lambda e, c=c, bk=bk: e.activation(out=QA[0:64, 2 * c, :], in_=ps_all[0:64, bk, :],
                                                                func=AF.Copy, scale=0.125),
                      reads=[psn(bk)], writes=["QAq.%d" % (2 * c)], grp="M")
                P.add("dve", lambda e, c=c, bk=bk: e.tensor_scalar(out=QA[0:64, 2 * c + 1, :],
                                                                   in0=ps_all[64:128, bk, :], scalar1=0.125,
                                                                   scalar2=None, op0=ALU.mult),
                      reads=[psn(bk)], writes=["QAq.%d" % (2 * c + 1)], grp="M")
                if f_ops:
                    f_ops.pop(0)()
            rel()
            wt, wn = take_unit(U_K)
            for c in range(4):
                bk = next_bank()
                proj_fm(wt, wn, c * 128, bk, "M")
                P.add("act", lambda e, c=c, bk=bk: e.activation(out=KA[0:64, 2 * c, t0:t0 + TB],
                                                                in_=ps_all[0:64, bk, :], func=AF.Copy),
                      reads=[psn(bk)], writes=["KAk.%d" % (2 * c)], grp="M")
                P.add("dve", lambda e, c=c, bk=bk: e.tensor_copy(out=KA[0:64, 2 * c + 1, t0:t0 + TB],
                                                                 in_=ps_all[64:128, bk, :]),
                      reads=[psn(bk)], writes=["KAk.%d" % (2 * c + 1)], grp="M")
                if f_ops:
                    f_ops.pop(0)()
            rel()
            while f_ops:
                f_ops.pop(0)()
            P.add("pool", lambda e: e.memset(QA[64:70, :, :], 1.0), writes=["QAaug.0", "QAaug.1", "QAaug.2"], grp="M")
            for r in range(3):
                P.add("pool", lambda e, r=r: e.dma_start(out=QA[64 + r:65 + r, :, :], in_=Fq[0:H, r, :]),
                      reads=["Fq%d" % r], writes=["QAaug.%d" % r], dma="fs", grp="M")
                P.add("pool", lambda e, r=r: e.dma_start(out=KA[67 + r:68 + r, :, t0:t0 + TB], in_=Fk[0:H, r, :]),
                      reads=["Fk"], writes=["KAaug.%d" % r], dma="fs", grp="M")

            wt, wn = take_unit(U_V)
            for i in range(4):
                bk = next_bank()
                kt = 4 * qb + i
                for kc in range(NKC):
                    P.add("pe", lambda e, i=i, kc=kc, bk=bk: e.matmul(ps_all[:, bk, :],
                                                                      hT[:, kc, i * 128:(i + 1) * 128],
                                                                      wt[:, kc, :], start=(kc == 0),
                                                                      stop=(kc == NKC - 1)),
                          reads=[wn, "hT.%d" % kc], writes=[psn(bk)], grp="M")
                pv_ = ps_all[:, bk, :].rearrange("p (j e d) -> p j e d", j=4, e=2)
                P.add("act", lambda e, kt=kt, pv_=pv_: e.activation(out=VA[:, kt, :, 0:64], in_=pv_[:, :, 0, :],
                                                                    func=AF.Copy),
                      reads=[psn(bk)], writes=["VA"], grp="M")
                P.add("dve", lambda e, kt=kt, pv_=pv_: e.tensor_copy(out=VA[:, kt, :, 128:192], in_=pv_[:, :, 1, :]),
                      reads=[psn(bk)], writes=["VA"], grp="M")

            rel()
            if pending_store:
                pb_row0, pb_xb = pending_store.pop()
                P.add("sp", lambda e, r=pb_row0, b_=pb_xb: e.dma_start(
                    out=out_d[r:r + TB, :].rearrange("(i p) d -> p i d", p=128), in_=x_tm[b_][:, :, :]),
                    reads=["x%d.%d" % (pb_xb, i) for i in range(4)], writes=["outstore%d" % pb_xb], dma="st%d" % pb_xb)
            if blk + 1 < NBLK:
                nb_ = (blk + 1) % 2
                P.add("sp", lambda e, r=row0 + TB, b_=nb_: e.dma_start(
                    out=x_tm[b_][:, :, :], in_=x_d[r:r + TB, :].rearrange("(i p) d -> p i d", p=128)),
                    reads=["outstore%d" % nb_], writes=["x%d.%d" % (nb_, i) for i in range(4)], dma="xl%d" % nb_)
            if _DBG["stop"] == 5:
                break
            ng = nkt // 2
            items = [(h, g) for h in range(H) for g in range(ng)]
            LAG = 2

            def grp_c0(g):
                return 256 if 2 * g == 4 * qb + 2 else 0

            def emit_qk(idx, h, g):
                sg_ = idx % 3
                qk_reads = (["KAk.%d" % h, "QAq.%d" % h] + ["KAaug.%d" % r_ for r_ in range(3)]
                            + ["QAaug.%d" % r_ for r_ in range(3)])
                c0 = grp_c0(g)
                for sl in range(2):
                    kt = 2 * g + sl
                    bk = 2 * sg_ + sl
                    diag = kt >= 4 * qb
                    kap = KA[0:70, h, kt * 128:(kt + 1) * 128]
                    P.add("pe", lambda e, bk=bk, kap=kap, diag=diag: e.matmul(
                        ps_all[:, bk, c0:TB], kap, QA[0:70, h, c0:TB], start=True, stop=(not diag)),
                        reads=qk_reads, writes=[psn(bk)], grp="M")
                    if diag:
                        j = kt - 4 * qb
                        c1 = 128 * (j + 1)
                        m0 = 384 - 128 * j + c0
                        P.add("pe", lambda e, bk=bk, c1=c1, m0=m0: e.matmul(
                            ps_all[:, bk, c0:c1], ident[:, :], maskw[:, m0:m0 + (c1 - c0)], start=False, stop=True),
                            reads=["cA"], writes=[psn(bk)], grp="M")
                P.add("act", lambda e, sg_=sg_: e.activation(out=PT[sg_][:, :, c0:TB],
                                                             in_=ps_all[:, 2 * sg_:2 * sg_ + 2, c0:TB], func=AF.Exp),
                      reads=[psn(2 * sg_), psn(2 * sg_ + 1)], writes=["PT%d" % sg_], grp="M")

            def emit_pv(idx, h, g):
                sg_ = idx % 3
                pvb = 6 + h % 2
                j = h // 2
                eo = h % 2
                for sl in range(2):
                    kt = 2 * g + sl
                    c0 = grp_c0(g)
                    P.add("pe", lambda e, kt=kt, sl=sl, j=j, eo=eo, pvb=pvb, sg_=sg_, c0=c0: e.matmul(
                        ps_all[:, pvb, c0:TB], VA[:, kt, j, eo * 64:eo * 64 + 128], PT[sg_][:, sl, c0:TB],
                        start=(kt == 0), stop=(kt == nkt - 1)),
                        reads=["VA", "PT%d" % sg_], writes=[psn(pvb)], grp="M")
                if g == ng - 1:
                    oP = slice(0, 64) if eo == 0 else slice(64, 128)
                    lP = slice(64, 128) if eo == 0 else slice(0, 64)
                    k = h % 2
                    if qb == 0:
                        P.add("act", lambda e, lP=lP, k=k, pvb=pvb: e.activation(out=t1[k][lP, :], in_=ps_all[lP, pvb, :],
                                                                                 func=AF.Ln),
                              reads=[psn(pvb)], writes=["t1%d" % k], grp="M")
                        P.add("act", lambda e, lP=lP, k=k: e.activation(out=t1[k][lP, :], in_=t1[k][lP, :], func=AF.Exp,
                                                                        scale=-1.0),
                              reads=["t1%d" % k], writes=["t1%d" % k], grp="M")
                        P.add("dve", lambda e, oP=oP, lP=lP, k=k: e.tensor_copy(out=rl[k][oP, :], in_=t1[k][lP, :]),
                              reads=["t1%d" % k], writes=["rl%d" % k], grp="M")
                    else:
                        P.add("dve", lambda e, oP=oP, lP=lP, k=k, pvb=pvb: e.reciprocal(out=rl[k][oP, :],
                                                                                        in_=ps_all[lP, pvb, :]),
                              reads=[psn(pvb)], writes=["rl%d" % k], grp="M")
                    P.add("dve", lambda e, oP=oP, k=k, pvb=pvb, j=j: e.tensor_tensor(
                        out=oT[oP, j, :], in0=ps_all[oP, pvb, :], in1=rl[k][oP, :], op=ALU.mult),
                        reads=[psn(pvb), "rl%d" % k], writes=["oT.%d" % h], grp="M")

            for idx, (h, g) in enumerate(items):
                emit_qk(idx, h, g)
                if idx >= LAG:
                    emit_pv(idx - LAG, *items[idx - LAG])
            for idx in range(max(0, len(items) - LAG), len(items)):
                emit_pv(idx, *items[idx])
            bank_rr["i"] = 0

            if _DBG["stop"] == 6:
                break
            for n in range(2):
                for qq in range(2):
                    q = 2 * n + qq
                    wg, wgn = take_unit(U_G[q])
                    if qq == 0:
                        wm, wmn = take_unit(U_M[n])
                    for cc in range(2):
                        c = 2 * q + cc
                        r = c % 2
                        b_gc, b_ga = next_bank(), next_bank()
                        proj_fm(wg, wgn + "a", cc * 128, b_gc, "M")
                        P.add("act", lambda e, c=c, r=r, b_gc=b_gc: e.activation(
                            out=sgc[r][:, :], in_=ps_all[:, b_gc, :], func=AF.Sigmoid, bias=bgate[:, c:c + 1]),
                            reads=[psn(b_gc), "cB"], writes=["sgc%d" % r], grp="M")
                        proj_fm(wg, wgn + "b", 256 + cc * 128, b_ga, "M")
                        P.add("act", lambda e, c=c, r=r, b_ga=b_ga: e.activation(
                            out=sga[r][:, :], in_=ps_all[:, b_ga, :], func=AF.Sigmoid, bias=bgate[:, 8 + c:9 + c]),
                            reads=[psn(b_ga), "cB"], writes=["sga%d" % r], grp="M")
                    for cc in range(2):
                        c = 2 * q + cc
                        r = c % 2
                        cl = c - 4 * n
                        b_yc, b_ya = next_bank(), next_bank()
                        for kc in range(4):
                            P.add("pe", lambda e, kc=kc, cl=cl, b_yc=b_yc: e.matmul(
                                ps_all[:, b_yc, :], wm[:, kc, cl * 128:(cl + 1) * 128], zT[:, kc, :],
                                start=(kc == 0), stop=(kc == 3)),
                                reads=[wmn, "zT.%d" % kc], writes=[psn(b_yc)], grp="M")
                        for kc in range(4):
                            P.add("pe", lambda e, kc=kc, cl=cl, b_ya=b_ya: e.matmul(
                                ps_all[:, b_ya, :], wm[:, 4 + kc, cl * 128:(cl + 1) * 128], oT[:, kc, :],
                                start=(kc == 0), stop=(kc == 3)),
                                reads=[wmn, "oT.%d" % (2 * kc), "oT.%d" % (2 * kc + 1)], writes=[psn(b_ya)], grp="M")
                        P.add("dve", lambda e, r=r, b_yc=b_yc: e.tensor_tensor(out=t1[r][:, :], in0=ps_all[:, b_yc, :],
                                                                               in1=sgc[r][:, :], op=ALU.mult),
                              reads=[psn(b_yc), "sgc%d" % r], writes=["t1%d" % r], grp="M")
                        P.add("dve", lambda e, r=r, b_ya=b_ya: e.tensor_tensor(out=sga[r][:, :], in0=ps_all[:, b_ya, :],
                                                                               in1=sga[r][:, :], op=ALU.mult),
                              reads=[psn(b_ya), "sga%d" % r], writes=["sga%d" % r], grp="M")
                        P.add("pool", lambda e, r=r, c=c: e.tensor_tensor(out=mT[:, c, :], in0=t1[r][:, :],
                                                                          in1=sga[r][:, :], op=ALU.add),
                              reads=["t1%d" % r, "sga%d" % r], writes=["mT.%d" % c], grp="M")
                    rel()
                    if qq == 1:
                        rel()
            P.add("act", lambda e: e.activation(out=lnv[:, 0:1], in_=epsT[:, 0:1], func=AF.Ln), reads=["cA"],
                  writes=["lnv"])
            wo = [take_unit(U_WO[0]), take_unit(U_WO[1])]
            tb2 = None

            def n2_group(c0, c1, gname):
                P.add("act", lambda e: e.activation(out=lnv[:, c0:c1], in_=ss[:, c0:c1], func=AF.Ln, scale=1.0 / D,
                                                    bias=epsT[:, 0:1]),
                      reads=["ssA.%d" % i_ for i_ in range(c0, c1)] + ["cA", "lnv"], writes=["lnvA." + gname])
                P.add("act", lambda e: e.activation(out=rstd[:, c0:c1], in_=lnv[:, c0:c1], func=AF.Exp, scale=-0.5),
                      reads=["lnvA." + gname, "rstd"], writes=["rstdA." + gname])
                P.add("dve", lambda e: e.memset(ss[:, c0:c1], 0.0), reads=["lnvA." + gname],
                      writes=["ssA.%d" % i_ for i_ in range(c0, c1)])

            def n2_scale(i, gname):
                if i % 2 == 0:
                    P.add("dve", lambda e: e.tensor_scalar(out=xs[i % 2][:, :], in0=X[:, i, :],
                                                           scalar1=rstd[:, i:i + 1], scalar2=None, op0=ALU.mult),
                          reads=[xr[i], "rstdA." + gname], writes=["xs%d" % (i % 2)])
                else:
                    P.add("act", lambda e: e.activation(out=xs[i % 2][:, :], in_=X[:, i, :], func=AF.Copy,
                                                        scale=rstd[:, i:i + 1]),
                          reads=[xr[i], "rstdA." + gname], writes=["xs%d" % (i % 2)])

            for i in range(4):
                for n in range(2):
                    wt, wn = wo[n]
                    bk = next_bank()
                    for kc in range(NKC):
                        P.add("pe", lambda e, i=i, kc=kc, bk=bk, wt=wt: e.matmul(
                            ps_all[:, bk, :], mT[:, kc, i * 128:(i + 1) * 128], wt[:, kc, :],
                            start=(kc == 0), stop=(kc == NKC - 1)),
                            reads=[wn, "mT.%d" % kc], writes=[psn(bk)], grp="M")
                    P.add("dve", lambda e, i=i, n=n, bk=bk: e.tensor_tensor(
                        out=X[:, i, n * 512:(n + 1) * 512], in0=ps_all[:, bk, :], in1=X[:, i, n * 512:(n + 1) * 512],
                        op=ALU.add),
                        reads=[psn(bk), xr[i]], writes=[xr[i]], grp="M")
                P.add("act", lambda e, i=i: e.activation(out=sqj[:, :], in_=X[:, i, :], func=AF.Square,
                                                         accum_out=ss[:, i:i + 1]),
                      reads=[xr[i], "ss"], writes=["sqj", "ssA.%d" % i])
                if i == 2:
                    n2_group(0, 3, "a")
                    n2_scale(0, "a")
                    n2_scale(1, "a")
            rel()
            rel()
            n2_group(3, 4, "b")
            tb2 = [next_bank() for _ in range(4)]
            norm_transp(0, tb2)
            norm_transp(1, tb2)
            n2_scale(2, "a")
            norm_transp(2, tb2)
            n2_scale(3, "b")
            norm_transp(3, tb2)
            P.add("dve", lambda e: e.memset(ss[:, :], 0.0), reads=["lnvA.a", "lnvA.b", "rstdA.a", "rstdA.b"],
                  writes=["ss", "lnv", "rstd"] + ["ssA.%d" % i_ for i_ in range(4)])
            norm_evac(tb2, gffn, "cB")

            if _DBG["stop"] == 7:
                break

            rr = {"i": 0}

            def ffn_chunk(wt, wn, colbase, jj, want_silu, j):
                bk = next_bank()
                proj_fm(wt, wn, colbase, bk, "F")
                r = rr["i"] % NR
                rr["i"] += 1
                A = A_sb[r]
                ac = acc[r]
                P.add("act", lambda e, A=A, bk=bk: e.activation(out=A[:, 2:TB + 2], in_=ps_all[:, bk, :], func=AF.Copy),
                      reads=[psn(bk)], writes=["A%d" % r], grp="F")
                P.add("act", lambda e, ac=ac, bk=bk, jj=jj: e.activation(out=ac[:, :], in_=ps_all[:, bk, :],
                                                                         func=AF.Copy, scale=cfw[:, jj, 2:3]),
                      reads=[psn(bk), "cB"], writes=["acc%d" % r], grp="F")
                if qb == 0:
                    P.add("pool", lambda e, A=A: e.memset(A[:, 0:2], 0.0), writes=["Ah%d" % r], grp="F")
                else:
                    P.add("pool", lambda e, A=A, jj=jj: e.tensor_copy(out=A[:, 0:2], in_=HS[:, jj, :]),
                          reads=["HS.%d" % jj], writes=["Ah%d" % r], grp="F")
                P.add("pool", lambda e, A=A, jj=jj: e.tensor_copy(out=HS[:, jj, :], in_=A[:, TB:TB + 2]),
                      reads=["A%d" % r], writes=["HS.%d" % jj], grp="F")
                eng = "dve"
                P.add(eng, lambda e, A=A, ac=ac, jj=jj: e.scalar_tensor_tensor(
                    out=ac[:, :], in0=A[:, 1:TB + 1], scalar=cfw[:, jj, 1:2], in1=ac[:, :], op0=ALU.mult, op1=ALU.add),
                    reads=["A%d" % r, "Ah%d" % r, "acc%d" % r, "cB"], writes=["acc%d" % r], grp="F")
                P.add(eng, lambda e, A=A, ac=ac, jj=jj: e.scalar_tensor_tensor(
                    out=ac[:, :], in0=A[:, 0:TB], scalar=cfw[:, jj, 0:1], in1=ac[:, :], op0=ALU.mult, op1=ALU.add),
                    reads=["A%d" % r, "Ah%d" % r, "acc%d" % r, "cB"], writes=["acc%d" % r], grp="F")
                return r

            pend_fin = []

            def ffn_finish(j, ra, rb):
                sr = j % 2
                P.add("act", lambda e, ra=ra, sr=sr: e.activation(out=sa[sr][:, :], in_=acc[ra][:, :], func=AF.Silu),
                      reads=["acc%d" % ra], writes=["sa%d" % sr], grp="F")
                P.add("pool", lambda e, rb=rb, sr=sr, j=j: e.tensor_tensor(out=G[:, j, :], in0=sa[sr][:, :],
                                                                           in1=acc[rb][:, :], op=ALU.mult),
                      reads=["sa%d" % sr, "acc%d" % rb], writes=["G.%d" % j], grp="F")

            for u in range(11):
                wt, wn = take_unit(U_UP[u])
                for cl in range(2):
                    j = 2 * u + cl
                    ra = ffn_chunk(wt, wn + "a", cl * 128, j, True, j)
                    rb = ffn_chunk(wt, wn + "b", 256 + cl * 128, NFC + j, False, j)
                    pend_fin.append((j, ra, rb))
                    if len(pend_fin) > 1:
                        ffn_finish(*pend_fin.pop(0))
                rel()
            while pend_fin:
                ffn_finish(*pend_fin.pop(0))

            if _DBG["stop"] == 8:
                break
            nxt = blk + 1 < NBLK
            nxb = (blk + 1) % 2
            if nxt:
                norm_stats(nxb, STB)
                norm_scale(nxb, 0, STB)
                norm_scale(nxb, 1, STB)
            bankset = [[next_bank() for _ in range(4)], [next_bank() for _ in range(4)]]

            def dn(n, gi):
                ka, kb = DN_K[gi]
                wt, wn = take_unit(U_DN[n * 3 + gi])
                for i in range(4):
                    for j in range(ka, kb):
                        P.add("pe", lambda e, i=i, j=j, bk=bankset[n][i]: e.matmul(
                            ps_all[:, bk, :], G[:, j, i * 128:(i + 1) * 128], wt[:, j - ka, :],
                            start=(j == 0), stop=(j == NFC - 1)),
                            reads=[wn, "G.%d" % j], writes=[psn(bankset[n][i])], grp="F")
                rel()

            def dn_add(n):
                for i in range(4):
                    P.add("dve", lambda e, i=i, bk=bankset[n][i]: e.tensor_tensor(
                        out=X[:, i, n * 512:(n + 1) * 512], in0=ps_all[:, bk, :], in1=X[:, i, n * 512:(n + 1) * 512],
                        op=ALU.add),
                        reads=[psn(bankset[n][i]), xr[i]], writes=[xr[i]], grp="F")

            dn(0, 0)
            dn(0, 1)
            dn(1, 0)
            dn(0, 2)
            dn_add(0)
            dn(1, 1)
            ntb = bankset[0]
            if nxt:
                norm_transp(0, ntb)
                norm_transp(1, ntb)
                norm_scale(nxb, 2, STB)
                norm_scale(nxb, 3, STB)
            dn(1, 2)
            dn_add(1)
            if nxt:
                norm_transp(2, ntb)
                norm_transp(3, ntb)
                norm_evac(ntb, gmix, "cG")

            for i in range(4):
                P.add("act", lambda e, i=i: e.activation(out=sqj[:, :], in_=X[:, i, :], func=AF.Square,
                                                         accum_out=ss[:, i:i + 1]),
                      reads=[xr[i]], writes=["sqj", "ss"])
            P.add("act", lambda e: e.activation(out=lnv[:, :], in_=ss[:, :], func=AF.Ln, scale=1.0 / D, bias=epsT[:, 0:1]),
                  reads=["ss", "cA"], writes=["lnv"])
            P.add("act", lambda e: e.activation(out=rstd[:, :], in_=lnv[:, :], func=AF.Exp, scale=-0.5),
                  reads=["lnv"], writes=["rstd"])
            P.add("dve", lambda e: e.memset(ss[:, :], 0.0), reads=["lnv"], writes=["ss"])
            for i in range(4):
                P.add("act", lambda e, i=i: e.activation(out=X[:, i, :], in_=X[:, i, :], func=AF.Copy,
                                                         scale=rstd[:, i:i + 1]),
                      reads=[xr[i], "rstd"], writes=[xr[i]])
                P.add("pool", lambda e, i=i: e.tensor_tensor(out=X[:, i, :], in0=X[:, i, :], in1=gfbc[:, :],
                                                             op=ALU.mult),
                      reads=[xr[i], "cB"], writes=[xr[i]])
            pending_store.append((row0, xb))

        if not pending_store:
            pending_store.append((0, 0))
        pb_row0, pb_xb = pending_store.pop()
        P.add("sp", lambda e, r=pb_row0, b_=pb_xb: e.dma_start(
            out=out_d[r:r + TB, :].rearrange("(i p) d -> p i d", p=128), in_=x_tm[b_][:, :, :]),
            reads=["x%d.%d" % (pb_xb, i) for i in range(4)], writes=["outstore%d" % pb_xb], dma="st%d" % pb_xb)
        P.add("sp", None, reads=["outstore0", "outstore1", "cA", "cB", "KAaug.0", "ss"] + ["wslot%d" % i for i in range(NSLOT)])

        _LASTP.append(P)
        P.emit(nc, es)
    return nc


_CACHE = {}


def _get_program(NSEQ, SEQ):
    key = (NSEQ, SEQ)
    if key not in _CACHE:
        _CACHE[key] = build_program(NSEQ, SEQ)
    return _CACHE[key]


def kernel(x, norm_mix_g, w_in, b_f, b_gate, conv_mix_w, w_out_conv, w_out_attn, w_o, norm_ffn_g, w_up,
           conv_ffn_w, w_down, norm_f_g, _n_cores=N_CORES):
    x = np.asarray(x, dtype=np.float32)
    B, S, _ = x.shape
    n = _n_cores
    assert B % n == 0
    nseq = B // n
    nc = _get_program(nseq, S)
    f = lambda a: np.ascontiguousarray(np.asarray(a, dtype=np.float32))
    shared = {
        "norm_mix_g": f(norm_mix_g).reshape(D),
        "w_in": f(w_in).reshape(D, INW),
        "b_f": f(b_f).reshape(1, H),
        "b_gate": f(b_gate).reshape(2 * D),
        "conv_mix_w": f(conv_mix_w).reshape(3, 512),
        "w_out_conv": f(w_out_conv).reshape(512, D),
        "w_out_attn": f(w_out_attn).reshape(512, D),
        "w_o": f(w_o).reshape(D, D),
        "norm_ffn_g": f(norm_ffn_g).reshape(D),
        "w_up": f(w_up).reshape(D, 2 * FF),
        "conv_ffn_w": f(conv_ffn_w).reshape(3, 2 * FF),
        "w_down": f(w_down).reshape(FF, D),
        "norm_f_g": f(norm_f_g).reshape(1, D),
    }
    in_maps = []
    for c in range(n):
        m = dict(shared)
        m["x"] = np.ascontiguousarray(x[c * nseq:(c + 1) * nseq].reshape(nseq * S, D))
        in_maps.append(m)
    res = run_bass_kernel_spmd(nc, in_maps, core_ids=list(range(n)))
    outs = [np.asarray(r["out"]).reshape(nseq, S, D) for r in res.results]
    return np.concatenate(outs, axis=0).astype(np.float32)
```

```python
import numpy as np
from contextlib import ExitStack
import concourse.bass as bass
import concourse.mybir as mybir
from concourse.bass_utils import run_bass_kernel_spmd

F32 = mybir.dt.float32
BF16 = mybir.dt.bfloat16
AF = mybir.ActivationFunctionType
ALU = mybir.AluOpType

N_CORES = 8
D = 1024
NKC = 8
H = 8
FF = 2816
NFC = 22
INW = 5128
TB = 512
EPS = 1e-6
NSLOT = 4
NUNIT = 31
MASKV = -30000.0
_LASTP = []
_DBG = {"stop": 0}


class _Op:
    __slots__ = ("idx", "eng", "fn", "dma", "deps", "signal", "val", "grp")

    def __init__(self, idx, eng, fn, dma, grp):
        self.idx = idx
        self.eng = eng
        self.fn = fn
        self.dma = dma
        self.deps = []
        self.signal = False
        self.val = 0
        self.grp = grp


class _Rec:
    def __init__(self):
        self.call = None

    def __getattr__(self, name):
        def f(*a, **k):
            assert self.call is None
            self.call = (name, a, k)
            return self
        return f


class Prog:
    ENGS = ("pe", "act", "dve", "pool", "sp")

    def __init__(self):
        self.ops = []
        self.last_w = {}
        self.readers = {}
        self.barrier_keys = set()
        self.batch_keys = {}
        self.grp_last = {"M": {}, "F": {}}

    def pending(self, res):
        out = []
        w = self.last_w.get(res)
        if w is not None:
            out.append(w)
        for rd in self.readers.get(res, {}).values():
            out.extend(rd if isinstance(rd, list) else [rd])
        return out

    def add(self, eng, fn, reads=(), writes=(), dma=None, grp=None, after=()):
        if fn is not None:
            rec = _Rec()
            fn(rec)
            fn = rec.call
        op = _Op(len(self.ops), eng, fn, dma, grp)
        hard = {}
        soft = {}
        for o in after:
            hard[o.idx] = o
        for r in reads:
            w = self.last_w.get(r)
            if w is not None:
                hard[w.idx] = w
            if r.startswith("ps") and dma is None:
                for e2, rd in self.readers.get(r, {}).items():
                    if e2 != eng and e2 != "dma":
                        hard[rd.idx] = rd
        for r in writes:
            w = self.last_w.get(r)
            if w is not None:
                hard[w.idx] = w
            for rd in self.readers.get(r, {}).values():
                for o in (rd if isinstance(rd, list) else [rd]):
                    soft[o.idx] = o
        if grp is not None:
            other = "F" if grp == "M" else "M"
            for o in self.grp_last[other].values():
                soft[o.idx] = o
            if dma is None:
                self.grp_last[grp][eng] = op
        deps = []
        for d in hard.values():
            if d.dma is None and op.dma is None and d.eng == eng and eng == "pe":
                continue
            deps.append(d)
        for d in soft.values():
            if d.idx in hard:
                continue
            if d.dma is None and d.eng == eng and op.dma is None and eng != "pool":
                continue
            deps.append(d)
        op.deps = deps
        for d in deps:
            d.signal = True
        for r in reads:
            rd = self.readers.setdefault(r, {})
            if dma is not None:
                rd.setdefault("dma", []).append(op)
            else:
                rd[eng] = op
        for r in writes:
            self.last_w[r] = op
            self.readers[r] = {}
        self.ops.append(op)
        return op

    def emit(self, nc, es):
        sems = {}
        for e in self.ENGS:
            sems[e] = es.enter_context(nc.semaphore("s_" + e))
        cnt = {e: 0 for e in self.ENGS}
        dcnt = {}
        for op in self.ops:
            if op.dma is not None:
                if op.dma not in sems:
                    sems[op.dma] = es.enter_context(nc.semaphore("d_" + op.dma))
                    dcnt[op.dma] = 0
                dcnt[op.dma] += 16
                op.val = dcnt[op.dma]
            elif op.signal:
                cnt[op.eng] += 1
                op.val = cnt[op.eng]
        final = dict(dcnt)
        block = es.enter_context(nc.Block())
        per_eng = {e: [o for o in self.ops if o.eng == e] for e in self.ENGS}

        def run(engine, ename):
            waited = {}
            for op in per_eng[ename]:
                need = {}
                for d in op.deps:
                    if d.dma is not None:
                        k = d.dma
                        v = final[k] if k in self.barrier_keys else d.val
                        if k in self.batch_keys:
                            bsz = self.batch_keys[k]
                            v = (v + bsz - 1) // bsz * bsz
                    else:
                        k = d.eng
                        v = d.val
                    if v > need.get(k, 0):
                        need[k] = v
                for k, v in need.items():
                    if v > waited.get(k, 0):
                        engine.wait_ge(sems[k], v)
                        waited[k] = v
                if op.fn is None:
                    continue
                name, a, k = op.fn
                ins = getattr(engine, name)(*a, **k)
                if op.dma is not None:
                    ins.then_inc(sems[op.dma], 16)
                elif op.signal:
                    ins.then_inc(sems[ename], 1)

        @block.tensor
        def _(e):
            run(e, "pe")

        @block.scalar
        def _(e):
            run(e, "act")

        @block.vector
        def _(e):
            run(e, "dve")

        @block.gpsimd
        def _(e):
            run(e, "pool")

        @block.sync
        def _(e):
            run(e, "sp")


def build_program(NSEQ=4, SEQ=2048):
    NQB = SEQ // TB
    NBLK = NSEQ * NQB
    NKT = SEQ // 128
    TOK = NSEQ * SEQ
    nc = bass.Bass("TRN2", target_bir_lowering=False)

    def din(name, shape):
        return nc.dram_tensor(name, shape, F32, kind="ExternalInput").ap()

    x_d = din("x", [TOK, D])
    norm_mix_g = din("norm_mix_g", [D])
    w_in = din("w_in", [D, INW])
    b_f = din("b_f", [1, H])
    b_gate = din("b_gate", [2 * D])
    conv_mix_w = din("conv_mix_w", [3, 512])
    w_out_conv = din("w_out_conv", [512, D])
    w_out_attn = din("w_out_attn", [512, D])
    w_o = din("w_o", [D, D])
    norm_ffn_g = din("norm_ffn_g", [D])
    w_up = din("w_up", [D, 2 * FF])
    conv_ffn_w = din("conv_ffn_w", [3, 2 * FF])
    w_down = din("w_down", [FF, D])
    norm_f_g = din("norm_f_g", [1, D])
    out_d = nc.dram_tensor("out", [TOK, D], F32, kind="ExternalOutput").ap()
    ws = nc.dram_tensor("ws", [NUNIT, 128, 4096], BF16, kind="Internal").ap()

    es = ExitStack()
    with es:
        def sb(name, shape, dt):
            return es.enter_context(nc.sbuf_tensor(name, shape, dt))

        ident = sb("ident", [128, 128], BF16)
        trineg = sb("trineg", [128, 128], F32)
        negones = sb("negones", [128, 128], F32)
        maskw = sb("maskw", [128, 896], BF16)
        gmix = sb("gmix", [128, NKC], F32)
        gffn = sb("gffn", [128, NKC], F32)
        gfbc = sb("gfbc", [128, D], F32)
        cmw = sb("cmw", [128, 4, 3], F32)
        cfw = sb("cfw", [128, 44, 3], F32)
        bgate = sb("bgate", [128, 16], F32)
        bf4 = sb("bf4", [128, 4, H], F32)
        wf = sb("wf", [128, NKC, H], BF16)
        ss = sb("ss", [128, 4], F32)
        epsT = sb("epsT", [128, 1], F32)
        ss1 = sb("ss1", [128, 4], F32)
        lnv1 = sb("lnv1", [128, 4], F32)
        rstd1 = sb("rstd1", [128, 4], F32)
        sqj = sb("sqj", [128, D], BF16)
        identf = sb("identf", [128, 64], F32)
        lnv = sb("lnv", [128, 4], F32)
        rstd = sb("rstd", [128, 4], F32)
        PH = sb("PH", [128, 4, 2], F32)
        HS = sb("HS", [128, 44, 2], F32)
        carry = sb("carry", [128, 2], F32)
        x_tm = [sb("x_tm%d" % i, [128, 4, D], F32) for i in range(2)]
        xs = [sb("xs%d" % i, [128, D], BF16) for i in range(2)]
        hT = sb("hT", [128, NKC, TB], BF16)
        KA = sb("KA", [128, H, SEQ], BF16)
        VA = sb("VA", [128, NKT, 4, 192], BF16)
        wslot = [sb("wslot%d" % i, [128, 8, 512], BF16) for i in range(NSLOT)]
        ps_all = es.enter_context(nc.psum_tensor("ps_all", [128, 8, 512], F32))

        OVL_BYTES = 66048
        ovl = sb("ovl", [128, OVL_BYTES // 2], BF16)

        class Carver:
            def __init__(self):
                self.off = 0

            def get(self, free_shape, dt):
                n = int(np.prod(free_shape))
                nb = n * (4 if dt == F32 else 2)
                nb_al = (nb + 63) // 64 * 64
                assert self.off + nb_al <= OVL_BYTES, ("overlay overflow", self.off, nb_al)
                a = ovl[:, self.off // 2:(self.off + nb) // 2]
                self.off += nb_al
                if dt == F32:
                    a = a.bitcast(F32)
                if len(free_shape) == 2:
                    a = a.rearrange("p (a b) -> p a b", a=free_shape[0])
                elif len(free_shape) == 3:
                    a = a.rearrange("p (a b c) -> p a b c", a=free_shape[0], b=free_shape[1])
                return a

        cm = Carver()
        cu = cm.get([4, TB], F32)
        Pb = cm.get([4, TB + 2], F32)
        zT = cm.get([4, TB], BF16)
        QA = cm.get([H, TB], BF16)
        zf = cm.get([4, H], F32)
        ef = cm.get([4, H], F32)
        spf = cm.get([4, H], F32)
        Fq = cm.get([3, TB], BF16)
        Fk = cm.get([3, TB], BF16)
        PT = [cm.get([2, TB], BF16) for _ in range(3)]
        rl = [cm.get([TB], F32) for _ in range(2)]
        Fblk, r1 = rl[0], rl[1]
        oT = cm.get([4, TB], BF16)
        sgc = [cm.get([TB], BF16) for _ in range(2)]
        sga = [cm.get([TB], BF16) for _ in range(2)]
        t1 = [cm.get([TB], F32) for _ in range(2)]
        mT = cm.get([NKC, TB], BF16)
        mixer_bytes = cm.off
        cf = Carver()
        NR = 8
        A_sb = [cf.get([TB + 2], F32) for _ in range(NR)]
        acc = [cf.get([TB], F32) for _ in range(NR)]
        sa = [cf.get([TB], F32) for _ in range(2)]
        G = cf.get([NFC, TB], BF16)

        P = Prog()
        P.barrier_keys = {"constA", "constB"}
        P.batch_keys = {"fs": 96}

        def psn(b):
            return "ps%d" % b

        cres = {"A": [], "B": []}
        cdefer = []

        def cdma(out_ap, in_ap, g="B"):
            def go():
                cres[g].append("cdma%s%d" % (g, len(cres[g])))
                P.add("pool", lambda e, o=out_ap, i=in_ap: e.dma_start(out=o, in_=i, allow_slow_non_contiguous=True),
                      writes=[cres[g][-1]], dma="const" + g)
            if g == "A":
                go()
            else:
                cdefer.append(go)

        ovl32 = ovl[:, :].bitcast(F32)
        stgF = ovl32[:, 0:384].rearrange("p (k n) -> p k n", k=3)
        stgM = ovl32[:, 384:512]
        stgG = ovl32[:, 512:640]
        stgG2 = ovl32[:, 640:768]
        stgBg = ovl32[:, 768:896]
        for k in range(3):
            cdma(stgF[0:44, k, :], conv_ffn_w[k, :].rearrange("(c p) -> c p", p=128), "A")
        cdma(stgM[0:12, :], conv_mix_w.rearrange("k (c p) -> (k c) p", p=128), "A")
        cdma(stgG[0:8, :], norm_mix_g.rearrange("(c p) -> c p", p=128), "A")
        cdma(stgG2[0:8, :], norm_ffn_g.rearrange("(c p) -> c p", p=128), "A")
        cdma(stgBg[0:16, :], b_gate.rearrange("(c p) -> c p", p=128), "A")
        cdma(gfbc[:, :], norm_f_g[0:1, :].broadcast_to([128, D]))
        for i in range(4):
            cdma(bf4[:, i, :], b_f[0:1, :].broadcast_to([128, H]), "A")
        cdma(wf[:, :, :], w_in[:, 3072:3080].rearrange("(k p) n -> p k n", p=128), "A")

        def wsv(u):
            return ws[u].rearrange("p (k n) -> p k n", k=8)

        ws_parts = {}
        cvq = []

        def conv(u, k0, k1, n0, n1, src, grp):
            ws_parts.setdefault(u, []).append((k0, k1, n0, n1, src))

        def kin(w, r0, r1_, c0, c1):
            return w[r0:r1_, c0:c1].rearrange("(k p) n -> p k n", p=128)

        U_CIN, U_CC, U_CB, U_Q, U_K, U_V = 0, 1, 2, 3, 4, 5
        U_G = [6, 8, 9, 11]
        U_M = [7, 10]
        U_WO = [12, 13]
        U_UP = list(range(14, 25))
        U_DN = list(range(25, 31))
        conv(U_CIN, 0, 8, 0, 512, kin(w_in, 0, D, 1024, 1536), 0)
        conv(U_CC, 0, 8, 0, 512, kin(w_in, 0, D, 512, 1024), 0)
        conv(U_CB, 0, 8, 0, 512, kin(w_in, 0, D, 0, 512), 0)
        conv(U_Q, 0, 8, 0, 512, kin(w_in, 0, D, 1536, 2048), 0)
        conv(U_K, 0, 8, 0, 512, kin(w_in, 0, D, 2048, 2560), 0)
        conv(U_V, 0, 8, 0, 512, kin(w_in, 0, D, 2560, 3072), 0)
        for n in range(2):
            conv(U_M[n], 0, 4, 0, 512, kin(w_out_conv, 0, 512, n * 512, (n + 1) * 512), 1)
            conv(U_M[n], 4, 8, 0, 512, kin(w_out_attn, 0, 512, n * 512, (n + 1) * 512), 1)
        for q in range(4):
            conv(U_G[q], 0, 8, 0, 256, kin(w_in, 0, D, 3080 + q * 256, 3080 + (q + 1) * 256), 1)
            conv(U_G[q], 0, 8, 256, 512, kin(w_in, 0, D, 4104 + q * 256, 4104 + (q + 1) * 256), 1)
        for n in range(2):
            conv(U_WO[n], 0, 8, 0, 512, kin(w_o, 0, D, n * 512, (n + 1) * 512), 1)
        for u in range(11):
            conv(U_UP[u], 0, 8, 0, 256, kin(w_up, 0, D, 256 * u, 256 * (u + 1)), 2)
            conv(U_UP[u], 0, 8, 256, 512, kin(w_up, 0, D, FF + 256 * u, FF + 256 * (u + 1)), 2)
        DN_K = [(0, 8), (8, 16), (16, 22)]
        for n in range(2):
            for gi, (ka, kb) in enumerate(DN_K):
                conv(U_DN[n * 3 + gi], 0, kb - ka, 0, 512, kin(w_down, ka * 128, kb * 128, n * 512, (n + 1) * 512), 3)

        def pool_c(fn):
            P.add("pool", fn, writes=["cA"])

        pool_c(lambda e: e.memset(ident[:, :], 0.0))
        pool_c(lambda e: e.affine_select(out=ident[:, :], in_=ident[:, :], pattern=[[-1, 128]],
                                         compare_op=ALU.not_equal, fill=1.0, base=0, channel_multiplier=1))
        pool_c(lambda e: e.memset(trineg[:, :], -1.0))
        pool_c(lambda e: e.affine_select(out=trineg[:, :], in_=trineg[:, :], pattern=[[1, 128]],
                                         compare_op=ALU.is_ge, fill=0.0, base=0, channel_multiplier=-1))
        pool_c(lambda e: e.memset(negones[:, :], -1.0))
        pool_c(lambda e: e.memset(maskw[:, :], 0.0))
        pool_c(lambda e: e.affine_select(out=maskw[:, :], in_=maskw[:, :], pattern=[[1, 896]],
                                         compare_op=ALU.is_ge, fill=MASKV, base=-384, channel_multiplier=-1))
        P.add("pool", lambda e: e.memset(ss[:, :], 0.0), writes=["ss"])
        P.add("pool", lambda e: e.memset(ss1[:, :], 0.0), writes=["ss1"])
        pool_c(lambda e: e.memset(epsT[:, :], EPS))
        pool_c(lambda e: e.memset(identf[:, :], 0.0))
        pool_c(lambda e: e.affine_select(out=identf[:, :], in_=identf[:, :], pattern=[[-1, 64]],
                                         compare_op=ALU.not_equal, fill=1.0, base=0, channel_multiplier=1))
        P.add("pool", lambda e: e.memset(lnv1[:, :], 0.0), reads=list(cres["A"]), writes=["cAd", "lnv1"])
        for go in cdefer:
            go()
        cbk = 7
        def cmm(c0, c1, lhsT, K):
            P.add("pe", lambda e: e.matmul(ps_all[:, cbk, c0:c1], lhsT, identf[0:K, 0:K], start=True, stop=True),
                  reads=["cA", "cAd"], writes=[psn(cbk)], grp="F")

        for k in range(3):
            cmm(k * 44, (k + 1) * 44, stgF[0:44, k, :], 44)
        cmm(132, 144, stgM[0:12, :], 12)
        cmm(144, 152, stgG[0:8, :], 8)
        cmm(152, 160, stgG2[0:8, :], 8)
        cmm(160, 176, stgBg[0:16, :], 16)
        P.add("dve", lambda e: e.tensor_copy(out=gmix[:, :], in_=ps_all[:, cbk, 144:152]), reads=[psn(cbk)], writes=["cG"])
        P.add("dve", lambda e: e.tensor_copy(out=gffn[:, :], in_=ps_all[:, cbk, 152:160]), reads=[psn(cbk)], writes=["cB0"])
        P.add("dve", lambda e: e.tensor_copy(out=bgate[:, :], in_=ps_all[:, cbk, 160:176]), reads=[psn(cbk)], writes=["cB1"])
        P.add("dve", lambda e: e.tensor_copy(out=cmw[:, :, :],
                                             in_=ps_all[:, cbk, 132:144].rearrange("p (k c) -> p c k", k=3)),
              reads=[psn(cbk)], writes=["cB2"])
        P.add("dve", lambda e: e.tensor_copy(out=cfw[:, :, :],
                                             in_=ps_all[:, cbk, 0:132].rearrange("p (k c) -> p c k", k=3)),
              reads=[psn(cbk), "cB0", "cB1", "cB2"] + list(cres["B"]), writes=["cB"])

        block_units = [U_CIN, U_CC, U_CB, U_Q, U_K, U_V,
                       U_G[0], U_M[0], U_G[1], U_G[2], U_M[1], U_G[3],
                       U_WO[0], U_WO[1]] + U_UP + [U_DN[0], U_DN[1], U_DN[3], U_DN[2], U_DN[4], U_DN[5]]
        stream = []
        for b in range(NBLK):
            stream += block_units
        st = {"issued": 0, "next_use": 0}

        def issue_load(k):
            if k >= len(stream):
                return
            assert k == st["issued"]
            u = stream[k]
            s = k % NSLOT
            nk = 6 if u in (U_DN[2], U_DN[5]) else 8
            if k < len(block_units):
                R = "wslot%d" % s
                if u in U_UP or u in U_G:
                    prev = P.pending(R)
                    for pi, (k0, k1, n0, n1, src) in enumerate(ws_parts[u]):
                        P.add("pool", lambda e, o=wslot[s][:, k0:k1, n0:n1], i=src: e.dma_start(out=o, in_=i),
                              writes=[R + "ab"[pi]], dma="wp%d%s" % (s, "ab"[pi]), after=prev)
                else:
                    for (k0, k1, n0, n1, src) in ws_parts[u]:
                        P.add("pool", lambda e, o=wslot[s][:, k0:k1, n0:n1], i=src: e.dma_start(out=o, in_=i),
                              writes=[R, R + "a", R + "b"], dma="wp%d" % s)
                P.add("sp", lambda e, o=wsv(u)[:, 0:nk, :], i=wslot[s][:, 0:nk, :]: e.dma_start(out=o, in_=i),
                      reads=[R, R + "a", R + "b"], writes=["ws%d" % u], dma="wst%d" % s)
            else:
                P.add("sp", lambda e, o=wslot[s][:, 0:nk, :], i=wsv(u)[:, 0:nk, :]: e.dma_start(out=o, in_=i),
                      reads=["ws%d" % u], writes=["wslot%d" % s, "wslot%da" % s, "wslot%db" % s], dma="w%d" % s)
            st["issued"] += 1

        P.add("sp", lambda e: e.dma_start(out=x_tm[0][:, :, :],
                                          in_=x_d[0:TB, :].rearrange("(i p) d -> p i d", p=128)),
              writes=["x0.%d" % i for i in range(4)], dma="xl0")
        for k0 in range(NSLOT):
            issue_load(k0)
        P.add("pool", lambda e: e.memset(VA[:, :, :, 64:128], 1.0), writes=["VA"])
        P.add("pool", lambda e: e.memset(KA[64:70, :, :], 1.0), writes=["KAaug.0", "KAaug.1", "KAaug.2"])

        def take_unit(u_expected):
            k = st["next_use"]
            assert stream[k] == u_expected, (k, stream[k], u_expected)
            assert k < st["issued"]
            st["next_use"] += 1
            s = k % NSLOT
            return wslot[s], "wslot%d" % s

        rel_state = {"k": 0}

        def rel():
            k = rel_state["k"]
            rel_state["k"] += 1
            issue_load(k + NSLOT)

        bank_rr = {"i": 0}

        def next_bank():
            b = bank_rr["i"] % 8
            bank_rr["i"] += 1
            return b

        STA = {"ss": ss, "lnv": lnv, "rstd": rstd, "n": ("ss", "lnv", "rstd")}
        STB = {"ss": ss1, "lnv": lnv1, "rstd": rstd1, "n": ("ss1", "lnv1", "rstd1")}

        def norm_stats(xb, S):
            X = x_tm[xb]
            rs, rl_, rr = S["n"]
            for i in range(4):
                P.add("act", lambda e, i=i: e.activation(out=sqj[:, :], in_=X[:, i, :], func=AF.Square,
                                                         accum_out=S["ss"][:, i:i + 1]),
                      reads=["x%d.%d" % (xb, i)], writes=["sqj", rs])
            P.add("act", lambda e: e.activation(out=S["lnv"][:, :], in_=S["ss"][:, :], func=AF.Ln, scale=1.0 / D,
                                                bias=epsT[:, 0:1]),
                  reads=[rs, "cA"], writes=[rl_])
            P.add("act", lambda e: e.activation(out=S["rstd"][:, :], in_=S["lnv"][:, :], func=AF.Exp, scale=-0.5),
                  reads=[rl_], writes=[rr])
            P.add("dve", lambda e: e.memset(S["ss"][:, :], 0.0), reads=[rl_], writes=[rs])

        def norm_scale(xb, i, S):
            X = x_tm[xb]
            if i % 2 == 0:
                P.add("dve", lambda e: e.tensor_scalar(out=xs[i % 2][:, :], in0=X[:, i, :],
                                                       scalar1=S["rstd"][:, i:i + 1], scalar2=None, op0=ALU.mult),
                      reads=["x%d.%d" % (xb, i), S["n"][2]], writes=["xs%d" % (i % 2)])
            else:
                P.add("act", lambda e: e.activation(out=xs[i % 2][:, :], in_=X[:, i, :], func=AF.Copy,
                                                    scale=S["rstd"][:, i:i + 1]),
                      reads=["x%d.%d" % (xb, i), S["n"][2]], writes=["xs%d" % (i % 2)])

        def norm_transp(i, tb):
            for c in range(NKC):
                bk = tb[c // 2]
                o = ps_all[:, bk, :].bitcast(BF16)[:, (c % 2) * 512 + i * 128:(c % 2) * 512 + (i + 1) * 128]
                P.add("pe", lambda e, o=o, c=c: e.transpose(o, xs[i % 2][:, c * 128:(c + 1) * 128], ident[:, :]),
                      reads=["xs%d" % (i % 2), "cA"], writes=[psn(bk)])

        def norm_evac(tb, gvec, gres):
            for c in range(NKC):
                bk = tb[c // 2]
                src = ps_all[:, bk, :].bitcast(BF16)[:, (c % 2) * 512:(c % 2 + 1) * 512]
                if (c // 2) % 2 == 0:
                    P.add("act", lambda e, c=c, src=src: e.activation(out=hT[:, c, :], in_=src, func=AF.Copy,
                                                                      scale=gvec[:, c:c + 1]),
                          reads=[psn(bk), gres], writes=["hT.%d" % c])
                else:
                    P.add("dve", lambda e, c=c, src=src: e.tensor_scalar(out=hT[:, c, :], in0=src,
                                                                         scalar1=gvec[:, c:c + 1], scalar2=None,
                                                                         op0=ALU.mult),
                          reads=[psn(bk), gres], writes=["hT.%d" % c])

        def rmsnorm_to_hT(xb, gvec, tag, grp):
            gres = "cG" if tag == "n1" else "cB"
            S = STB if tag == "n1" else STA
            norm_stats(xb, S)
            tb = [next_bank() for _ in range(4)]
            for i in range(4):
                norm_scale(xb, i, S)
                norm_transp(i, tb)
            norm_evac(tb, gvec, gres)

        hT_all = ["hT.%d" % c for c in range(NKC)]

        def proj_fm(wt, wname, col0, bk, grp):
            for kc in range(NKC):
                P.add("pe", lambda e, kc=kc: e.matmul(ps_all[:, bk, :], wt[:, kc, col0:col0 + 128], hT[:, kc, :],
                                                      start=(kc == 0), stop=(kc == NKC - 1)),
                      reads=[wname, "hT.%d" % kc], writes=[psn(bk)], grp=grp)

        pending_store = []
        for blk in range(0 if _DBG["stop"] == 1 else NBLK):
            seq = blk // NQB
            qb = blk % NQB
            xb = blk % 2
            t0 = qb * TB
            row0 = blk * TB
            nkt = 4 * (qb + 1)
            X = x_tm[xb]
            xr = ["x%d.%d" % (xb, i) for i in range(4)]

            if blk == 0:
                rmsnorm_to_hT(xb, gmix, "n1", "M")

            if _DBG["stop"] in (2, 20, 21, 22, 23, 24):
                break
            fb = next_bank()
            for i in range(4):
                for kc in range(NKC):
                    P.add("pe", lambda e, i=i, kc=kc: e.matmul(ps_all[:, fb, i * H:(i + 1) * H],
                                                               hT[:, kc, i * 128:(i + 1) * 128], wf[:, kc, :],
                                                               start=(kc == 0), stop=(kc == NKC - 1)),
                          reads=["hT.%d" % kc, "cAd"], writes=[psn(fb)], grp="M")
            P.add("dve", lambda e: e.tensor_tensor(out=zf[:, :, :],
                                                   in0=ps_all[:, fb, 0:4 * H].rearrange("p (i h) -> p i h", i=4),
                                                   in1=bf4[:, :, :], op=ALU.add),
                  reads=[psn(fb), "cAd"], writes=["zf"], grp="M")
            P.add("act", lambda e: e.activation(out=ef[:, :, :], in_=zf[:, :, :], func=AF.Exp, scale=-1.0),
                  reads=["zf"], writes=["ef"], grp="M")
            P.add("act", lambda e: e.activation(out=spf[:, :, :], in_=ef[:, :, :], func=AF.Ln, bias=1.0),
                  reads=["ef"], writes=["spf"], grp="M")
            wt, wn = take_unit(U_CIN)
            for c in range(4):
                bk = next_bank()
                proj_fm(wt, wn, c * 128, bk, "M")
                P.add("act", lambda e, c=c, bk=bk: e.activation(out=cu[:, c, :], in_=ps_all[:, bk, :], func=AF.Copy),
                      reads=[psn(bk)], writes=["cu.%d" % c], grp="M")
            rel()
            if qb == 0:
                P.add("pool", lambda e: e.memset(Pb[:, :, 0:2], 0.0), writes=["P.halo"], grp="M")
            else:
                P.add("pool", lambda e: e.tensor_copy(out=Pb[:, :, 0:2], in_=PH[:, :, :]), reads=["PH"],
                      writes=["P.halo"], grp="M")
            wt, wn = take_unit(U_CC)
            for c in range(4):
                bk = next_bank()
                proj_fm(wt, wn, c * 128, bk, "M")
                P.add("dve", lambda e, c=c, bk=bk: e.tensor_tensor(out=Pb[:, c, 2:TB + 2], in0=ps_all[:, bk, :],
                                                                   in1=cu[:, c, :], op=ALU.mult),
                      reads=[psn(bk), "cu.%d" % c], writes=["P.%d" % c], grp="M")
                P.add("act", lambda e, c=c: e.activation(out=cu[:, c, :], in_=Pb[:, c, 2:TB + 2], func=AF.Copy,
                                                         scale=cmw[:, c, 2:3]),
                      reads=["P.%d" % c, "cB"], writes=["cu.%d" % c], grp="M")
                P.add("dve", lambda e, c=c: e.scalar_tensor_tensor(out=cu[:, c, :], in0=Pb[:, c, 1:TB + 1],
                                                                    scalar=cmw[:, c, 1:2], in1=cu[:, c, :],
                                                                    op0=ALU.mult, op1=ALU.add),
                      reads=["P.%d" % c, "P.halo", "cB", "cu.%d" % c], writes=["cu.%d" % c], grp="M")
                P.add("dve", lambda e, c=c: e.scalar_tensor_tensor(out=cu[:, c, :], in0=Pb[:, c, 0:TB],
                                                                    scalar=cmw[:, c, 0:1], in1=cu[:, c, :],
                                                                    op0=ALU.mult, op1=ALU.add),
                      reads=["P.%d" % c, "P.halo", "cB", "cu.%d" % c], writes=["cu.%d" % c], grp="M")
            rel()
            P.add("pool", lambda e: e.tensor_copy(out=PH[:, :, :], in_=Pb[:, :, TB:TB + 2]),
                  reads=["P.%d" % c for c in range(4)], writes=["PH"], grp="M")
            wt, wn = take_unit(U_CB)
            for c in range(4):
                bk = next_bank()
                proj_fm(wt, wn, c * 128, bk, "M")
                P.add("dve", lambda e, c=c, bk=bk: e.tensor_tensor(out=zT[:, c, :], in0=ps_all[:, bk, :],
                                                                   in1=cu[:, c, :], op=ALU.mult),
                      reads=[psn(bk), "cu.%d" % c], writes=["zT.%d" % c], grp="M")

            rel()
            if _DBG["stop"] == 3:
                break
            fb2 = next_bank()
            for i in range(4):
                P.add("pe", lambda e, i=i: e.matmul(ps_all[0:H, fb2, i * 128:(i + 1) * 128], spf[:, i, :],
                                                    trineg[:, :], start=True, stop=(i == 0)),
                      reads=["spf", "cA"], writes=[psn(fb2)], grp="M")
                for i2 in range(i):
                    P.add("pe", lambda e, i=i, i2=i2: e.matmul(ps_all[0:H, fb2, i * 128:(i + 1) * 128],
                                                               spf[:, i2, :], negones[:, :], start=False,
                                                               stop=(i2 == i - 1)),
                          reads=["spf", "cA"], writes=[psn(fb2)], grp="M")
            cb_ = blk % 2
            if qb == 0:
                P.add("dve", lambda e, cb_=cb_: e.memset(carry[0:H, cb_:cb_ + 1], 0.0), writes=["carry%d" % cb_])
            P.add("dve", lambda e, cb_=cb_: e.tensor_scalar(out=Fblk[0:H, :], in0=ps_all[0:H, fb2, :],
                                                            scalar1=carry[0:H, cb_:cb_ + 1], scalar2=None,
                                                            op0=ALU.add),
                  reads=[psn(fb2), "carry%d" % cb_], writes=["rl0"], grp="M")
            f_ops = []
            f_ops.append(lambda: P.add("dve", lambda e, cb_=cb_: e.tensor_copy(out=carry[0:H, 1 - cb_:2 - cb_], in_=Fblk[0:H, TB - 1:TB]),
                      reads=["rl0"], writes=["carry%d" % (1 - cb_)], grp="M"))
            f_ops.append(lambda: P.add("dve", lambda e: e.tensor_copy(out=Fq[0:H, 0, :], in_=Fblk[0:H, :]),
                      reads=["rl0"], writes=["Fq0"], grp="M"))
            f_ops.append(lambda: P.add("dve", lambda e: e.tensor_tensor(out=r1[0:H, :], in0=Fblk[0:H, :], in1=Fq[0:H, 0, :],
                                                        op=ALU.subtract),
                      reads=["rl0", "Fq0"], writes=["rl1"], grp="M"))
            f_ops.append(lambda: P.add("dve", lambda e: e.tensor_copy(out=Fq[0:H, 1, :], in_=r1[0:H, :]),
                      reads=["rl1"], writes=["Fq1"], grp="M"))
            f_ops.append(lambda: P.add("dve", lambda e: e.tensor_tensor(out=Fblk[0:H, :], in0=r1[0:H, :], in1=Fq[0:H, 1, :],
                                                        op=ALU.subtract),
                      reads=["rl1", "Fq1"], writes=["rl0"], grp="M"))
            f_ops.append(lambda: P.add("dve", lambda e: e.tensor_copy(out=Fq[0:H, 2, :], in_=Fblk[0:H, :]),
                      reads=["rl0"], writes=["Fq2"], grp="M"))
            f_ops.append(lambda: P.add("dve", lambda e: e.tensor_scalar(out=Fk[0:H, :, :], in0=Fq[0:H, :, :], scalar1=-1.0,
                                                        scalar2=None, op0=ALU.mult),
                      reads=["Fq0", "Fq1", "Fq2"], writes=["Fk"], grp="M"))

            if _DBG["stop"] == 4:
                break
            wt, wn = take_unit(U_Q)
            for c in range(4):
                bk = next_bank()
                proj_fm(wt, wn, c * 128, bk, "M")
                P.add("act", lambda e, c=c, bk=bk: e.activation(out=QA[0:64, 2 * c, :], in_=ps_all[0:64, bk, :],
                                                                func=AF.Copy, scale=0.125),
                      reads=[psn(bk)], writes=["QAq.%d" % (2 * c)], grp="M")
                P.add("dve", lambda e, c=c, bk=bk: e.tensor_scalar(out=QA[0:64, 2 * c + 1, :],
                                                                   in0=ps_all[64:128, bk, :], scalar1=0.125,
                                                                   scalar2=None, op0=ALU.mult),
                      reads=[psn(bk)], writes=["QAq.%d" % (2 * c + 1)], grp="M")
                if f_ops:
                    f_ops.pop(0)()
            rel()
            wt, wn = take_unit(U_K)
            for c in range(4):
                bk = next_bank()
                proj_fm(wt, wn, c * 128, bk, "M")
                P.add("act", lambda e, c=c, bk=bk: e.activation(out=KA[0:64, 2 * c, t0:t0 + TB],
                                                                in_=ps_all[0:64, bk, :], func=AF.Copy),
                      reads=[psn(bk)], writes=["KAk.%d" % (2 * c)], grp="M")
                P.add("dve", lambda e, c=c, bk=bk: e.tensor_copy(out=KA[0:64, 2 * c + 1, t0:t0 + TB],
                                                                 in_=ps_all[64:128, bk, :]),
                      reads=[psn(bk)], writes=["KAk.%d" % (2 * c + 1)], grp="M")
                if f_ops:
                    f_ops.pop(0)()
            rel()
            while f_ops:
                f_ops.pop(0)()
            P.add("pool", lambda e: e.memset(QA[64:70, :, :], 1.0), writes=["QAaug.0", "QAaug.1", "QAaug.2"], grp="M")
            for r in range(3):
                P.add("pool", lambda e, r=r: e.dma_start(out=QA[64 + r:65 + r, :, :], in_=Fq[0:H, r, :]),
                      reads=["Fq%d" % r], writes=["QAaug.%d" % r], dma="fs", grp="M")
                P.add("pool", lambda e, r=r: e.dma_start(out=KA[67 + r:68 + r, :, t0:t0 + TB], in_=Fk[0:H, r, :]),
                      reads=["Fk"], writes=["KAaug.%d" % r], dma="fs", grp="M")

            wt, wn = take_unit(U_V)
            for i in range(4):
                bk = next_bank()
                kt = 4 * qb + i
                for kc in range(NKC):
                    P.add("pe", lambda e, i=i, kc=kc, bk=bk: e.matmul(ps_all[:, bk, :],
                                                                      hT[:, kc, i * 128:(i + 1) * 128],
                                                                      wt[:, kc, :], start=(kc == 0),
                                                                      stop=(kc == NKC - 1)),
                          reads=[wn, "hT.%d" % kc], writes=[psn(bk)], grp="M")
                pv_ = ps_all[:, bk, :].rearrange("p (j e d) -> p j e d", j=4, e=2)
                P.add("act", lambda e, kt=kt, pv_=pv_: e.activation(out=VA[:, kt, :, 0:64], in_=pv_[:, :, 0, :],
                                                                    func=AF.Copy),
                      reads=[psn(bk)], writes=["VA"], grp="M")
                P.add("dve", lambda e, kt=kt, pv_=pv_: e.tensor_copy(out=VA[:, kt, :, 128:192], in_=pv_[:, :, 1, :]),
                      reads=[psn(bk)], writes=["VA"], grp="M")

            rel()
            if pending_store:
                pb_row0, pb_xb = pending_store.pop()
                P.add("sp", lambda e, r=pb_row0, b_=pb_xb: e.dma_start(
                    out=out_d[r:r + TB, :].rearrange("(i p) d -> p i d", p=128), in_=x_tm[b_][:, :, :]),
                    reads=["x%d.%d" % (pb_xb, i) for i in range(4)], writes=["outstore%d" % pb_xb], dma="st%d" % pb_xb)
            if blk + 1 < NBLK:
                nb_ = (blk + 1) % 2
                P.add("sp", lambda e, r=row0 + TB, b_=nb_: e.dma_start(
                    out=x_tm[b_][:, :, :], in_=x_d[r:r + TB, :].rearrange("(i p) d -> p i d", p=128)),
                    reads=["outstore%d" % nb_], writes=["x%d.%d" % (nb_, i) for i in range(4)], dma="xl%d" % nb_)
            if _DBG["stop"] == 5:
                break
            ng = nkt // 2
            items = [(h, g) for h in range(H) for g in range(ng)]
            LAG = 2

            def grp_c0(g):
                return 256 if 2 * g == 4 * qb + 2 else 0

            def emit_qk(idx, h, g):
                sg_ = idx % 3
                qk_reads = (["KAk.%d" % h, "QAq.%d" % h] + ["KAaug.%d" % r_ for r_ in range(3)]
                            + ["QAaug.%d" % r_ for r_ in range(3)])
                c0 = grp_c0(g)
                for sl in range(2):
                    kt = 2 * g + sl
                    bk = 2 * sg_ + sl
                    diag = kt >= 4 * qb
                    kap = KA[0:70, h, kt * 128:(kt + 1) * 128]
                    P.add("pe", lambda e, bk=bk, kap=kap, diag=diag: e.matmul(
                        ps_all[:, bk, c0:TB], kap, QA[0:70, h, c0:TB], start=True, stop=(not diag)),
                        reads=qk_reads, writes=[psn(bk)], grp="M")
                    if diag:
                        j = kt - 4 * qb
                        c1 = 128 * (j + 1)
                        m0 = 384 - 128 * j + c0
                        P.add("pe", lambda e, bk=bk, c1=c1, m0=m0: e.matmul(
                            ps_all[:, bk, c0:c1], ident[:, :], maskw[:, m0:m0 + (c1 - c0)], start=False, stop=True),
                            reads=["cA"], writes=[psn(bk)], grp="M")
                P.add("act", lambda e, sg_=sg_: e.activation(out=PT[sg_][:, :, c0:TB],
                                                             in_=ps_all[:, 2 * sg_:2 * sg_ + 2, c0:TB], func=AF.Exp),
                      reads=[psn(2 * sg_), psn(2 * sg_ + 1)], writes=["PT%d" % sg_], grp="M")

            def emit_pv(idx, h, g):
                sg_ = idx % 3
                pvb = 6 + h % 2
                j = h // 2
                eo = h % 2
                for sl in range(2):
                    kt = 2 * g + sl
                    c0 = grp_c0(g)
                    P.add("pe", lambda e, kt=kt, sl=sl, j=j, eo=eo, pvb=pvb, sg_=sg_, c0=c0: e.matmul(
                        ps_all[:, pvb, c0:TB], VA[:, kt, j, eo * 64:eo * 64 + 128], PT[sg_][:, sl, c0:TB],
                        start=(kt == 0), stop=(kt == nkt - 1)),
                        reads=["VA", "PT%d" % sg_], writes=[psn(pvb)], grp="M")
                if g == ng - 1:
                    oP = slice(0, 64) if eo == 0 else slice(64, 128)
                    lP = slice(64, 128) if eo == 0 else slice(0, 64)
                    k = h % 2
                    if qb == 0:
                        P.add("act", lambda e, lP=lP, k=k, pvb=pvb: e.activation(out=t1[k][lP, :], in_=ps_all[lP, pvb, :],
                                                                                 func=AF.Ln),
                              reads=[psn(pvb)], writes=["t1%d" % k], grp="M")
                        P.add("act", lambda e, lP=lP, k=k: e.activation(out=t1[k][lP, :], in_=t1[k][lP, :], func=AF.Exp,
                                                                        scale=-1.0),
                              reads=["t1%d" % k], writes=["t1%d" % k], grp="M")
                        P.add("dve", lambda e, oP=oP, lP=lP, k=k: e.tensor_copy(out=rl[k][oP, :], in_=t1[k][lP, :]),
                              reads=["t1%d" % k], writes=["rl%d" % k], grp="M")
                    else:
                        P.add("dve", lambda e, oP=oP, lP=lP, k=k, pvb=pvb: e.reciprocal(out=rl[k][oP, :],
                                                                                        in_=ps_all[lP, pvb, :]),
                              reads=[psn(pvb)], writes=["rl%d" % k], grp="M")
                    P.add("dve", lambda e, oP=oP, k=k, pvb=pvb, j=j: e.tensor_tensor(
                        out=oT[oP, j, :], in0=ps_all[oP, pvb, :], in1=rl[k][oP, :], op=ALU.mult),
                        reads=[psn(pvb), "rl%d" % k], writes=["oT.%d" % h], grp="M")

            for idx, (h, g) in enumerate(items):
                emit_qk(idx, h, g)
                if idx >= LAG:
                    emit_pv(idx - LAG, *items[idx - LAG])
            for idx in range(max(0, len(items) - LAG), len(items)):
                emit_pv(idx, *items[idx])
            bank_rr["i"] = 0

            if _DBG["stop"] == 6:
                break
            for n in range(2):
                for qq in range(2):
                    q = 2 * n + qq
                    wg, wgn = take_unit(U_G[q])
                    if qq == 0:
                        wm, wmn = take_unit(U_M[n])
                    for cc in range(2):
                        c = 2 * q + cc
                        r = c % 2
                        b_gc, b_ga = next_bank(), next_bank()
                        proj_fm(wg, wgn + "a", cc * 128, b_gc, "M")
                        P.add("act", lambda e, c=c, r=r, b_gc=b_gc: e.activation(
                            out=sgc[r][:, :], in_=ps_all[:, b_gc, :], func=AF.Sigmoid, bias=bgate[:, c:c + 1]),
                            reads=[psn(b_gc), "cB"], writes=["sgc%d" % r], grp="M")
                        proj_fm(wg, wgn + "b", 256 + cc * 128, b_ga, "M")
                        P.add("act", lambda e, c=c, r=r, b_ga=b_ga: e.activation(
                            out=sga[r][:, :], in_=ps_all[:, b_ga, :], func=AF.Sigmoid, bias=bgate[:, 8 + c:9 + c]),
                            reads=[psn(b_ga), "cB"], writes=["sga%d" % r], grp="M")
                    for cc in range(2):
                        c = 2 * q + cc
                        r = c % 2
                        cl = c - 4 * n
                        b_yc, b_ya = next_bank(), next_bank()
                        for kc in range(4):
                            P.add("pe", lambda e, kc=kc, cl=cl, b_yc=b_yc: e.matmul(
                                ps_all[:, b_yc, :], wm[:, kc, cl * 128:(cl + 1) * 128], zT[:, kc, :],
                                start=(kc == 0), stop=(kc == 3)),
                                reads=[wmn, "zT.%d" % kc], writes=[psn(b_yc)], grp="M")
                        for kc in range(4):
                            P.add("pe", lambda e, kc=kc, cl=cl, b_ya=b_ya: e.matmul(
                                ps_all[:, b_ya, :], wm[:, 4 + kc, cl * 128:(cl + 1) * 128], oT[:, kc, :],
                                start=(kc == 0), stop=(kc == 3)),
                                reads=[wmn, "oT.%d" % (2 * kc), "oT.%d" % (2 * kc + 1)], writes=[psn(b_ya)], grp="M")
                        P.add("dve", lambda e, r=r, b_yc=b_yc: e.tensor_tensor(out=t1[r][:, :], in0=ps_all[:, b_yc, :],
                                                                               in1=sgc[r][:, :], op=ALU.mult),
                              reads=[psn(b_yc), "sgc%d" % r], writes=["t1%d" % r], grp="M")
                        P.add("dve", lambda e, r=r, b_ya=b_ya: e.tensor_tensor(out=sga[r][:, :], in0=ps_all[:, b_ya, :],
                                                                               in1=sga[r][:, :], op=ALU.mult),
                              reads=[psn(b_ya), "sga%d" % r], writes=["sga%d" % r], grp="M")
                        P.add("pool", lambda e, r=r, c=c: e.tensor_tensor(out=mT[:, c, :], in0=t1[r][:, :],
                                                                          in1=sga[r][:, :], op=ALU.add),
                              reads=["t1%d" % r, "sga%d" % r], writes=["mT.%d" % c], grp="M")
                    rel()
                    if qq == 1:
                        rel()
            P.add("act", lambda e: e.activation(out=lnv[:, 0:1], in_=epsT[:, 0:1], func=AF.Ln), reads=["cA"],
                  writes=["lnv"])
            wo = [take_unit(U_WO[0]), take_unit(U_WO[1])]
            tb2 = None

            def n2_group(c0, c1, gname):
                P.add("act", lambda e: e.activation(out=lnv[:, c0:c1], in_=ss[:, c0:c1], func=AF.Ln, scale=1.0 / D,
                                                    bias=epsT[:, 0:1]),
                      reads=["ssA.%d" % i_ for i_ in range(c0, c1)] + ["cA", "lnv"], writes=["lnvA." + gname])
                P.add("act", lambda e: e.activation(out=rstd[:, c0:c1], in_=lnv[:, c0:c1], func=AF.Exp, scale=-0.5),
                      reads=["lnvA." + gname, "rstd"], writes=["rstdA." + gname])
                P.add("dve", lambda e: e.memset(ss[:, c0:c1], 0.0), reads=["lnvA." + gname],
                      writes=["ssA.%d" % i_ for i_ in range(c0, c1)])

            def n2_scale(i, gname):
                if i % 2 == 0:
                    P.add("dve", lambda e: e.tensor_scalar(out=xs[i % 2][:, :], in0=X[:, i, :],
                                                           scalar1=rstd[:, i:i + 1], scalar2=None, op0=ALU.mult),
                          reads=[xr[i], "rstdA." + gname], writes=["xs%d" % (i % 2)])
                else:
                    P.add("act", lambda e: e.activation(out=xs[i % 2][:, :], in_=X[:, i, :], func=AF.Copy,
                                                        scale=rstd[:, i:i + 1]),
                          reads=[xr[i], "rstdA." + gname], writes=["xs%d" % (i % 2)])

            for i in range(4):
                for n in range(2):
                    wt, wn = wo[n]
                    bk = next_bank()
                    for kc in range(NKC):
                        P.add("pe", lambda e, i=i, kc=kc, bk=bk, wt=wt: e.matmul(
                            ps_all[:, bk, :], mT[:, kc, i * 128:(i + 1) * 128], wt[:, kc, :],
                            start=(kc == 0), stop=(kc == NKC - 1)),
                            reads=[wn, "mT.%d" % kc], writes=[psn(bk)], grp="M")
                    P.add("dve", lambda e, i=i, n=n, bk=bk: e.tensor_tensor(
                        out=X[:, i, n * 512:(n + 1) * 512], in0=ps_all[:, bk, :], in1=X[:, i, n * 512:(n + 1) * 512],
                        op=ALU.add),
                        reads=[psn(bk), xr[i]], writes=[xr[i]], grp="M")
                P.add("act", lambda e, i=i: e.activation(out=sqj[:, :], in_=X[:, i, :], func=AF.Square,
                                                         accum_out=ss[:, i:i + 1]),
                      reads=[xr[i], "ss"], writes=["sqj", "ssA.%d" % i])
                if i == 2:
                    n2_group(0, 3, "a")
                    n2_scale(0, "a")
                    n2_scale(1, "a")
            rel()
            rel()
            n2_group(3, 4, "b")
            tb2 = [next_bank() for _ in range(4)]
            norm_transp(0, tb2)
            norm_transp(1, tb2)
            n2_scale(2, "a")
            norm_transp(2, tb2)
            n2_scale(3, "b")
            norm_transp(3, tb2)
            P.add("dve", lambda e: e.memset(ss[:, :], 0.0), reads=["lnvA.a", "lnvA.b", "rstdA.a", "rstdA.b"],
                  writes=["ss", "lnv", "rstd"] + ["ssA.%d" % i_ for i_ in range(4)])
            norm_evac(tb2, gffn, "cB")

            if _DBG["stop"] == 7:
                break

            rr = {"i": 0}

            def ffn_chunk(wt, wn, colbase, jj, want_silu, j):
                bk = next_bank()
                proj_fm(wt, wn, colbase, bk, "F")
                r = rr["i"] % NR
                rr["i"] += 1
                A = A_sb[r]
                ac = acc[r]
                P.add("act", lambda e, A=A, bk=bk: e.activation(out=A[:, 2:TB + 2], in_=ps_all[:, bk, :], func=AF.Copy),
                      reads=[psn(bk)], writes=["A%d" % r], grp="F")
                P.add("act", lambda e, ac=ac, bk=bk, jj=jj: e.activation(out=ac[:, :], in_=ps_all[:, bk, :],
                                                                         func=AF.Copy, scale=cfw[:, jj, 2:3]),
                      reads=[psn(bk), "cB"], writes=["acc%d" % r], grp="F")
                if qb == 0:
                    P.add("pool", lambda e, A=A: e.memset(A[:, 0:2], 0.0), writes=["Ah%d" % r], grp="F")
                else:
                    P.add("pool", lambda e, A=A, jj=jj: e.tensor_copy(out=A[:, 0:2], in_=HS[:, jj, :]),
                          reads=["HS.%d" % jj], writes=["Ah%d" % r], grp="F")
                P.add("pool", lambda e, A=A, jj=jj: e.tensor_copy(out=HS[:, jj, :], in_=A[:, TB:TB + 2]),
                      reads=["A%d" % r], writes=["HS.%d" % jj], grp="F")
                eng = "dve"
                P.add(eng, lambda e, A=A, ac=ac, jj=jj: e.scalar_tensor_tensor(
                    out=ac[:, :], in0=A[:, 1:TB + 1], scalar=cfw[:, jj, 1:2], in1=ac[:, :], op0=ALU.mult, op1=ALU.add),
                    reads=["A%d" % r, "Ah%d" % r, "acc%d" % r, "cB"], writes=["acc%d" % r], grp="F")
                P.add(eng, lambda e, A=A, ac=ac, jj=jj: e.scalar_tensor_tensor(
                    out=ac[:, :], in0=A[:, 0:TB], scalar=cfw[:, jj, 0:1], in1=ac[:, :], op0=ALU.mult, op1=ALU.add),
                    reads=["A%d" % r, "Ah%d" % r, "acc%d" % r, "cB"], writes=["acc%d" % r], grp="F")
                return r

            pend_fin = []

            def ffn_finish(j, ra, rb):
                sr = j % 2
                P.add("act", lambda e, ra=ra, sr=sr: e.activation(out=sa[sr][:, :], in_=acc[ra][:, :], func=AF.Silu),
                      reads=["acc%d" % ra], writes=["sa%d" % sr], grp="F")
                P.add("pool", lambda e, rb=rb, sr=sr, j=j: e.tensor_tensor(out=G[:, j, :], in0=sa[sr][:, :],
                                                                           in1=acc[rb][:, :], op=ALU.mult),
                      reads=["sa%d" % sr, "acc%d" % rb], writes=["G.%d" % j], grp="F")

            for u in range(11):
                wt, wn = take_unit(U_UP[u])
                for cl in range(2):
                    j = 2 * u + cl
                    ra = ffn_chunk(wt, wn + "a", cl * 128, j, True, j)
                    rb = ffn_chunk(wt, wn + "b", 256 + cl * 128, NFC + j, False, j)
                    pend_fin.append((j, ra, rb))
                    if len(pend_fin) > 1:
                        ffn_finish(*pend_fin.pop(0))
                rel()
            while pend_fin:
                ffn_finish(*pend_fin.pop(0))

            if _DBG["stop"] == 8:
                break
            nxt = blk + 1 < NBLK
            nxb = (blk + 1) % 2
            if nxt:
                norm_stats(nxb, STB)
                norm_scale(nxb, 0, STB)
                norm_scale(nxb, 1, STB)
            bankset = [[next_bank() for _ in range(4)], [next_bank() for _ in range(4)]]

            def dn(n, gi):
                ka, kb = DN_K[gi]
                wt, wn = take_unit(U_DN[n * 3 + gi])
                for i in range(4):
                    for j in range(ka, kb):
                        P.add("pe", lambda e, i=i, j=j, bk=bankset[n][i]: e.matmul(
                            ps_all[:, bk, :], G[:, j, i * 128:(i + 1) * 128], wt[:, j - ka, :],
                            start=(j == 0), stop=(j == NFC - 1)),
                            reads=[wn, "G.%d" % j], writes=[psn(bankset[n][i])], grp="F")
                rel()

            def dn_add(n):
                for i in range(4):
                    P.add("dve", lambda e, i=i, bk=bankset[n][i]: e.tensor_tensor(
                        out=X[:, i, n * 512:(n + 1) * 512], in0=ps_all[:, bk, :], in1=X[:, i, n * 512:(n + 1) * 512],
                        op=ALU.add),
                        reads=[psn(bankset[n][i]), xr[i]], writes=[xr[i]], grp="F")

            dn(0, 0)
            dn(0, 1)
            dn(1, 0)
            dn(0, 2)
            dn_add(0)
            dn(1, 1)
            ntb = bankset[0]
            if nxt:
                norm_transp(0, ntb)
                norm_transp(1, ntb)
                norm_scale(nxb, 2, STB)
                norm_scale(nxb, 3, STB)
            dn(1, 2)
            dn_add(1)
            if nxt:
                norm_transp(2, ntb)
                norm_transp(3, ntb)
                norm_evac(ntb, gmix, "cG")

            for i in range(4):
                P.add("act", lambda e, i=i: e.activation(out=sqj[:, :], in_=X[:, i, :], func=AF.Square,
                                                         accum_out=ss[:, i:i + 1]),
                      reads=[xr[i]], writes=["sqj", "ss"])
            P.add("act", lambda e: e.activation(out=lnv[:, :], in_=ss[:, :], func=AF.Ln, scale=1.0 / D, bias=epsT[:, 0:1]),
                  reads=["ss", "cA"], writes=["lnv"])
            P.add("act", lambda e: e.activation(out=rstd[:, :], in_=lnv[:, :], func=AF.Exp, scale=-0.5),
                  reads=["lnv"], writes=["rstd"])
            P.add("dve", lambda e: e.memset(ss[:, :], 0.0), reads=["lnv"], writes=["ss"])
            for i in range(4):
                P.add("act", lambda e, i=i: e.activation(out=X[:, i, :], in_=X[:, i, :], func=AF.Copy,
                                                         scale=rstd[:, i:i + 1]),
                      reads=[xr[i], "rstd"], writes=[xr[i]])
                P.add("pool", lambda e, i=i: e.tensor_tensor(out=X[:, i, :], in0=X[:, i, :], in1=gfbc[:, :],
                                                             op=ALU.mult),
                      reads=[xr[i], "cB"], writes=[xr[i]])
            pending_store.append((row0, xb))

        if not pending_store:
            pending_store.append((0, 0))
        pb_row0, pb_xb = pending_store.pop()
        P.add("sp", lambda e, r=pb_row0, b_=pb_xb: e.dma_start(
            out=out_d[r:r + TB, :].rearrange("(i p) d -> p i d", p=128), in_=x_tm[b_][:, :, :]),
            reads=["x%d.%d" % (pb_xb, i) for i in range(4)], writes=["outstore%d" % pb_xb], dma="st%d" % pb_xb)
        P.add("sp", None, reads=["outstore0", "outstore1", "cA", "cB", "KAaug.0", "ss"] + ["wslot%d" % i for i in range(NSLOT)])

        _LASTP.append(P)
        P.emit(nc, es)
    return nc


_CACHE = {}


def _get_program(NSEQ, SEQ):
    key = (NSEQ, SEQ)
    if key not in _CACHE:
        _CACHE[key] = build_program(NSEQ, SEQ)
    return _CACHE[key]


def kernel(x, norm_mix_g, w_in, b_f, b_gate, conv_mix_w, w_out_conv, w_out_attn, w_o, norm_ffn_g, w_up,
           conv_ffn_w, w_down, norm_f_g, _n_cores=N_CORES):
    x = np.asarray(x, dtype=np.float32)
    B, S, _ = x.shape
    n = _n_cores
    assert B % n == 0
    nseq = B // n
    nc = _get_program(nseq, S)
    f = lambda a: np.ascontiguousarray(np.asarray(a, dtype=np.float32))
    shared = {
        "norm_mix_g": f(norm_mix_g).reshape(D),
        "w_in": f(w_in).reshape(D, INW),
        "b_f": f(b_f).reshape(1, H),
        "b_gate": f(b_gate).reshape(2 * D),
        "conv_mix_w": f(conv_mix_w).reshape(3, 512),
        "w_out_conv": f(w_out_conv).reshape(512, D),
        "w_out_attn": f(w_out_attn).reshape(512, D),
        "w_o": f(w_o).reshape(D, D),
        "norm_ffn_g": f(norm_ffn_g).reshape(D),
        "w_up": f(w_up).reshape(D, 2 * FF),
        "conv_ffn_w": f(conv_ffn_w).reshape(3, 2 * FF),
        "w_down": f(w_down).reshape(FF, D),
        "norm_f_g": f(norm_f_g).reshape(1, D),
    }
    in_maps = []
    for c in range(n):
        m = dict(shared)
        m["x"] = np.ascontiguousarray(x[c * nseq:(c + 1) * nseq].reshape(nseq * S, D))
        in_maps.append(m)
    res = run_bass_kernel_spmd(nc, in_maps, core_ids=list(range(n)))
    outs = [np.asarray(r["out"]).reshape(nseq, S, D) for r in res.results]
    return np.concatenate(outs, axis=0).astype(np.float32)
```
